# Optimizing a Trainium2 kernel written in Bass

```python
import jax, jax.numpy as jnp
from jax import lax
import numpy as np

D_MODEL = 2048
BATCH = 4
SEQ = 2048
DEPTH = 4

N_A_LAYERS = DEPTH // 2
N_B_LAYERS = DEPTH - N_A_LAYERS
A_HEADS = 8
A_DV = D_MODEL // A_HEADS
A_DQK = A_DV // 2
A_CHUNK = 64
GATE_SOFTCAP = 15.0
A_IN_COLS = 2 * A_HEADS * A_DQK + 2 * A_HEADS * A_DV + 2 * A_HEADS
B_HEADS = 16
B_DH = D_MODEL // B_HEADS
B_QBLOCK = 128
D_FF = 5632
EPS = 1e-6

kernel_name = "mlstm_stickbreak_yoco_macaron"


def rms_norm(x, g):
    xf = x.astype(jnp.float32)
    y = xf * lax.rsqrt(jnp.mean(xf * xf, axis=-1, keepdims=True) + EPS)
    return (y * g.astype(jnp.float32)).astype(x.dtype)


def swiglu(x, w_in, w_out):
    gate, up = jnp.split(x @ w_in, 2, axis=-1)
    return (jax.nn.silu(gate) * up) @ w_out


def mlstm_mix(h, w_in, b_gate, head_norm, w_out):
    B, S, _ = h.shape
    H, L = A_HEADS, A_CHUNK
    NC = S // L
    proj = h @ w_in
    o1 = H * A_DQK
    o2 = 2 * H * A_DQK
    o3 = o2 + H * A_DV
    o4 = o3 + H * A_DV
    q, k, v, og, gates = proj[..., :o1], proj[..., o1:o2], proj[..., o2:o3], proj[..., o3:o4], proj[..., o4:]
    gates = gates.astype(jnp.float32) + b_gate.astype(jnp.float32)
    ig = GATE_SOFTCAP * jnp.tanh(gates[..., :H] / GATE_SOFTCAP)
    logf = jax.nn.log_sigmoid(gates[..., H:])

    def to_chunks(t, d):
        t = t.astype(jnp.float32).reshape(B, NC, L, H, d)
        return t.transpose(1, 0, 3, 2, 4)

    qc = to_chunks(q, A_DQK) * (A_DQK ** -0.5)
    kc = to_chunks(k, A_DQK)
    vc = to_chunks(v, A_DV)
    igc = ig.reshape(B, NC, L, H).transpose(1, 0, 3, 2)
    lfc = logf.reshape(B, NC, L, H).transpose(1, 0, 3, 2)
    tril = jnp.tril(jnp.ones((L, L), dtype=bool))

    def chunk_step(carry, inp):
        C, n, m = carry
        qq, kk, vv, ii, lf = inp
        b = jnp.cumsum(lf, axis=-1)
        Dm = jnp.where(tril, b[..., :, None] - b[..., None, :] + ii[..., None, :], -jnp.inf)
        m_inter = b + m[..., None]
        m_t = jnp.maximum(jnp.max(Dm, axis=-1), m_inter)
        Sw = jnp.einsum('bhtd,bhsd->bhts', qq, kk) * jnp.exp(Dm - m_t[..., None])
        dec = jnp.exp(m_inter - m_t)
        num = jnp.einsum('bhts,bhsv->bhtv', Sw, vv) + dec[..., None] * jnp.einsum('bhvd,bhtd->bhtv', C, qq)
        den = jnp.sum(Sw, axis=-1) + dec * jnp.einsum('bhd,bhtd->bht', n, qq)
        out = num / jnp.maximum(jnp.abs(den), jnp.exp(-m_t))[..., None]
        bL = b[..., -1]
        g = bL[..., None] - b + ii
        m_new = jnp.maximum(bL + m, jnp.max(g, axis=-1))
        w = jnp.exp(g - m_new[..., None])
        cdec = jnp.exp(bL + m - m_new)
        C = cdec[..., None, None] * C + jnp.einsum('bhs,bhsv,bhsd->bhvd', w, vv, kk)
        n = cdec[..., None] * n + jnp.einsum('bhs,bhsd->bhd', w, kk)
        return (C, n, m_new), out

    init = (jnp.zeros((B, H, A_DV, A_DQK), jnp.float32),
            jnp.zeros((B, H, A_DQK), jnp.float32),
            jnp.zeros((B, H), jnp.float32))
    _, hs = lax.scan(chunk_step, init, (qc, kc, vc, igc, lfc))
    hs = hs.transpose(1, 0, 3, 2, 4).reshape(B, S, H, A_DV)
    hs = hs * lax.rsqrt(jnp.mean(hs * hs, axis=-1, keepdims=True) + EPS)
    hs = hs.reshape(B, S, H * A_DV) * head_norm.astype(jnp.float32)
    hs = hs * jax.nn.sigmoid(og.astype(jnp.float32))
    return hs.astype(h.dtype) @ w_out


def shared_kv(x, g, w_kv):
    B, S, _ = x.shape
    k, v = jnp.split(rms_norm(x, g) @ w_kv, 2, axis=-1)
    k = k.astype(jnp.float32).reshape(B, S, B_HEADS, B_DH).transpose(0, 2, 1, 3)
    v = v.astype(jnp.float32).reshape(B, S, B_HEADS, B_DH).transpose(0, 2, 1, 3)
    return k, v


def stick_breaking_mix(h, w_q, k, v, w_out):
    B, S, _ = h.shape
    q = (h @ w_q).astype(jnp.float32).reshape(B, S, B_HEADS, B_DH).transpose(0, 2, 1, 3) * (B_DH ** -0.5)
    outs = []
    for blk in range(S // B_QBLOCK):
        end = (blk + 1) * B_QBLOCK
        qb = q[:, :, end - B_QBLOCK:end]
        kb = k[:, :, :end]
        vb = v[:, :, :end]
        z = jnp.einsum('bhqd,bhkd->bhqk', qb, kb)
        qpos = jnp.arange(end - B_QBLOCK, end)
        kpos = jnp.arange(end)
        mask = kpos[None, :] < qpos[:, None]
        log_not = jnp.where(mask, jax.nn.log_sigmoid(-z), 0.0)
        after = lax.cumsum(log_not, axis=3, reverse=True) - log_not
        att = jnp.where(mask, jnp.exp(jax.nn.log_sigmoid(z) + after), 0.0)
        outs.append(jnp.einsum('bhqk,bhkd->bhqd', att, vb))
    o = jnp.concatenate(outs, axis=2).transpose(0, 2, 1, 3).reshape(B, S, B_HEADS * B_DH)
    return o.astype(h.dtype) @ w_out


def setup_inputs(seed: int = 0) -> dict:
    key = jax.random.key(seed)
    ks = jax.random.split(key, 20)
    out_scale = (2 * DEPTH) ** -0.5

    def w(k_, shape, fan_in, scale=1.0):
        return jax.random.normal(k_, shape, jnp.float32) * (scale * fan_in ** -0.5)

    def gain(k_, shape):
        return 1.0 + 0.02 * jax.random.normal(k_, shape, jnp.float32)

    gate_bias = jnp.concatenate([
        0.1 * jax.random.normal(ks[17], (N_A_LAYERS, A_HEADS), jnp.float32),
        3.0 + 0.5 * jax.random.normal(ks[18], (N_A_LAYERS, A_HEADS), jnp.float32)], axis=-1)
    return {
        "x": jax.random.normal(ks[0], (BATCH, SEQ, D_MODEL), jnp.float32),
        "ffn1_norm": gain(ks[1], (DEPTH, D_MODEL)),
        "ffn1_w_in": w(ks[2], (DEPTH, D_MODEL, 2 * D_FF), D_MODEL),
        "ffn1_w_out": w(ks[3], (DEPTH, D_FF, D_MODEL), D_FF, out_scale),
        "mix_norm": gain(ks[4], (DEPTH, D_MODEL)),
        "ffn2_norm": gain(ks[5], (DEPTH, D_MODEL)),
        "ffn2_w_in": w(ks[6], (DEPTH, D_MODEL, 2 * D_FF), D_MODEL),
        "ffn2_w_out": w(ks[7], (DEPTH, D_FF, D_MODEL), D_FF, out_scale),
        "a_w_in": w(ks[8], (N_A_LAYERS, D_MODEL, A_IN_COLS), D_MODEL),
        "a_b_gate": gate_bias,
        "a_head_norm": gain(ks[9], (N_A_LAYERS, A_HEADS * A_DV)),
        "a_w_out": w(ks[10], (N_A_LAYERS, A_HEADS * A_DV, D_MODEL), A_HEADS * A_DV, out_scale),
        "kv_norm": gain(ks[11], (D_MODEL,)),
        "kv_w": w(ks[12], (D_MODEL, 2 * B_HEADS * B_DH), D_MODEL),
        "b_w_q": w(ks[13], (N_B_LAYERS, D_MODEL, B_HEADS * B_DH), D_MODEL),
        "b_w_out": w(ks[14], (N_B_LAYERS, B_HEADS * B_DH, D_MODEL), B_HEADS * B_DH, out_scale),
        "final_norm": gain(ks[15], (D_MODEL,)),
    }


def reference(x, ffn1_norm, ffn1_w_in, ffn1_w_out, mix_norm, ffn2_norm, ffn2_w_in, ffn2_w_out,
              a_w_in, a_b_gate, a_head_norm, a_w_out, kv_norm, kv_w, b_w_q, b_w_out, final_norm):
    k_sh, v_sh = None, None
    for l in range(DEPTH):
        x = x + 0.5 * swiglu(rms_norm(x, ffn1_norm[l]), ffn1_w_in[l], ffn1_w_out[l])
        hn = rms_norm(x, mix_norm[l])
        if l < N_A_LAYERS:
            x = x + mlstm_mix(hn, a_w_in[l], a_b_gate[l], a_head_norm[l], a_w_out[l])
        else:
            j = l - N_A_LAYERS
            x = x + stick_breaking_mix(hn, b_w_q[j], k_sh, v_sh, b_w_out[j])
        x = x + 0.5 * swiglu(rms_norm(x, ffn2_norm[l]), ffn2_w_in[l], ffn2_w_out[l])
        if l == N_A_LAYERS - 1:
            k_sh, v_sh = shared_kv(x, kv_norm, kv_w)
    return rms_norm(x, final_norm)
```

```python
import numpy as np
import concourse.bass as bass
import concourse.mybir as mybir
from concourse.bass_utils import run_bass_kernel_spmd

F32 = mybir.dt.float32
BF16 = mybir.dt.bfloat16
AF = mybir.ActivationFunctionType
ALU = mybir.AluOpType

D = 2048
KC = 16
DFF = 5632
NFF = DFF // 128
T = 1024
EPS = 1e-6

ENGS = ["pe", "act", "dve", "pool", "sp"]
PAIRS = [[0, 1], [2, 3], [4, 5], [6, 7]]


class Res:
    __slots__ = ("name", "w", "r")

    def __init__(self, name):
        self.name = name
        self.w = None
        self.r = []


class Op:
    __slots__ = ("eng", "fn", "deps", "inc", "cnt", "dma_sem", "dma_val", "is_coll")

    def __init__(self, eng, fn):
        self.eng = eng
        self.fn = fn
        self.deps = set()
        self.inc = False
        self.cnt = 0
        self.dma_sem = None
        self.dma_val = 0
        self.is_coll = False


class Prog:
    def __init__(self, nc):
        self.nc = nc
        self.ops = {e: [] for e in ENGS}
        self.dma_cnt = {}
        self.globalR = Res("global")
        self.rank = {}
        self.use_rank = False

    def _track(self, o, reads, writes):
        deps = set()
        if self.globalR not in writes:
            reads = list(reads) + [self.globalR]
        for r in reads:
            if r.w is not None:
                deps.add(r.w)
        for w in writes:
            if w.w is not None:
                deps.add(w.w)
            deps.update(w.r)
        deps.discard(o)
        o.deps = deps
        for r in reads:
            r.r.append(o)
        for w in writes:
            w.w = o
            w.r = []

    def op(self, eng, fn, reads=(), writes=()):
        o = Op(eng, fn)
        self._track(o, reads, writes)
        self.ops[eng].append(o)
        return o

    def dma(self, queue, out, in_, sem, reads=(), writes=(), **kw):
        def fn(e):
            src = in_(self.rank[queue]) if callable(in_) else in_
            return e.dma_start(out=out, in_=src, **kw)

        o = Op(queue, fn)
        o.dma_sem = sem
        self.dma_cnt[sem] = self.dma_cnt.get(sem, 0) + 16
        o.dma_val = self.dma_cnt[sem]
        self._track(o, reads, writes)
        self.ops[queue].append(o)
        return o

    def coll(self, kind, in_ap, out_ap, sem, reads=(), writes=()):
        o = Op("pool", lambda e: e.collective_compute(kind, ALU.bypass, replica_groups=PAIRS, ins=[in_ap.opt()], outs=[out_ap.opt()]))
        o.dma_sem = sem
        o.is_coll = True
        self.dma_cnt[sem] = self.dma_cnt.get(sem, 0) + 1
        o.dma_val = self.dma_cnt[sem]
        self._track(o, reads, writes)
        self.ops["pool"].append(o)
        return o

    def coll_rows(self, x2d, g2d, n, reads=(), writes=()):
        rows = x2d.shape[0]
        for c in range(rows // n):
            self.coll("AllGather", x2d[c * n:(c + 1) * n, :], g2d[c * 2 * n:(c + 1) * 2 * n, :], "cc", reads=reads, writes=writes)

    def barrier(self, scratch):
        self.op("dve", lambda e: e.memset(scratch, 0.0), writes=[self.globalR])

    def emit(self, final_dma_ops=()):
        nc = self.nc
        for e in ENGS:
            for o in self.ops[e]:
                for d in o.deps:
                    if d.dma_sem is None:
                        if d.eng == "pe" and o.eng == "pe":
                            continue
                        d.inc = True
        for e in ENGS:
            c = 0
            for o in self.ops[e]:
                if o.dma_sem is None and o.inc:
                    c += 1
                    o.cnt = c
        from contextlib import ExitStack

        with ExitStack() as st:
            esem = {e: st.enter_context(nc.semaphore("s_" + e)) for e in ENGS}
            dsem = {k: st.enter_context(nc.semaphore("d_" + k)) for k in self.dma_cnt}
            block = st.enter_context(nc.Block())

            def run(eng_name, eng):
                waited = {}
                if self.use_rank and eng_name == "sp":
                    self.rank[eng_name] = eng.cc_rank(PAIRS)
                for o in self.ops[eng_name]:
                    need = {}
                    for d in o.deps:
                        if d.dma_sem is not None:
                            key = ("d", d.dma_sem)
                            val = d.dma_val
                        else:
                            if d.eng == "pe" and eng_name == "pe":
                                continue
                            key = ("e", d.eng)
                            val = d.cnt
                        if val > need.get(key, 0):
                            need[key] = val
                    for key, val in need.items():
                        if val > waited.get(key, 0):
                            waited[key] = val
                            s = dsem[key[1]] if key[0] == "d" else esem[key[1]]
                            eng.wait_ge(s, val)
                    ins = o.fn(eng)
                    if o.is_coll:
                        ins.then_inc(dsem[o.dma_sem], 1)
                    elif o.dma_sem is not None:
                        ins.then_inc(dsem[o.dma_sem], 16)
                    elif o.inc:
                        ins.then_inc(esem[eng_name], 1)
                if eng_name == "sp":
                    for k, v in self.dma_cnt.items():
                        eng.wait_ge(dsem[k], v)

            @block.tensor
            def _(e):
                run("pe", e)

            @block.scalar
            def _(e):
                run("act", e)

            @block.vector
            def _(e):
                run("dve", e)

            @block.gpsimd
            def _(e):
                run("pool", e)

            @block.sync
            def _(e):
                run("sp", e)


class Arena:
    def __init__(self, nc, nbytes):
        self.nc = nc
        self.slab = nc.alloc_sbuf_tensor("arena_slab", [128, nbytes // 4], F32)
        self.base = nc.lookup_mloc(self.slab).addr
        self.nbytes = nbytes
        self.off = 0
        self.n = 0
        self.hi = 0

    def alloc(self, name, shape, dtype):
        sz = 1
        for s in shape[1:]:
            sz *= s
        sz *= 2 if dtype == BF16 else 4
        sz = (sz + 63) // 64 * 64
        assert self.off + sz <= self.nbytes, (name, self.off, sz, self.nbytes)
        self.n += 1
        t = self.nc.alloc_sbuf_tensor_at(f"{name}_{self.n}", list(shape), dtype, offset=self.base + self.off)
        self.off += sz
        self.hi = max(self.hi, self.off)
        return t

    def mark(self):
        return self.off

    def reset(self, m):
        self.off = m


ARENA = 204 * 1024
NCST = 264
EPSC = 260


class Builder:
    def __init__(self, nc, resident=True, fused=False):
        self.nc = nc
        self.fused = fused
        self.P = Prog(nc)
        self.A = Arena(nc, ARENA)
        A = self.A
        self.xT = A.alloc("xT", [128, KC, T], F32)
        self.xR = [[Res(f"x{k}_{h}") for h in range(2)] for k in range(KC)]
        self.cst = A.alloc("cst", [128, NCST], F32)
        self.cstR = Res("cst")
        self.ones32 = A.alloc("ones32", [128, 128], F32)
        self.onesR = Res("ones32")
        self.rs = A.alloc("rs", [128, T], F32)
        self.rsR = Res("rs")
        self.ps = [nc.alloc_psum_tensor(f"ps{i}", [128, 512], F32) for i in range(8)]
        self.psR = [Res(f"ps{i}") for i in range(8)]
        self.P.op("pool", lambda e: e.memset(self.ones32[:], 1.0), writes=[self.onesR])
        self.bscr = A.alloc("bscr", [128, 16], F32)
        self.idx = A.alloc("idx", [128, 48], mybir.dt.uint32)
        self.idxR = Res("idx")
        self.sel = A.alloc("sel", [128, 2], F32)
        self.selR = Res("sel")
        self.hgt = A.alloc("hgt", [128, 16], F32)
        self.hgR = Res("hgt")
        self.c16 = A.alloc("c16", [128, 896], BF16)
        self.c16R = Res("c16")
        self.c32 = A.alloc("c32", [4, 512], F32)
        self.c32R = Res("c32")
        self.on16 = A.alloc("on16", [128, 128], BF16)
        self.low_mark = A.mark()
        self.hT = A.alloc("hT", [128, KC, T], BF16)
        self.hR = [Res(f"h{k}") for k in range(KC)]
        self.phase_mark = A.mark()

    def idma(self, out, in2d, col, sem, reads=(), writes=()):
        P = self.P
        off = bass.IndirectOffsetOnAxis(ap=self.idx[:, col:col + 1], axis=0)
        o = Op("pool", lambda e: e.indirect_dma_start(out=out, out_offset=None, in_=in2d, in_offset=off))
        o.dma_sem = sem
        P.dma_cnt[sem] = P.dma_cnt.get(sem, 0) + 16
        o.dma_val = P.dma_cnt[sem]
        P._track(o, list(reads) + [self.idxR], writes)
        P.ops["pool"].append(o)
        return o

    def release(self, m):
        self.A.reset(m)
        self.P.barrier(self.bscr[:, 0:1])

    def rmsnorm(self, gi, out=None, outR=None):
        P, A = self.P, self.A
        out = self.hT if out is None else out
        outR = self.hR if outR is None else outR
        m = A.mark()
        sq = [A.alloc("sq", [128, T], F32) for _ in range(2)]
        sqR = [Res("sq0"), Res("sq1")]
        xT, ones32, rs, ps, psR = self.xT, self.ones32, self.rs, self.ps, self.psR
        for kc in range(KC):
            b = kc % 2
            P.op("act", lambda e, kc=kc, b=b: e.activation(out=sq[b][:], in_=xT[:, kc, :], func=AF.Square),
                 reads=self.xR[kc], writes=[sqR[b]])
            for h in range(2):
                P.op("pe", lambda e, kc=kc, b=b, h=h: e.matmul(ps[6 + h][:], lhsT=ones32[:], rhs=sq[b][:, h * 512:(h + 1) * 512],
                                                              start=(kc == 0), stop=(kc == KC - 1)),
                     reads=[sqR[b], self.onesR], writes=[psR[6 + h]])
        for h in range(2):
            P.op("act", lambda e, h=h: e.activation(out=rs[:, h * 512:(h + 1) * 512], in_=ps[6 + h][:], func=AF.Sqrt,
                                                    scale=1.0 / D, bias=self.eps_ap()),
                 reads=[psR[6 + h], self.cstR], writes=[self.rsR])
        P.op("dve", lambda e: e.reciprocal(out=rs[:], in_=rs[:]), reads=[self.rsR], writes=[self.rsR])
        for kc in range(KC):
            P.op("dve", lambda e, kc=kc: e.scalar_tensor_tensor(out=out[:, kc, :], in0=xT[:, kc, :],
                                                                scalar=self.cst[:, gi * 16 + kc:gi * 16 + kc + 1], in1=rs[:],
                                                                op0=ALU.mult, op1=ALU.mult),
                 reads=self.xR[kc] + [self.rsR, self.cstR], writes=[outR[kc]])
        self.release(m)

    def final_norm(self, gi):
        P, A = self.P, self.A
        m = A.mark()
        sq = [A.alloc("sq", [128, T], F32) for _ in range(2)]
        sqR = [Res("sq0"), Res("sq1")]
        xT, ones32, rs, ps, psR = self.xT, self.ones32, self.rs, self.ps, self.psR
        for kc in range(KC):
            b = kc % 2
            P.op("act", lambda e, kc=kc, b=b: e.activation(out=sq[b][:], in_=xT[:, kc, :], func=AF.Square),
                 reads=self.xR[kc], writes=[sqR[b]])
            for h in range(2):
                P.op("pe", lambda e, kc=kc, b=b, h=h: e.matmul(ps[6 + h][:], lhsT=ones32[:], rhs=sq[b][:, h * 512:(h + 1) * 512],
                                                              start=(kc == 0), stop=(kc == KC - 1)),
                     reads=[sqR[b], self.onesR], writes=[psR[6 + h]])
        for h in range(2):
            P.op("act", lambda e, h=h: e.activation(out=rs[:, h * 512:(h + 1) * 512], in_=ps[6 + h][:], func=AF.Sqrt,
                                                    scale=1.0 / D, bias=self.eps_ap()),
                 reads=[psR[6 + h], self.cstR], writes=[self.rsR])
        P.op("dve", lambda e: e.reciprocal(out=rs[:], in_=rs[:]), reads=[self.rsR], writes=[self.rsR])
        for kc in range(KC):
            P.op("dve", lambda e, kc=kc: e.scalar_tensor_tensor(out=xT[:, kc, :], in0=xT[:, kc, :],
                                                                scalar=self.cst[:, gi * 16 + kc:gi * 16 + kc + 1], in1=rs[:],
                                                                op0=ALU.mult, op1=ALU.mult),
                 reads=self.xR[kc] + [self.rsR, self.cstR], writes=self.xR[kc])
        self.release(m)

    def eps_ap(self):
        return self.cst[:, EPSC:EPSC + 1]

    def ffn(self, w_in, w_out, tag):
        P, A = self.P, self.A
        m = A.mark()
        w_in_v = w_in.rearrange("(kc p) f -> p kc f", p=128)
        w_out_v = w_out.rearrange("(j p) d -> p j d", p=128)
        NST = 6
        stg = [A.alloc("stg", [128, 2048], F32) for _ in range(NST)]
        stgR = [Res(f"stg{i}") for i in range(NST)]
        wg = [A.alloc("wg", [128, KC, 128], BF16) for _ in range(2)]
        wu = [A.alloc("wu", [128, KC, 128], BF16) for _ in range(2)]
        wgR = [Res("wg0"), Res("wg1")]
        wuR = [Res("wu0"), Res("wu1")]
        wo = [A.alloc("wo", [128, D], BF16) for _ in range(4)]
        woR = [Res(f"wo{i}") for i in range(4)]
        g = [A.alloc("g", [128, T], BF16) for _ in range(4)]
        gR = [[Res(f"g{i}_{h}") for h in range(2)] for i in range(4)]
        sg = [A.alloc("sg", [128, 512], F32) for _ in range(2)]
        sgR = [Res("sg0"), Res("sg1")]
        xT, hT, ps, psR = self.xT, self.hT, self.ps, self.psR

        def dma_issue(j):
            s = 3 * (j % 2)
            P.dma("sp", stg[s][:].rearrange("p (k f) -> p k f", k=KC), w_in_v[:, :, j * 128:(j + 1) * 128], f"st{s}",
                  writes=[stgR[s]])
            P.dma("sp", stg[s + 1][:].rearrange("p (k f) -> p k f", k=KC), w_in_v[:, :, DFF + j * 128:DFF + (j + 1) * 128],
                  f"st{s+1}", writes=[stgR[s + 1]])
            P.dma("sp", stg[s + 2][:], w_out_v[:, j, :], f"st{s+2}", writes=[stgR[s + 2]])

        def cast_in(j):
            s = 3 * (j % 2)
            b = j % 2
            P.op("act", lambda e: e.activation(out=wg[b][:].rearrange("p k f -> p (k f)"), in_=stg[s][:], func=AF.Copy),
                 reads=[stgR[s]], writes=[wgR[b]])
            P.op("pool", lambda e: e.tensor_copy(out=wu[b][:].rearrange("p k f -> p (k f)"), in_=stg[s + 1][:]),
                 reads=[stgR[s + 1]], writes=[wuR[b]])

        def cast_wo(j):
            s = 3 * (j % 2)
            P.op("pool", lambda e: e.tensor_copy(out=wo[j % 4][:], in_=stg[s + 2][:]),
                 reads=[stgR[s + 2]], writes=[woR[j % 4]])

        def win(j, after_half=None):
            b = j % 2
            gs = j % 4
            for h in range(2):
                for (wt, wR, pi) in ((wg, wgR, h), (wu, wuR, 2 + h)):
                    for kc in range(KC):
                        P.op("pe", lambda e, wt=wt, pi=pi, kc=kc, h=h: e.matmul(
                            ps[pi][:], lhsT=wt[b][:, kc, :], rhs=hT[:, kc, h * 512:(h + 1) * 512],
                            start=(kc == 0), stop=(kc == KC - 1)),
                            reads=[wR[b], self.hR[kc]], writes=[psR[pi]])
                P.op("act", lambda e, h=h: e.activation(out=sg[h][:], in_=ps[h][:], func=AF.Silu),
                     reads=[psR[h]], writes=[sgR[h]])
                P.op("dve", lambda e, h=h: e.tensor_tensor(out=g[gs][:, h * 512:(h + 1) * 512], in0=sg[h][:], in1=ps[2 + h][:],
                                                           op=ALU.mult),
                     reads=[sgR[h], psR[2 + h]], writes=[gR[gs][h]])
                if after_half is not None:
                    after_half(h)

        ycnt = [0]

        def wout(grp, dr=range(KC)):
            for d in dr:
                for h in range(2):
                    pi = 4 + (ycnt[0] % 4)
                    ycnt[0] += 1
                    for n, j in enumerate(grp):
                        P.op("pe", lambda e, pi=pi, j=j, d=d, h=h, n=n: e.matmul(
                            ps[pi][:], lhsT=wo[j % 4][:, d * 128:(d + 1) * 128], rhs=g[j % 4][:, h * 512:(h + 1) * 512],
                            start=(n == 0), stop=(n == len(grp) - 1)),
                            reads=[woR[j % 4], gR[j % 4][h]], writes=[psR[pi]])
                    P.op("dve", lambda e, pi=pi, d=d, h=h: e.scalar_tensor_tensor(
                        out=xT[:, d, h * 512:(h + 1) * 512], in0=ps[pi][:], scalar=0.5,
                        in1=xT[:, d, h * 512:(h + 1) * 512], op0=ALU.mult, op1=ALU.add),
                        reads=[psR[pi], self.xR[d][h]], writes=[self.xR[d][h]])

        dma_issue(0)
        dma_issue(1)
        cast_in(0)
        cast_wo(0)
        for j in range(NFF):
            if j + 2 < NFF:
                dma_issue(j + 2)
            if j + 1 < NFF:
                cast_in(j + 1)
            g0 = j - 2 if j % 2 == 0 else j - 3
            if g0 >= 0:
                win(j, lambda h, j=j, g0=g0: wout((g0, g0 + 1), range(4 * (2 * (j % 2) + h), 4 * (2 * (j % 2) + h) + 4)))
            else:
                win(j)
            if j + 1 < NFF:
                cast_wo(j + 1)
        wout((NFF - 2, NFF - 1))
        self.release(m)

    def fm_proj(self, w, col0, ncols, evac, tag, src=None, srcR=None):
        P, A = self.P, self.A
        src = self.hT if src is None else src
        srcR = self.hR if srcR is None else srcR
        m = A.mark()
        wv = w.rearrange("(kc p) f -> p kc f", p=128)
        nu = ncols // 128
        stg = [A.alloc("fstg", [128, KC, 128], F32) for _ in range(3)]
        stgR = [Res(f"fstg{i}") for i in range(3)]
        wb = [A.alloc("fwb", [128, KC, 128], BF16) for _ in range(2)]
        wbR = [Res("fwb0"), Res("fwb1")]
        ps, psR = self.ps, self.psR

        def issue(u):
            P.dma("sp", stg[u % 3][:], wv[:, :, col0 + u * 128:col0 + (u + 1) * 128], f"fst{u % 3}", writes=[stgR[u % 3]])

        def cast(u):
            eng = "act" if u % 2 == 0 else "pool"
            if eng == "act":
                P.op("act", lambda e: e.activation(out=wb[u % 2][:], in_=stg[u % 3][:], func=AF.Copy),
                     reads=[stgR[u % 3]], writes=[wbR[u % 2]])
            else:
                P.op("pool", lambda e: e.tensor_copy(out=wb[u % 2][:], in_=stg[u % 3][:]),
                     reads=[stgR[u % 3]], writes=[wbR[u % 2]])

        issue(0)
        if nu > 1:
            issue(1)
        cast(0)
        for u in range(nu):
            if u + 2 < nu:
                issue(u + 2)
            if u + 1 < nu:
                cast(u + 1)
            for h in range(2):
                pi = (2 * u + h) % 4
                for kc in range(KC):
                    P.op("pe", lambda e, pi=pi, kc=kc, h=h, u=u: e.matmul(
                        ps[pi][:], lhsT=wb[u % 2][:, kc, :], rhs=src[:, kc, h * 512:(h + 1) * 512],
                        start=(kc == 0), stop=(kc == KC - 1)), reads=[wbR[u % 2], srcR[kc]], writes=[psR[pi]])
                evac(u, h, ps[pi], psR[pi])
        self.release(m)

    def tm_proj(self, w, col0, ncols, evac, tag, src=None, srcR=None):
        P, A = self.P, self.A
        src = self.hT if src is None else src
        srcR = self.hR if srcR is None else srcR
        m = A.mark()
        wv = w.rearrange("(kc p) f -> p kc f", p=128)
        nb = ncols // 512
        stg = [A.alloc("tstg", [128, 4, 512], F32) for _ in range(4)]
        stgR = [Res(f"tstg{i}") for i in range(4)]
        wb = [A.alloc("twb", [128, KC, 512], BF16) for _ in range(2)]
        wbR = [[Res(f"twb{i}_{q}") for q in range(4)] for i in range(2)]
        ps, psR = self.ps, self.psR
        n = [0]
        for cb in range(nb):
            for q in range(4):
                s = n[0] % 4
                n[0] += 1
                P.dma("sp", stg[s][:], wv[:, q * 4:(q + 1) * 4, col0 + cb * 512:col0 + (cb + 1) * 512], f"tst{s}",
                      writes=[stgR[s]])
                if q % 2 == 0:
                    P.op("act", lambda e, s=s, q=q, cb=cb: e.activation(out=wb[cb % 2][:, q * 4:(q + 1) * 4, :], in_=stg[s][:], func=AF.Copy),
                         reads=[stgR[s]], writes=[wbR[cb % 2][q]])
                else:
                    P.op("pool", lambda e, s=s, q=q, cb=cb: e.tensor_copy(out=wb[cb % 2][:, q * 4:(q + 1) * 4, :], in_=stg[s][:]),
                         reads=[stgR[s]], writes=[wbR[cb % 2][q]])
            for tt in range(T // 128):
                pi = tt % 4
                for kc in range(KC):
                    P.op("pe", lambda e, pi=pi, kc=kc, tt=tt, cb=cb: e.matmul(
                        ps[pi][:], lhsT=src[:, kc, tt * 128:(tt + 1) * 128], rhs=wb[cb % 2][:, kc, :],
                        start=(kc == 0), stop=(kc == KC - 1)), reads=[wbR[cb % 2][kc // 4], srcR[kc]], writes=[psR[pi]])
                evac(cb, tt, ps[pi], psR[pi])
        self.release(m)

    def mlstm_proj(self, w_in, l, xq, xk, xv, xg, ogs, xR, ogsR, hook=None):
        P, A = self.P, self.A
        m = A.mark()
        ob = [A.alloc("ob", [128, T], BF16) for _ in range(3)]
        obR = [Res(f"ob{i}") for i in range(3)]
        o32 = [A.alloc("o32", [128, T], F32) for _ in range(2)]
        o32R = [Res("o32a"), Res("o32b")]
        tb = [A.alloc("tb", [128, 512], BF16) for _ in range(3)]
        tbR = [Res(f"tb{i}") for i in range(3)]

        def ev_qk(u, h, ps, psR):
            b = u % 3
            P.op("act", lambda e: e.activation(out=ob[b][:, h * 512:(h + 1) * 512], in_=ps[:], func=AF.Copy),
                 reads=[psR], writes=[obR[b]])
            if h == 1:
                head, j = (u, 0) if u < 8 else (u - 8, 1)
                P.dma("sp", xq[head // 4, head % 4, j], ob[b][:], f"ob{b}", reads=[obR[b], xR["q"]])

        self.fm_proj(w_in, 0, 2048, ev_qk, "qk")
        if hook:
            hook("q")

        def ev_og(u, h, ps, psR):
            b = u % 2
            P.op("act", lambda e: e.activation(out=o32[b][:, h * 512:(h + 1) * 512], in_=ps[:], func=AF.Sigmoid),
                 reads=[psR], writes=[o32R[b]])
            if h == 1:
                P.dma("sp", ogs[u * 128:(u + 1) * 128, :], o32[b][:], f"o32{b}", reads=[o32R[b]], writes=[ogsR[u]])

        self.fm_proj(w_in, 4096, 2048, ev_og, "og")

        cnt = [0]

        def ev_k(cb, tt, ps, psR):
            b = cnt[0] % 3
            cnt[0] += 1
            P.op("act", lambda e: e.activation(out=tb[b][:], in_=ps[:], func=AF.Copy), reads=[psR], writes=[tbR[b]])
            P.dma("sp", xk[cb, tt * 128:(tt + 1) * 128, :], tb[b][:], f"tb{b}", reads=[tbR[b], xR["k"]])

        self.tm_proj(w_in, 1024, 1024, ev_k, "k")
        if hook:
            hook("k")

        def ev_v(cb, tt, ps, psR):
            b = cnt[0] % 3
            cnt[0] += 1
            P.op("act", lambda e: e.activation(out=tb[b][:], in_=ps[:], func=AF.Copy), reads=[psR], writes=[tbR[b]])
            P.dma("sp", xv[cb // 2, tt * 128:(tt + 1) * 128, (cb % 2) * 512:(cb % 2 + 1) * 512], tb[b][:], f"tb{b}",
                  reads=[tbR[b], xR["v"]])

        self.tm_proj(w_in, 2048, 2048, ev_v, "v")
        if hook:
            hook("v")

        gs = A.alloc("gs", [128, KC, 16], F32)
        gsR = Res("gs")
        gw = A.alloc("gw", [128, KC, 16], BF16)
        gwR = Res("gw")
        gio = A.alloc("gio", [8, 2, T], F32)
        gioR = Res("gio")
        gt = A.alloc("gt", [8, T], F32)
        gtR = Res("gt")
        bb = A.alloc("bb", [8, 2], F32)
        bbR = Res("bb")
        wv = w_in.rearrange("(kc p) f -> p kc f", p=128)
        P.dma("sp", gs[:], wv[:, :, 6144:6160], "gs", writes=[gsR])
        P.op("dve", lambda e: e.tensor_copy(out=gw[:], in_=gs[:]), reads=[gsR], writes=[gwR])
        cb = 256 + 2 * l
        P.op("dve", lambda e: e.tensor_scalar(out=bb[:, 0:1], in0=self.cst[0:8, cb:cb + 1], scalar1=1.0 / 15.0, scalar2=None,
                                              op0=ALU.mult), reads=[self.cstR], writes=[bbR])
        P.op("dve", lambda e: e.tensor_scalar(out=bb[:, 1:2], in0=self.cst[0:8, cb + 1:cb + 2], scalar1=-1.0, scalar2=None,
                                              op0=ALU.mult), reads=[self.cstR, bbR], writes=[bbR])
        ps, psR = self.ps, self.psR
        for h in range(2):
            for gi in range(2):
                pi = 2 * h + gi
                for kc in range(KC):
                    P.op("pe", lambda e, pi=pi, kc=kc, h=h, gi=gi: e.matmul(
                        ps[pi][0:8, :], lhsT=gw[:, kc, gi * 8:(gi + 1) * 8], rhs=self.hT[:, kc, h * 512:(h + 1) * 512],
                        start=(kc == 0), stop=(kc == KC - 1)), reads=[gwR, self.hR[kc]], writes=[psR[pi]])
            P.op("act", lambda e, h=h: e.activation(out=gt[:, h * 512:(h + 1) * 512], in_=ps[2 * h][0:8, :], func=AF.Tanh,
                                                    scale=1.0 / 15.0, bias=bb[:, 0:1]), reads=[psR[2 * h], bbR], writes=[gtR])
            P.op("dve", lambda e, h=h: e.tensor_scalar(out=gio[:, 0, h * 512:(h + 1) * 512], in0=gt[:, h * 512:(h + 1) * 512],
                                                       scalar1=15.0, scalar2=None, op0=ALU.mult), reads=[gtR], writes=[gioR])
            P.op("act", lambda e, h=h: e.activation(out=gt[:, h * 512:(h + 1) * 512], in_=ps[2 * h + 1][0:8, :], func=AF.Exp,
                                                    scale=-1.0, bias=bb[:, 1:2]), reads=[psR[2 * h + 1], bbR, gioR], writes=[gtR])
            P.op("act", lambda e, h=h: e.activation(out=gt[:, h * 512:(h + 1) * 512], in_=gt[:, h * 512:(h + 1) * 512], func=AF.Ln,
                                                    scale=1.0, bias=self.one_ap()[0:8, :]), reads=[gtR, self.cstR], writes=[gtR])
            P.op("dve", lambda e, h=h: e.tensor_scalar(out=gio[:, 1, h * 512:(h + 1) * 512], in0=gt[:, h * 512:(h + 1) * 512],
                                                       scalar1=-1.0, scalar2=None, op0=ALU.mult), reads=[gtR], writes=[gioR])
        for d in range(2):
            for gi in range(2):
                P.dma("sp", xg[d, gi], gio[4 * d:4 * d + 4, gi, :], "gio", reads=[gioR, xR["g"]])
        self.release(m)

    def one_ap(self):
        return self.ones32[:, 0:1]

    def load_mix_consts(self, mc16, mc32, idx_d, sel_d, hng_d):
        P = self.P
        self.mc32 = mc32
        P.dma("sp", self.c16[:], mc16, "c16", writes=[self.c16R])
        P.dma("sp", self.c32[:], mc32[:, 0:512], "c32", writes=[self.c32R])
        P.dma("sp", self.idx[:], idx_d, "idx", writes=[self.idxR])
        P.dma("sp", self.sel[:], sel_d, "sel", writes=[self.selR])
        P.dma("sp", self.hgt[:], hng_d, "hgt", writes=[self.hgR])
        P.op("pool", lambda e: e.memset(self.on16[:], 1.0), writes=[self.c16R], reads=[])
        self.ident = self.c16[:, 0:128]
        self.negmask = self.c16[:, 128:640]
        self.maskd = self.c16[:, 640:768]
        self.utri = self.c16[:, 768:896]
        self.ind4 = self.c32[:, 0:512]

    def mlstm_mix(self, gq, gk, gv, gg, l, xh, gR_in, xhR, half_hook=None):
        P, A = self.P, self.A
        A.reset(self.low_mark)
        NT, NCH, L = 2048, 16, 128
        SC = 128 ** -0.5
        ps, psR = self.ps, self.psR
        qk = A.alloc("qk", [128, 4, 2, T], BF16)
        qkR = [Res(f"qk{hl}") for hl in range(4)]
        ktm = A.alloc("ktm", [128, 8, 512], BF16)
        vtm = A.alloc("vtm", [128, 8, 1024], BF16)
        kvR = Res("kv")
        hg = self.hgt[:, l * 8:(l + 1) * 8]
        hgR = self.hgR

        def load_half(hf):
            for hl in range(4):
                for j in range(2):
                    self.idma(qk[:, hl, j, :], gq, 16 + hf * 8 + hl * 2 + j, f"qk{hl}", reads=[gR_in], writes=[qkR[hl]])
            for tt in range(8):
                self.idma(ktm[:, tt, :], gk, hf * 8 + tt, "kvk", reads=[gR_in], writes=[kvR])
                self.idma(vtm[:, tt, :], gv, 16 + hf * 8 + tt, "kvv", reads=[gR_in], writes=[kvR])

        load_half(0)
        IG = A.alloc("IG", [4, NT], F32)
        LF = A.alloc("LF", [4, NT], F32)
        NM = A.alloc("NM", [4, NT], F32)
        R2 = A.alloc("R2", [4, NT], F32)
        MS = A.alloc("MS", [4, 32], F32)
        ON4 = A.alloc("ON4", [4, 128], F32)
        tA = A.alloc("tA", [4, T], F32)
        tB = A.alloc("tB", [4, T], F32)
        gR = Res("gates")
        G = [gR, self.c32R]
        P.dma("sp", NM[:], self.mc32[:, 512:2560], "mg", writes=[gR])
        P.dma("sp", R2[:], self.mc32[:, 2560:4608], "mg", writes=[gR])
        for hf in range(2):
            for gi, dst in enumerate((IG, LF)):
                P.dma("sp", tA[:], gg[hf * 16 + gi * 4:hf * 16 + gi * 4 + 4, :], "mg", reads=[gR_in], writes=[gR])
                P.dma("sp", tB[:], gg[hf * 16 + 8 + gi * 4:hf * 16 + 8 + gi * 4 + 4, :], "mg", reads=[gR_in], writes=[gR])
                P.op("dve", lambda e: e.tensor_scalar(out=tB[:], in0=tB[:], scalar1=self.sel[0:4, 1:2], scalar2=None, op0=ALU.mult),
                     reads=G + [self.selR], writes=[gR])
                P.op("dve", lambda e, dst=dst, hf=hf: e.scalar_tensor_tensor(out=dst[:, hf * T:(hf + 1) * T], in0=tA[:], scalar=self.sel[0:4, 0:1],
                                                                             in1=tB[:], op0=ALU.mult, op1=ALU.add),
                     reads=G + [self.selR], writes=[gR])
        c3 = lambda t: t[:].rearrange("p (c t) -> p c t", t=L)
        P.op("dve", lambda e: e.memset(ON4[:], 1.0), reads=G, writes=[gR])
        P.op("dve", lambda e: e.tensor_tensor_scan(out=LF[:], data0=NM[:], data1=LF[:], initial=0.0, op0=ALU.mult, op1=ALU.add),
             reads=G, writes=[gR])
        P.op("dve", lambda e: e.tensor_tensor(out=IG[:], in0=IG[:], in1=LF[:], op=ALU.subtract), reads=G, writes=[gR])
        P.op("dve", lambda e: e.tensor_tensor_scan(out=NM[:], data0=R2[:], data1=IG[:], initial=0.0, op0=ALU.add, op1=ALU.max),
             reads=G, writes=[gR])
        P.op("dve", lambda e: e.memset(MS[:], 0.0), reads=G, writes=[gR])
        P.op("dve", lambda e: e.tensor_tensor_scan(out=MS[:, 1:17], data0=c3(NM)[:, :, L - 1], data1=c3(LF)[:, :, L - 1], initial=0.0,
                                                   op0=ALU.max, op1=ALU.add), reads=G, writes=[gR])
        P.op("dve", lambda e: e.tensor_tensor(out=c3(NM), in0=c3(NM), in1=MS[:, 0:16].unsqueeze(2).to_broadcast([4, NCH, L]), op=ALU.max),
             reads=G, writes=[gR])
        P.op("dve", lambda e: e.tensor_scalar(out=NM[:], in0=NM[:], scalar1=-1.0, scalar2=None, op0=ALU.mult), reads=G, writes=[gR])
        nmm = [A.alloc("nmm", [4, 4, L], F32) for _ in range(2)]
        nmd = [A.alloc("nmd", [4, 4, L], F32) for _ in range(2)]
        nmt = [A.alloc("nmt", [4, 4, L], F32) for _ in range(2)]
        tmc = [A.alloc("tmc", [4, L], F32) for _ in range(2)]
        nmR = [Res("nm0"), Res("nm1")]
        ED = [A.alloc("ED", [128, 512], F32) for _ in range(2)]
        DEC = [A.alloc("DEC", [128, 512], F32) for _ in range(2)]
        EMT = [A.alloc("EMT", [128, 512], F32) for _ in range(2)]
        eR = [[Res(f"e{k}_{b}") for b in range(2)] for k in range(3)]
        swt = [A.alloc("swt", [128, L], BF16) for _ in range(2)]
        swR = [Res("sw0"), Res("sw1")]
        qd = [A.alloc("qd", [128, L], BF16) for _ in range(2)]
        qdR = [Res("qd0"), Res("qd1")]
        wv = [A.alloc("wv", [128, 384], BF16) for _ in range(2)]
        wvR = [Res("wv0"), Res("wv1")]
        rr = [A.alloc("rr", [128, L], F32) for _ in range(2)]
        rrR = [Res("rr0"), Res("rr1")]
        hs = [A.alloc("hs", [128, 2, L], F32) for _ in range(2)]
        hsR = [Res("hs0"), Res("hs1")]
        sq = [A.alloc("hsq", [128, 2, L], F32) for _ in range(2)]
        sqR = [Res("hsq0"), Res("hsq1")]
        ho = [A.alloc("ho", [128, 2, L], F32) for _ in range(3)]
        hoR = [Res(f"ho{i}") for i in range(3)]
        C32 = [A.alloc("C32", [128, 384], F32) for _ in range(4)]
        C16 = [A.alloc("C16", [128, 384], BF16) for _ in range(4)]
        cR = [Res(f"C32_{h}") for h in range(4)]
        c16R = [Res(f"C16_{h}") for h in range(4)]
        ind3 = self.ind4.rearrange("p (h t) -> p h t", t=L)
        f2 = lambda t: t[:].rearrange("p h t -> p (h t)")
        n = 0
        for c in range(NCH):
            cb = c % 2
            sl = slice(c * L, (c + 1) * L)
            hf = c // 8
            cl = c % 8
            lsl = slice(cl * L, (cl + 1) * L)
            if c == 8:
                load_half(1)
            def prep(c):
                cb = c % 2
                sl = slice(c * L, (c + 1) * L)
                bc = lambda t: t[:, sl].unsqueeze(1).to_broadcast([4, 4, L])
                P.op("dve", lambda e: e.tensor_tensor(out=nmm[cb][:], in0=ind3, in1=bc(NM), op=ALU.mult), reads=G + [nmR[cb]], writes=[nmR[cb]])
                P.op("dve", lambda e: e.scalar_tensor_tensor(out=nmd[cb][:], in0=bc(NM), scalar=MS[:, c:c + 1], in1=ind3, op0=ALU.add, op1=ALU.mult),
                     reads=G + [nmR[cb]], writes=[nmR[cb]])
                P.op("dve", lambda e: e.tensor_tensor(out=tmc[cb][:], in0=NM[:, sl], in1=LF[:, sl], op=ALU.subtract), reads=G + [nmR[cb]], writes=[nmR[cb]])
                P.op("dve", lambda e: e.tensor_tensor(out=nmt[cb][:], in0=ind3, in1=tmc[cb][:].unsqueeze(1).to_broadcast([4, 4, L]), op=ALU.mult),
                     reads=G + [nmR[cb]], writes=[nmR[cb]])
                P.op("pe", lambda e: e.matmul(ps[0][:], lhsT=self.ident, rhs=self.negmask, start=True, stop=False), reads=[self.c16R], writes=[psR[0]])
                P.op("pe", lambda e: e.matmul(ps[0][:], lhsT=IG[:, sl], rhs=self.ind4, start=False, stop=False), reads=G, writes=[psR[0]])
                P.op("pe", lambda e: e.matmul(ps[0][:], lhsT=ON4[:], rhs=f2(nmm[cb]), start=False, stop=True), reads=G + [nmR[cb]], writes=[psR[0]])
                P.op("pe", lambda e: e.matmul(ps[1][:], lhsT=ON4[:], rhs=f2(nmd[cb]), start=True, stop=True), reads=G + [nmR[cb]], writes=[psR[1]])
                P.op("pe", lambda e: e.matmul(ps[2][:], lhsT=ON4[:], rhs=f2(nmt[cb]), start=True, stop=True), reads=G + [nmR[cb]], writes=[psR[2]])
                for k, (dst, pi) in enumerate(((ED, 0), (DEC, 1), (EMT, 2))):
                    P.op("act", lambda e, dst=dst, pi=pi: e.activation(out=dst[cb][:], in_=ps[pi][:], func=AF.Exp),
                         reads=[psR[pi]], writes=[eR[k][cb]])

            if c == 0:
                prep(0)
            def st1(hl, c=c, cb=cb, lsl=lsl, cl=cl, hf=hf):
                b = hl % 2
                hsl = slice(hl * L, (hl + 1) * L)
                P.op("pe", lambda e: e.matmul(ps[3][:, b * L:(b + 1) * L], lhsT=qk[:, hl, 1, lsl], rhs=qk[:, hl, 0, lsl], start=True, stop=True),
                     reads=[qkR[hl]], writes=[psR[3]])

            def st2(hl, c=c, cb=cb, lsl=lsl, cl=cl, hf=hf):
                b = hl % 2
                hsl = slice(hl * L, (hl + 1) * L)
                P.op("dve", lambda e: e.scalar_tensor_tensor(out=swt[b][:], in0=ps[3][:, b * L:(b + 1) * L], scalar=SC, in1=ED[cb][:, hsl],
                                                             op0=ALU.mult, op1=ALU.mult),
                     reads=[psR[3], eR[0][cb]], writes=[swR[b]])
                if c > 0:
                    P.op("pool", lambda e: e.tensor_tensor(out=qd[b][:], in0=qk[:, hl, 0, lsl], in1=DEC[cb][:, hsl], op=ALU.mult),
                         reads=[qkR[hl], eR[1][cb]], writes=[qdR[b]])

            def st3(hl, c=c, cb=cb, lsl=lsl, cl=cl, hf=hf):
                b = hl % 2
                pn = 4 + b
                for vc in range(3):
                    lh = vtm[:, cl, hl * 256 + vc * 128: hl * 256 + (vc + 1) * 128] if vc < 2 else self.on16[:]
                    P.op("pe", lambda e, vc=vc, lh=lh: e.matmul(ps[pn][:, vc * L:(vc + 1) * L], lhsT=lh, rhs=swt[b][:], start=True, stop=(c == 0)),
                         reads=[kvR, swR[b], self.c16R], writes=[psR[pn]])
                    if c > 0:
                        P.op("pe", lambda e, vc=vc: e.matmul(ps[pn][:, vc * L:(vc + 1) * L], lhsT=C16[hl][:, vc * 128:(vc + 1) * 128], rhs=qd[b][:],
                                                             start=False, stop=True),
                             reads=[c16R[hl], qdR[b]], writes=[psR[pn]])

            def st4(hl, c=c, cb=cb, lsl=lsl, cl=cl, hf=hf):
                b = hl % 2
                pn = 4 + b
                hsl = slice(hl * L, (hl + 1) * L)
                P.op("act", lambda e: e.activation(out=rr[b][:], in_=ps[pn][:, 2 * L:3 * L], func=AF.Abs), reads=[psR[pn]], writes=[rrR[b]])
                P.op("dve", lambda e: e.tensor_tensor(out=rr[b][:], in0=rr[b][:], in1=EMT[cb][:, hsl], op=ALU.max),
                     reads=[eR[2][cb], rrR[b]], writes=[rrR[b]])
                P.op("act", lambda e: e.activation(out=rr[b][:], in_=rr[b][:], func=AF.Ln), reads=[rrR[b]], writes=[rrR[b]])
                P.op("act", lambda e: e.activation(out=rr[b][:], in_=rr[b][:], func=AF.Exp, scale=-1.0), reads=[rrR[b]], writes=[rrR[b]])
                P.op("dve", lambda e: e.tensor_tensor(out=hs[b][:], in0=ps[pn][:, 0:2 * L].rearrange("p (v t) -> p v t", t=L),
                                                      in1=rr[b][:].unsqueeze(1).to_broadcast([128, 2, L]), op=ALU.mult),
                     reads=[psR[pn], rrR[b]], writes=[hsR[b]])
                P.op("act", lambda e: e.activation(out=sq[b][:], in_=hs[b][:], func=AF.Square), reads=[hsR[b]], writes=[sqR[b]])

            def st5(hl, c=c, cb=cb, lsl=lsl, cl=cl, hf=hf):
                b = hl % 2
                for vc in range(2):
                    P.op("pe", lambda e, vc=vc: e.matmul(ps[3][:, (2 + b) * L:(3 + b) * L], lhsT=self.ones32[:], rhs=sq[b][:, vc, :], start=(vc == 0), stop=(vc == 1)),
                         reads=[sqR[b], self.onesR], writes=[psR[3]])

            def st6(hl, c=c, cb=cb, lsl=lsl, cl=cl, hf=hf):
                b = hl % 2
                P.op("act", lambda e: e.activation(out=rr[b][:], in_=ps[3][:, (2 + b) * L:(3 + b) * L], func=AF.Ln, scale=1.0 / 256.0, bias=self.eps_ap()),
                     reads=[psR[3], self.cstR, rrR[b]], writes=[rrR[b]])
                P.op("act", lambda e: e.activation(out=rr[b][:], in_=rr[b][:], func=AF.Exp, scale=-0.5), reads=[rrR[b]], writes=[rrR[b]])
                o = (c * 4 + hl) % 3
                for vc in range(2):
                    P.op("dve", lambda e, vc=vc: e.scalar_tensor_tensor(out=ho[o][:, vc, :], in0=hs[b][:, vc, :], scalar=hg[:, hl * 2 + vc:hl * 2 + vc + 1],
                                                                        in1=rr[b][:], op0=ALU.mult, op1=ALU.mult),
                         reads=[hsR[b], rrR[b], hgR], writes=[hoR[o]])
                P.dma("sp", xh[hf, hl * 256:(hl + 1) * 256, cl * L:(cl + 1) * L].rearrange("(v p) t -> p v t", p=128), ho[o][:], f"ho{o}",
                      reads=[hoR[o], xhR[hf]])

            def st7(hl, c=c, cb=cb, lsl=lsl, cl=cl, hf=hf):
                b = hl % 2
                pc = 6 if b == 0 else 7
                if c >= NCH - 1:
                    return
                wcol = ED[cb][:, hl * L + L - 1: hl * L + L]
                P.op("act", lambda e: e.activation(out=wv[b][:, 0:256], in_=vtm[:, cl, hl * 256:(hl + 1) * 256], func=AF.Copy, scale=wcol),
                     reads=[kvR, eR[0][cb]], writes=[wvR[b]])
                P.op("act", lambda e: e.activation(out=wv[b][:, 256:384], in_=self.on16[:], func=AF.Copy, scale=wcol),
                     reads=[self.c16R, eR[0][cb], wvR[b]], writes=[wvR[b]])
                P.op("pe", lambda e: e.matmul(ps[pc][:, 0:384], lhsT=ktm[:, cl, hl * 128:(hl + 1) * 128], rhs=wv[b][:], start=True, stop=True),
                     reads=[kvR, wvR[b]], writes=[psR[pc]])

            def st8(hl, c=c, cb=cb, lsl=lsl, cl=cl, hf=hf):
                b = hl % 2
                pc = 6 if b == 0 else 7
                if c >= NCH - 1:
                    return
                dcol = DEC[cb][:, hl * L + L - 1: hl * L + L]
                if c == 0:
                    P.op("dve", lambda e: e.tensor_copy(out=C32[hl][:], in_=ps[pc][:, 0:384]), reads=[psR[pc]], writes=[cR[hl]])
                else:
                    P.op("dve", lambda e: e.scalar_tensor_tensor(out=C32[hl][:], in0=C32[hl][:], scalar=dcol, in1=ps[pc][:, 0:384],
                                                                 op0=ALU.mult, op1=ALU.add),
                         reads=[psR[pc], cR[hl], eR[1][cb]], writes=[cR[hl]])
                P.op("pool", lambda e: e.tensor_scalar(out=C16[hl][:], in0=C32[hl][:], scalar1=SC, scalar2=0.0, op0=ALU.mult, op1=ALU.add),
                     reads=[cR[hl]], writes=[c16R[hl]])

            for hp in range(2):
                for st in (st1, st2, st3, st7, st4, st5, st8, st6):
                    for hl in (2 * hp, 2 * hp + 1):
                        st(hl)
                if hp == 0 and c + 1 < NCH:
                    prep(c + 1)
            if c == 7 and half_hook:
                half_hook(0)
        if half_hook:
            half_hook(1)
        self.release(self.low_mark)
        A.reset(self.phase_mark)

    def post_mix(self, gh, ghR, ogs, ogsR, w_out):
        P, A = self.P, self.A
        m = A.mark()
        hc = [A.alloc("hc", [128, T], F32) for _ in range(2)]
        oc = [A.alloc("oc", [128, T], F32) for _ in range(2)]
        hcR = [Res("hc0"), Res("hc1")]
        ocR = [Res("oc0"), Res("oc1")]
        for kc in range(KC):
            b = kc % 2
            self.idma(hc[b][:], gh, 32 + kc, f"hc{b}", reads=[ghR], writes=[hcR[b]])
            if ogs is not None:
                P.dma("sp", oc[b][:], ogs[kc * 128:(kc + 1) * 128, :], f"oc{b}", reads=[ogsR[kc]], writes=[ocR[b]])
                P.op("dve", lambda e, kc=kc, b=b: e.tensor_tensor(out=self.hT[:, kc, :], in0=hc[b][:], in1=oc[b][:], op=ALU.mult),
                     reads=[hcR[b], ocR[b]], writes=[self.hR[kc]])
            else:
                P.op("dve", lambda e, kc=kc, b=b: e.tensor_copy(out=self.hT[:, kc, :], in_=hc[b][:]),
                     reads=[hcR[b]], writes=[self.hR[kc]])
        self.release(m)

        def ev(u, h, ps, psR):
            P.op("dve", lambda e: e.tensor_tensor(out=self.xT[:, u, h * 512:(h + 1) * 512], in0=ps[:], in1=self.xT[:, u, h * 512:(h + 1) * 512],
                                                  op=ALU.add), reads=[psR, self.xR[u][h]], writes=[self.xR[u][h]])

        self.fm_proj(w_out, 0, D, ev, "wo")

    def load_x(self, xT_d):
        xv = xT_d.rearrange("(kc p) t -> p kc t", p=128)
        for kc in range(KC):
            self.P.dma("sp", self.xT[:, kc, :], xv[:, kc, :], f"xin{kc}", writes=self.xR[kc])

    def store_x(self, xT_d):
        xv = xT_d.rearrange("(kc p) t -> p kc t", p=128)
        for kc in range(KC):
            self.P.dma("sp", xv[:, kc, :], self.xT[:, kc, :], "xout", reads=self.xR[kc])

    def load_cst(self, cst_d):
        self.P.dma("sp", self.cst[:], cst_d, "cst", writes=[self.cstR])

    def sb_fmproj(self, w, col0, nheads, dst, xR, head0=0):
        P, A = self.P, self.A
        m = A.mark()
        ob = [A.alloc("sob", [128, T], BF16) for _ in range(3)]
        obR = [Res(f"sob{i}") for i in range(3)]

        def ev(u, h, ps, psR):
            b = u % 3
            P.op("act", lambda e: e.activation(out=ob[b][:, h * 512:(h + 1) * 512], in_=ps[:], func=AF.Copy),
                 reads=[psR], writes=[obR[b]])
            if h == 1:
                head = head0 + u
                P.dma("sp", dst[head // 8, head % 8], ob[b][:], f"sob{b}", reads=[obR[b], xR])

        self.fm_proj(w, col0, nheads * 128, ev, "sbfm")
        self.release(m)

    def sb_vproj(self, w, col0, dst, xR):
        P, A = self.P, self.A
        m = A.mark()
        tb = [A.alloc("stb", [128, 512], BF16) for _ in range(3)]
        tbR = [Res(f"stb{i}") for i in range(3)]
        cnt = [0]

        def ev(cb, tt, ps, psR):
            b = cnt[0] % 3
            cnt[0] += 1
            P.op("act", lambda e: e.activation(out=tb[b][:], in_=ps[:], func=AF.Copy), reads=[psR], writes=[tbR[b]])
            P.dma("sp", dst[cb // 2, tt * 128:(tt + 1) * 128, (cb % 2) * 512:(cb % 2 + 1) * 512], tb[b][:], f"stb{b}",
                  reads=[tbR[b], xR])

        self.tm_proj(w, col0, 2048, ev, "sbv")
        self.release(m)

    def sb_mix(self, gqq, gkk, gvv, xo, gR_in, xoR, half_hook=None):
        P, A = self.P, self.A
        A.reset(self.low_mark)
        NT, L = 2048, 128
        SC = 128 ** -0.5
        ps, psR = self.ps, self.psR
        qT = A.alloc("sqT", [128, 8, NT], BF16)
        kT = A.alloc("skT", [128, 8, NT], BF16)
        vtm = A.alloc("svtm", [128, 16, 1024], BF16)
        qR = [Res(f"sq{h}") for h in range(8)]
        kR = [Res(f"sk{h}") for h in range(8)]
        vR = [Res("sv0"), Res("sv1")]
        gRq, gRkv = gR_in
        for hf in range(2):
            for tt in range(8):
                self.idma(vtm[:, hf * 8 + tt, :], gvv, 16 + hf * 8 + tt, f"sv{hf}", reads=[gRkv], writes=[vR[hf]])
        for hl in range(8):
            for hf in range(2):
                self.idma(kT[:, hl, hf * 1024:(hf + 1) * 1024], gkk, 16 + hf * 8 + hl, f"sk{hl}", reads=[gRkv], writes=[kR[hl]])
        for hl in range(8):
            for hf in range(2):
                self.idma(qT[:, hl, hf * 1024:(hf + 1) * 1024], gqq, 16 + hf * 8 + hl, f"sq{hl}", reads=[gRq], writes=[qR[hl]])
        E = [A.alloc("sE", [128, 512], F32) for _ in range(2)]
        SP = [A.alloc("sSP", [128, 512], F32) for _ in range(2)]
        LB = [A.alloc("sLB", [128, 512], BF16) for _ in range(3)]
        T1 = [A.alloc("sT1", [128, 512], F32) for _ in range(5)]
        AT = [A.alloc("sAT", [128, 512], BF16) for _ in range(2)]
        OB = [A.alloc("sOB", [128, 512], F32) for _ in range(2)]
        eR = [Res(f"sE{i}") for i in range(2)]
        spR = [Res(f"sSP{i}") for i in range(2)]
        lbR = [Res(f"sLB{i}") for i in range(3)]
        t1R = [Res(f"sT1{i}") for i in range(5)]
        atR = [Res("sAT0"), Res("sAT1")]
        obR = [Res("sOB0"), Res("sOB1")]
        tiles = []
        grp = 0
        for G in range(4):
            for hl in range(8):
                kbs = list(range(4 * G + 3, -1, -1))
                for kb in kbs:
                    tiles.append((hl, G, kb, grp % 2, kb == kbs[0], kb == 0))
                grp += 1
        nt = len(tiles)
        last_d0 = max(i for i, t_ in enumerate(tiles) if t_[1] == 1)

        class Gm:
            pass

        def geom(k):
            g = Gm()
            g.hl, g.G, g.kb, g.g2, g.first, g.last = tiles[k]
            g.c0 = max(g.kb, 4 * g.G) - 4 * g.G
            g.diag = g.kb >= 4 * g.G
            r0 = (g.c0 + 1) * L if g.diag else 0
            g.cs = slice(g.c0 * L, 512)
            g.ds = slice(g.c0 * L, (g.c0 + 1) * L)
            g.rs = slice(r0, 512)
            g.has_rt = r0 < 512
            g.pz, g.pa = k % 2, 2 + k % 2
            g.prt = 4 if g.g2 == 0 else 7
            g.po = 5 + g.g2
            g.q0 = (4 * g.G + g.c0) * L
            g.e, g.sp, g.lb, g.t1, g.at = k % 2, k % 2, k % 3, k % 5, k % 2
            return g

        def pe_z(k):
            g = geom(k)
            P.op("pe", lambda e: e.matmul(ps[g.pz][:, g.cs], lhsT=kT[:, g.hl, g.kb * L:(g.kb + 1) * L], rhs=qT[:, g.hl, g.q0:(4 * g.G + 4) * L], start=True, stop=True),
                 reads=[kR[g.hl], qR[g.hl]], writes=[psR[g.pz]])

        def act_a(k):
            g = geom(k)
            P.op("act", lambda e: e.activation(out=E[g.e][:, g.cs], in_=ps[g.pz][:, g.cs], func=AF.Exp, scale=SC), reads=[psR[g.pz]], writes=[eR[g.e]])
            P.op("act", lambda e: e.activation(out=SP[g.sp][:, g.cs], in_=E[g.e][:, g.cs], func=AF.Ln, scale=1.0, bias=self.one_ap()),
                 reads=[eR[g.e], self.onesR], writes=[spR[g.sp]])

        def dve_t1(k):
            g = geom(k)
            P.op("dve", lambda e: e.scalar_tensor_tensor(out=T1[g.t1][:, g.cs], in0=ps[g.pz][:, g.cs], scalar=SC, in1=SP[g.sp][:, g.cs], op0=ALU.mult, op1=ALU.subtract),
                 reads=[psR[g.pz], spR[g.sp]], writes=[t1R[g.t1]])

        def pool_lb(k):
            g = geom(k)
            P.op("pool", lambda e: e.tensor_scalar(out=LB[g.lb][:, g.cs], in0=SP[g.sp][:, g.cs], scalar1=-1.0, scalar2=0.0, op0=ALU.mult, op1=ALU.add),
                 reads=[spR[g.sp]], writes=[lbR[g.lb]])
            if g.diag:
                P.op("pool", lambda e: e.tensor_tensor(out=LB[g.lb][:, g.ds], in0=LB[g.lb][:, g.ds], in1=self.maskd, op=ALU.mult),
                     reads=[lbR[g.lb], self.c16R], writes=[lbR[g.lb]])

        def pe_u(k):
            g = geom(k)
            P.op("pe", lambda e: e.matmul(ps[g.pa][:, g.cs], lhsT=self.utri, rhs=LB[g.lb][:, g.cs], start=True, stop=True),
                 reads=[lbR[g.lb], self.c16R], writes=[psR[g.pa]])

        def dve_x(k):
            g = geom(k)
            P.op("dve", lambda e: e.tensor_tensor(out=T1[g.t1][:, g.cs], in0=ps[g.pa][:, g.cs], in1=T1[g.t1][:, g.cs], op=ALU.add),
                 reads=[psR[g.pa], t1R[g.t1]], writes=[t1R[g.t1]])
            if g.has_rt:
                P.op("dve", lambda e: e.tensor_tensor(out=T1[g.t1][:, g.rs], in0=ps[g.prt][:, g.rs], in1=T1[g.t1][:, g.rs], op=ALU.add),
                     reads=[psR[g.prt], t1R[g.t1]], writes=[t1R[g.t1]])

        def pe_ones(k):
            g = geom(k)
            if not g.last:
                P.op("pe", lambda e: e.matmul(ps[g.prt][:, g.cs], lhsT=self.on16[:], rhs=LB[g.lb][:, g.cs], start=g.first, stop=(g.kb == 1)),
                     reads=[lbR[g.lb], self.c16R], writes=[psR[g.prt]])

        def act_b(k):
            g = geom(k)
            P.op("act", lambda e: e.activation(out=AT[g.at][:, g.cs], in_=T1[g.t1][:, g.cs], func=AF.Exp), reads=[t1R[g.t1]], writes=[atR[g.at]])

        def pool_at(k):
            g = geom(k)
            if g.diag:
                P.op("pool", lambda e: e.tensor_tensor(out=AT[g.at][:, g.ds], in0=AT[g.at][:, g.ds], in1=self.maskd, op=ALU.mult),
                     reads=[atR[g.at], self.c16R], writes=[atR[g.at]])

        def pe_av(k):
            g = geom(k)
            P.op("pe", lambda e: e.matmul(ps[g.po][:, g.cs], lhsT=vtm[:, g.kb, g.hl * L:(g.hl + 1) * L], rhs=AT[g.at][:, g.cs], start=g.first, stop=g.last),
                 reads=[vR[g.kb // 8], atR[g.at]], writes=[psR[g.po]])
            if g.last:
                P.op("act", lambda e: e.activation(out=OB[g.g2][:], in_=ps[g.po][:], func=AF.Copy), reads=[psR[g.po]], writes=[obR[g.g2]])
                P.dma("sp", xo[g.G // 2, g.hl * L:(g.hl + 1) * L, (g.G % 2) * 512:(g.G % 2 + 1) * 512], OB[g.g2][:], f"sOB{g.g2}", reads=[obR[g.g2], xoR[g.G // 2]])
                if half_hook and k == last_d0:
                    half_hook(0)

        ok = lambda k: 0 <= k < nt
        for j in range(nt + 7):
            if ok(j - 6):
                pe_av(j - 6)
            if ok(j - 3):
                pe_u(j - 3)
            if ok(j):
                pe_z(j)
            if ok(j - 4):
                dve_x(j - 4)
                pe_ones(j - 4)
            if ok(j - 5):
                act_b(j - 5)
                pool_at(j - 5)
            if ok(j - 1):
                act_a(j - 1)
                dve_t1(j - 1)
            if ok(j - 2):
                pool_lb(j - 2)
        if half_hook:
            half_hook(1)
        self.release(self.low_mark)
        A.reset(self.phase_mark)


def _mk(nc):
    def I(name, shape, dt=F32):
        return nc.dram_tensor(name, list(shape), dt, kind="ExternalInput").ap()

    def O(name, shape, dt=F32):
        return nc.dram_tensor(name, list(shape), dt, kind="ExternalOutput").ap()

    return I, O


A_IN = 6160
XQ = (2, 4, 2, 128, T)
XK = (2, T, 512)
XV = (2, T, 1024)
XG = (2, 2, 4, T)
XH = (2, 1024, T)
SQ = (2, 8, 128, T)
SV = (2, T, 1024)


def build_fused(dbg=None):
    nc = bass.Bass("TRN2", target_bir_lowering=False)
    I, O = _mk(nc)

    def N(name, shape, dt=F32):
        return nc.dram_tensor(name, list(shape), dt, kind="Internal").ap()

    B = Builder(nc, fused=True)
    P = B.P
    cst = I("cst", [128, NCST])
    idx = I("idx", [128, 48], mybir.dt.uint32)
    sel = I("sel", [128, 2])
    hng = I("hng", [128, 16])
    mc16, mc32 = I("mc16", [128, 896], BF16), I("mc32", [4, 4608])
    xT = I("xT", [D, T])
    yT = O("yT", [D, T])
    if dbg is None:
        f1i, f1o = I("ffn1_w_in", [4, D, 2 * DFF]), I("ffn1_w_out", [4, DFF, D])
        f2i, f2o = I("ffn2_w_in", [4, D, 2 * DFF]), I("ffn2_w_out", [4, DFF, D])
        kvw = I("kv_w", [D, 2 * D])
        bwq, bwo = I("b_w_q", [2, D, D]), I("b_w_out", [2, D, D])
    awi, awo = I("a_w_in", [2, D, A_IN]), I("a_w_out", [2, D, D])
    xq, gq = N("xq", [2048, T], BF16), N("gq", [4096, T], BF16)
    xk, gk = N("xk", [2048, 512], BF16), N("gk", [4096, 512], BF16)
    xv, gv = N("xv", [2048, 1024], BF16), N("gv", [4096, 1024], BF16)
    xg, gg = N("xg", [16, T]), N("gg", [32, T])
    xh, gh = N("xh", [2048, T]), N("gh", [4096, T])
    xkk, gkk = N("xkk", [2048, T], BF16), N("gkk", [4096, T], BF16)
    ogs = N("ogs", [D, T])
    xR = {k: Res("xbuf_" + k) for k in ("q", "k", "v", "g", "kk")}
    gR, ghR = Res("gbuf"), Res("ghbuf")
    gRq, gRkv = Res("gbuf_q"), Res("gbuf_kv")
    xhR = [Res("xh0"), Res("xh1")]
    ogsR = [Res(f"ogs{i}") for i in range(KC)]
    xq5 = xq.rearrange("(d h j p) t -> d h j p t", d=2, h=4, j=2)
    xk3 = xk.rearrange("(d t) c -> d t c", d=2)
    xv3 = xv.rearrange("(d t) c -> d t c", d=2)
    xg4 = xg.rearrange("(d g h) t -> d g h t", d=2, g=2)
    xh3 = xh.rearrange("(d r) t -> d r t", d=2)
    xqq4 = xq.rearrange("(d h p) t -> d h p t", d=2, h=8)
    xkk4 = xkk.rearrange("(d h p) t -> d h p t", d=2, h=8)

    def xh_hook(hf):
        for c in (2 * hf, 2 * hf + 1):
            P.coll("AllGather", xh[c * 512:(c + 1) * 512, :], gh[c * 1024:(c + 1) * 1024, :], "cc", writes=[xhR[hf], ghR])

    def proj_hook(which):
        a_, g_, n_ = {"q": (xq, gq, 1024), "k": (xk, gk, 2048), "v": (xv, gv, 1024)}[which]
        P.coll_rows(a_, g_, n_, writes=[xR[which], gR])

    B.load_cst(cst)
    B.load_mix_consts(mc16, mc32, idx, sel, hng)
    B.load_x(xT)
    for l in range(2):
        B.rmsnorm(l)
        B.ffn(f1i[l], f1o[l], "f1")
        B.rmsnorm(4 + l)
        B.mlstm_proj(awi[l], l, xq5, xk3, xv3, xg4, ogs, xR, ogsR, hook=proj_hook)
        P.coll_rows(xg, gg, 16, writes=[xR["g"], gR])
        B.mlstm_mix(gq, gk, gv, gg, l, xh3, gR, xhR, half_hook=xh_hook)
        B.post_mix(gh, ghR, ogs, ogsR, awo[l])
        B.rmsnorm(8 + l)
        B.ffn(f2i[l], f2o[l], "f2")
    B.rmsnorm(12)
    B.sb_fmproj(kvw, 0, 16, xkk4, xR["kk"])
    P.coll_rows(xkk, gkk, 1024, writes=[xR["kk"], gRkv])
    B.sb_vproj(kvw, 2048, xv3, xR["v"])
    P.coll_rows(xv, gv, 1024, reads=[gR], writes=[xR["v"], gRkv])
    for j in range(2):
        l = 2 + j
        B.rmsnorm(l)
        B.ffn(f1i[l], f1o[l], "f1")
        B.rmsnorm(4 + l)
        B.sb_fmproj(bwq[j], 0, 16, xqq4, xR["q"])
        P.coll_rows(xq, gq, 1024, reads=[gR], writes=[xR["q"], gRq])
        B.sb_mix(gq, gkk, gv, xh3, (gRq, gRkv), xhR, half_hook=xh_hook)
        B.post_mix(gh, ghR, None, None, bwo[j])
        B.rmsnorm(8 + l)
        B.ffn(f2i[l], f2o[l], "f2")
    B.final_norm(13)
    B.store_x(yT)
    P.emit()
    return nc


def make_idx(r):
    idx = np.zeros((128, 48), np.uint32)
    p = np.arange(128)
    for t, n in enumerate((2048, 1024, 512)):
        for hf in range(2):
            for m_ in range(8):
                R = r * 1024 + m_ * 128 + p
                idx[:, t * 16 + hf * 8 + m_] = (R // n) * 2 * n + hf * n + (R % n)
    return idx


def mix_consts():
    import ml_dtypes
    c16 = np.zeros((128, 896), np.float32)
    c16[:, 0:128] = np.eye(128)
    s = np.arange(128)[:, None]
    t = np.arange(128)[None, :]
    nm = np.where(s > t, -30000.0, 0.0)
    c16[:, 128:640] = np.tile(nm, (1, 4))
    c16[:, 640:768] = (s < t)
    c16[:, 768:896] = (s > t)
    c32 = np.zeros((4, 4608), np.float32)
    for h in range(4):
        c32[h, h * 128:(h + 1) * 128] = 1.0
    r1 = np.ones(2048, np.float32)
    r1[::128] = 0.0
    r2 = np.zeros(2048, np.float32)
    r2[::128] = -1e30
    c32[:, 512:2560] = r1
    c32[:, 2560:4608] = r2
    return c16.astype(ml_dtypes.bfloat16), c32


def pack_cst(inp):
    c = np.zeros((128, NCST), np.float32)
    vecs = [inp["ffn1_norm"][l] for l in range(4)] + [inp["mix_norm"][l] for l in range(4)] + \
           [inp["ffn2_norm"][l] for l in range(4)] + [inp["kv_norm"], inp["final_norm"]] + \
           [inp["a_head_norm"][l] for l in range(2)]
    for i, v in enumerate(vecs):
        c[:, i * 16:(i + 1) * 16] = np.asarray(v, np.float32).reshape(16, 128).T
    for l in range(2):
        c[0:8, 256 + 2 * l] = inp["a_b_gate"][l][0:8]
        c[0:8, 257 + 2 * l] = inp["a_b_gate"][l][8:16]
    c[:, EPSC] = EPS
    return c


_NC = []


def kernel(**inp):
    inp = {k: np.asarray(v) for k, v in inp.items()}
    NCORE = 8
    if not _NC:
        _NC.append(build_fused())
    nc = _NC[0]
    cst = pack_cst(inp)
    mc16, mc32 = mix_consts()
    x = inp["x"]
    ca = np.ascontiguousarray
    wnames = ["ffn1_w_in", "ffn1_w_out", "ffn2_w_in", "ffn2_w_out", "a_w_in", "a_w_out", "kv_w", "b_w_q", "b_w_out"]
    w = {k: ca(inp[k], dtype=np.float32) for k in wnames}
    maps = []
    p = np.arange(128, dtype=np.uint32)[:, None]
    for c in range(NCORE):
        b, r = c // 2, c % 2
        idx = make_idx(r)
        sel = np.zeros((128, 2), np.float32)
        sel[:, 0] = 1.0 - r
        sel[:, 1] = float(r)
        hng = np.zeros((128, 16), np.float32)
        for l in range(2):
            hng[:, l * 8:(l + 1) * 8] = inp["a_head_norm"][l].reshape(8, 2, 128)[4 * r:4 * r + 4].reshape(8, 128).T
        mp = {"cst": cst, "idx": idx, "sel": sel, "hng": hng, "mc16": mc16, "mc32": mc32,
              "xT": ca(x[b, r * T:(r + 1) * T].T)}
        mp.update(w)
        maps.append(mp)
    res = run_bass_kernel_spmd(nc, maps, core_ids=list(range(NCORE)))
    out = np.empty((4, 2048, D), np.float32)
    for c in range(NCORE):
        b, r = c // 2, c % 2
        out[b, r * T:(r + 1) * T, :] = res.results[c]["yT"].T
    return out
```

```python
import numpy as np
import concourse.bass as bass
import concourse.mybir as mybir
from concourse.bass_utils import run_bass_kernel_spmd

F32 = mybir.dt.float32
BF16 = mybir.dt.bfloat16
AF = mybir.ActivationFunctionType
ALU = mybir.AluOpType

D = 2048
KC = 16
DFF = 5632
NFF = DFF // 128
T = 1024
EPS = 1e-6

ENGS = ["pe", "act", "dve", "pool", "sp"]
PAIRS = [[0, 1], [2, 3], [4, 5], [6, 7]]


class Res:
    __slots__ = ("name", "w", "r")

    def __init__(self, name):
        self.name = name
        self.w = None
        self.r = []


class Op:
    __slots__ = ("eng", "fn", "deps", "inc", "cnt", "dma_sem", "dma_val", "is_coll")

    def __init__(self, eng, fn):
        self.eng = eng
        self.fn = fn
        self.deps = set()
        self.inc = False
        self.cnt = 0
        self.dma_sem = None
        self.dma_val = 0
        self.is_coll = False


class Prog:
    def __init__(self, nc):
        self.nc = nc
        self.ops = {e: [] for e in ENGS}
        self.dma_cnt = {}
        self.globalR = Res("global")
        self.rank = {}
        self.use_rank = False

    def _track(self, o, reads, writes):
        deps = set()
        if self.globalR not in writes:
            reads = list(reads) + [self.globalR]
        for r in reads:
            if r.w is not None:
                deps.add(r.w)
        for w in writes:
            if w.w is not None:
                deps.add(w.w)
            deps.update(w.r)
        deps.discard(o)
        o.deps = deps
        for r in reads:
            r.r.append(o)
        for w in writes:
            w.w = o
            w.r = []

    def op(self, eng, fn, reads=(), writes=()):
        o = Op(eng, fn)
        self._track(o, reads, writes)
        self.ops[eng].append(o)
        return o

    def dma(self, queue, out, in_, sem, reads=(), writes=(), **kw):
        def fn(e):
            src = in_(self.rank[queue]) if callable(in_) else in_
            return e.dma_start(out=out, in_=src, **kw)

        o = Op(queue, fn)
        o.dma_sem = sem
        self.dma_cnt[sem] = self.dma_cnt.get(sem, 0) + 16
        o.dma_val = self.dma_cnt[sem]
        self._track(o, reads, writes)
        self.ops[queue].append(o)
        return o

    def coll(self, kind, in_ap, out_ap, sem, reads=(), writes=()):
        o = Op("pool", lambda e: e.collective_compute(kind, ALU.bypass, replica_groups=PAIRS, ins=[in_ap.opt()], outs=[out_ap.opt()]))
        o.dma_sem = sem
        o.is_coll = True
        self.dma_cnt[sem] = self.dma_cnt.get(sem, 0) + 1
        o.dma_val = self.dma_cnt[sem]
        self._track(o, reads, writes)
        self.ops["pool"].append(o)
        return o

    def coll_rows(self, x2d, g2d, n, reads=(), writes=()):
        rows = x2d.shape[0]
        for c in range(rows // n):
            self.coll("AllGather", x2d[c * n:(c + 1) * n, :], g2d[c * 2 * n:(c + 1) * 2 * n, :], "cc", reads=reads, writes=writes)

    def barrier(self, scratch):
        self.op("dve", lambda e: e.memset(scratch, 0.0), writes=[self.globalR])

    def emit(self, final_dma_ops=()):
        nc = self.nc
        for e in ENGS:
            for o in self.ops[e]:
                for d in o.deps:
                    if d.dma_sem is None:
                        if d.eng == "pe" and o.eng == "pe":
                            continue
                        d.inc = True
        for e in ENGS:
            c = 0
            for o in self.ops[e]:
                if o.dma_sem is None and o.inc:
                    c += 1
                    o.cnt = c
        from contextlib import ExitStack

        with ExitStack() as st:
            esem = {e: st.enter_context(nc.semaphore("s_" + e)) for e in ENGS}
            dsem = {k: st.enter_context(nc.semaphore("d_" + k)) for k in self.dma_cnt}
            block = st.enter_context(nc.Block())

            def run(eng_name, eng):
                waited = {}
                if self.use_rank and eng_name == "sp":
                    self.rank[eng_name] = eng.cc_rank(PAIRS)
                for o in self.ops[eng_name]:
                    need = {}
                    for d in o.deps:
                        if d.dma_sem is not None:
                            key = ("d", d.dma_sem)
                            val = d.dma_val
                        else:
                            if d.eng == "pe" and eng_name == "pe":
                                continue
                            key = ("e", d.eng)
                            val = d.cnt
                        if val > need.get(key, 0):
                            need[key] = val
                    for key, val in need.items():
                        if val > waited.get(key, 0):
                            waited[key] = val
                            s = dsem[key[1]] if key[0] == "d" else esem[key[1]]
                            eng.wait_ge(s, val)
                    ins = o.fn(eng)
                    if o.is_coll:
                        ins.then_inc(dsem[o.dma_sem], 1)
                    elif o.dma_sem is not None:
                        ins.then_inc(dsem[o.dma_sem], 16)
                    elif o.inc:
                        ins.then_inc(esem[eng_name], 1)
                if eng_name == "sp":
                    for k, v in self.dma_cnt.items():
                        eng.wait_ge(dsem[k], v)

            @block.tensor
            def _(e):
                run("pe", e)

            @block.scalar
            def _(e):
                run("act", e)

            @block.vector
            def _(e):
                run("dve", e)

            @block.gpsimd
            def _(e):
                run("pool", e)

            @block.sync
            def _(e):
                run("sp", e)


class Arena:
    def __init__(self, nc, nbytes):
        self.nc = nc
        self.slab = nc.alloc_sbuf_tensor("arena_slab", [128, nbytes // 4], F32)
        self.base = nc.lookup_mloc(self.slab).addr
        self.nbytes = nbytes
        self.off = 0
        self.n = 0
        self.hi = 0

    def alloc(self, name, shape, dtype):
        sz = 1
        for s in shape[1:]:
            sz *= s
        sz *= 2 if dtype == BF16 else 4
        sz = (sz + 63) // 64 * 64
        assert self.off + sz <= self.nbytes, (name, self.off, sz, self.nbytes)
        self.n += 1
        t = self.nc.alloc_sbuf_tensor_at(f"{name}_{self.n}", list(shape), dtype, offset=self.base + self.off)
        self.off += sz
        self.hi = max(self.hi, self.off)
        return t

    def mark(self):
        return self.off

    def reset(self, m):
        self.off = m


ARENA = 204 * 1024
NCST = 264
EPSC = 260


class Builder:
    def __init__(self, nc, resident=True, fused=False):
        self.nc = nc
        self.fused = fused
        self.P = Prog(nc)
        self.A = Arena(nc, ARENA)
        A = self.A
        self.xT = A.alloc("xT", [128, KC, T], F32)
        self.xR = [[Res(f"x{k}_{h}") for h in range(2)] for k in range(KC)]
        self.cst = A.alloc("cst", [128, NCST], F32)
        self.cstR = Res("cst")
        self.ones32 = A.alloc("ones32", [128, 128], F32)
        self.onesR = Res("ones32")
        self.rs = A.alloc("rs", [128, T], F32)
        self.rsR = Res("rs")
        self.ps = [nc.alloc_psum_tensor(f"ps{i}", [128, 512], F32) for i in range(8)]
        self.psR = [Res(f"ps{i}") for i in range(8)]
        self.P.op("pool", lambda e: e.memset(self.ones32[:], 1.0), writes=[self.onesR])
        self.bscr = A.alloc("bscr", [128, 16], F32)
        self.idx = A.alloc("idx", [128, 48], mybir.dt.uint32)
        self.idxR = Res("idx")
        self.sel = A.alloc("sel", [128, 2], F32)
        self.selR = Res("sel")
        self.hgt = A.alloc("hgt", [128, 16], F32)
        self.hgR = Res("hgt")
        self.c16 = A.alloc("c16", [128, 1024], BF16)
        self.c16R = Res("c16")
        self.c32 = A.alloc("c32", [4, 512], F32)
        self.c32R = Res("c32")
        self.on16 = A.alloc("on16", [128, 128], BF16)
        self.low_mark = A.mark()
        self.hT = A.alloc("hT", [128, KC, T], BF16)
        self.hR = [Res(f"h{k}") for k in range(KC)]
        self.phase_mark = A.mark()

    def idma(self, out, in2d, col, sem, reads=(), writes=()):
        P = self.P
        off = bass.IndirectOffsetOnAxis(ap=self.idx[:, col:col + 1], axis=0)
        o = Op("pool", lambda e: e.indirect_dma_start(out=out, out_offset=None, in_=in2d, in_offset=off))
        o.dma_sem = sem
        P.dma_cnt[sem] = P.dma_cnt.get(sem, 0) + 16
        o.dma_val = P.dma_cnt[sem]
        P._track(o, list(reads) + [self.idxR], writes)
        P.ops["pool"].append(o)
        return o

    def release(self, m):
        self.A.reset(m)
        self.P.barrier(self.bscr[:, 0:1])

    def rmsnorm(self, gi, out=None, outR=None):
        P, A = self.P, self.A
        out = self.hT if out is None else out
        outR = self.hR if outR is None else outR
        m = A.mark()
        sq = [A.alloc("sq", [128, T], F32) for _ in range(2)]
        sqR = [Res("sq0"), Res("sq1")]
        xT, ones32, rs, ps, psR = self.xT, self.ones32, self.rs, self.ps, self.psR
        for kc in range(KC):
            b = kc % 2
            P.op("act", lambda e, kc=kc, b=b: e.activation(out=sq[b][:], in_=xT[:, kc, :], func=AF.Square),
                 reads=self.xR[kc], writes=[sqR[b]])
            for h in range(2):
                P.op("pe", lambda e, kc=kc, b=b, h=h: e.matmul(ps[6 + h][:], lhsT=ones32[:], rhs=sq[b][:, h * 512:(h + 1) * 512],
                                                              start=(kc == 0), stop=(kc == KC - 1)),
                     reads=[sqR[b], self.onesR], writes=[psR[6 + h]])
        for h in range(2):
            P.op("act", lambda e, h=h: e.activation(out=rs[:, h * 512:(h + 1) * 512], in_=ps[6 + h][:], func=AF.Sqrt,
                                                    scale=1.0 / D, bias=self.eps_ap()),
                 reads=[psR[6 + h], self.cstR], writes=[self.rsR])
        P.op("dve", lambda e: e.reciprocal(out=rs[:], in_=rs[:]), reads=[self.rsR], writes=[self.rsR])
        for kc in range(KC):
            P.op("dve", lambda e, kc=kc: e.scalar_tensor_tensor(out=out[:, kc, :], in0=xT[:, kc, :],
                                                                scalar=self.cst[:, gi * 16 + kc:gi * 16 + kc + 1], in1=rs[:],
                                                                op0=ALU.mult, op1=ALU.mult),
                 reads=self.xR[kc] + [self.rsR, self.cstR], writes=[outR[kc]])
        self.release(m)

    def final_norm(self, gi):
        P, A = self.P, self.A
        m = A.mark()
        sq = [A.alloc("sq", [128, T], F32) for _ in range(2)]
        sqR = [Res("sq0"), Res("sq1")]
        xT, ones32, rs, ps, psR = self.xT, self.ones32, self.rs, self.ps, self.psR
        for kc in range(KC):
            b = kc % 2
            P.op("act", lambda e, kc=kc, b=b: e.activation(out=sq[b][:], in_=xT[:, kc, :], func=AF.Square),
                 reads=self.xR[kc], writes=[sqR[b]])
            for h in range(2):
                P.op("pe", lambda e, kc=kc, b=b, h=h: e.matmul(ps[6 + h][:], lhsT=ones32[:], rhs=sq[b][:, h * 512:(h + 1) * 512],
                                                              start=(kc == 0), stop=(kc == KC - 1)),
                     reads=[sqR[b], self.onesR], writes=[psR[6 + h]])
        for h in range(2):
            P.op("act", lambda e, h=h: e.activation(out=rs[:, h * 512:(h + 1) * 512], in_=ps[6 + h][:], func=AF.Sqrt,
                                                    scale=1.0 / D, bias=self.eps_ap()),
                 reads=[psR[6 + h], self.cstR], writes=[self.rsR])
        P.op("dve", lambda e: e.reciprocal(out=rs[:], in_=rs[:]), reads=[self.rsR], writes=[self.rsR])
        for kc in range(KC):
            P.op("dve", lambda e, kc=kc: e.scalar_tensor_tensor(out=xT[:, kc, :], in0=xT[:, kc, :],
                                                                scalar=self.cst[:, gi * 16 + kc:gi * 16 + kc + 1], in1=rs[:],
                                                                op0=ALU.mult, op1=ALU.mult),
                 reads=self.xR[kc] + [self.rsR, self.cstR], writes=self.xR[kc])
        self.release(m)

    def eps_ap(self):
        return self.cst[:, EPSC:EPSC + 1]

    def ffn(self, w_in, w_out, tag):
        P, A = self.P, self.A
        m = A.mark()
        w_in_v = w_in.rearrange("(kc p) f -> p kc f", p=128)
        w_out_v = w_out.rearrange("(j p) d -> p j d", p=128)
        NST = 6
        stg = [A.alloc("stg", [128, 2048], F32) for _ in range(NST)]
        stgR = [Res(f"stg{i}") for i in range(NST)]
        wg = [A.alloc("wg", [128, KC, 128], BF16) for _ in range(2)]
        wu = [A.alloc("wu", [128, KC, 128], BF16) for _ in range(2)]
        wgR = [Res("wg0"), Res("wg1")]
        wuR = [Res("wu0"), Res("wu1")]
        wo = [A.alloc("wo", [128, D], BF16) for _ in range(4)]
        woR = [Res(f"wo{i}") for i in range(4)]
        g = [A.alloc("g", [128, T], BF16) for _ in range(4)]
        gR = [[Res(f"g{i}_{h}") for h in range(2)] for i in range(4)]
        sg = [A.alloc("sg", [128, 512], F32) for _ in range(2)]
        sgR = [Res("sg0"), Res("sg1")]
        xT, hT, ps, psR = self.xT, self.hT, self.ps, self.psR

        def dma_issue(j):
            s = 3 * (j % 2)
            P.dma("sp", stg[s][:].rearrange("p (k f) -> p k f", k=KC), w_in_v[:, :, j * 128:(j + 1) * 128], f"st{s}",
                  writes=[stgR[s]])
            P.dma("sp", stg[s + 1][:].rearrange("p (k f) -> p k f", k=KC), w_in_v[:, :, DFF + j * 128:DFF + (j + 1) * 128],
                  f"st{s+1}", writes=[stgR[s + 1]])
            P.dma("sp", stg[s + 2][:], w_out_v[:, j, :], f"st{s+2}", writes=[stgR[s + 2]])

        def cast_in(j):
            s = 3 * (j % 2)
            b = j % 2
            P.op("act", lambda e: e.activation(out=wg[b][:].rearrange("p k f -> p (k f)"), in_=stg[s][:], func=AF.Copy),
                 reads=[stgR[s]], writes=[wgR[b]])
            P.op("pool", lambda e: e.tensor_copy(out=wu[b][:].rearrange("p k f -> p (k f)"), in_=stg[s + 1][:]),
                 reads=[stgR[s + 1]], writes=[wuR[b]])

        def cast_wo(j):
            s = 3 * (j % 2)
            P.op("pool", lambda e: e.tensor_copy(out=wo[j % 4][:], in_=stg[s + 2][:]),
                 reads=[stgR[s + 2]], writes=[woR[j % 4]])

        def win(j, after_half=None):
            b = j % 2
            gs = j % 4
            for h in range(2):
                for (wt, wR, pi) in ((wg, wgR, h), (wu, wuR, 2 + h)):
                    for kc in range(KC):
                        P.op("pe", lambda e, wt=wt, pi=pi, kc=kc, h=h: e.matmul(
                            ps[pi][:], lhsT=wt[b][:, kc, :], rhs=hT[:, kc, h * 512:(h + 1) * 512],
                            start=(kc == 0), stop=(kc == KC - 1)),
                            reads=[wR[b], self.hR[kc]], writes=[psR[pi]])
                P.op("act", lambda e, h=h: e.activation(out=sg[h][:], in_=ps[h][:], func=AF.Silu),
                     reads=[psR[h]], writes=[sgR[h]])
                P.op("dve", lambda e, h=h: e.tensor_tensor(out=g[gs][:, h * 512:(h + 1) * 512], in0=sg[h][:], in1=ps[2 + h][:],
                                                           op=ALU.mult),
                     reads=[sgR[h], psR[2 + h]], writes=[gR[gs][h]])
                if after_half is not None:
                    after_half(h)

        ycnt = [0]

        def wout(grp, dr=range(KC)):
            for d in dr:
                for h in range(2):
                    pi = 4 + (ycnt[0] % 4)
                    ycnt[0] += 1
                    for n, j in enumerate(grp):
                        P.op("pe", lambda e, pi=pi, j=j, d=d, h=h, n=n: e.matmul(
                            ps[pi][:], lhsT=wo[j % 4][:, d * 128:(d + 1) * 128], rhs=g[j % 4][:, h * 512:(h + 1) * 512],
                            start=(n == 0), stop=(n == len(grp) - 1)),
                            reads=[woR[j % 4], gR[j % 4][h]], writes=[psR[pi]])
                    P.op("dve", lambda e, pi=pi, d=d, h=h: e.scalar_tensor_tensor(
                        out=xT[:, d, h * 512:(h + 1) * 512], in0=ps[pi][:], scalar=0.5,
                        in1=xT[:, d, h * 512:(h + 1) * 512], op0=ALU.mult, op1=ALU.add),
                        reads=[psR[pi], self.xR[d][h]], writes=[self.xR[d][h]])

        dma_issue(0)
        dma_issue(1)
        cast_in(0)
        cast_wo(0)
        for j in range(NFF):
            if j + 2 < NFF:
                dma_issue(j + 2)
            if j + 1 < NFF:
                cast_in(j + 1)
            g0 = j - 2 if j % 2 == 0 else j - 3
            if g0 >= 0:
                win(j, lambda h, j=j, g0=g0: wout((g0, g0 + 1), range(4 * (2 * (j % 2) + h), 4 * (2 * (j % 2) + h) + 4)))
            else:
                win(j)
            if j + 1 < NFF:
                cast_wo(j + 1)
        wout((NFF - 2, NFF - 1))
        self.release(m)

    def fm_proj(self, w, col0, ncols, evac, tag, src=None, srcR=None):
        P, A = self.P, self.A
        src = self.hT if src is None else src
        srcR = self.hR if srcR is None else srcR
        m = A.mark()
        wv = w.rearrange("(kc p) f -> p kc f", p=128)
        nu = ncols // 128
        stg = [A.alloc("fstg", [128, KC, 128], F32) for _ in range(3)]
        stgR = [Res(f"fstg{i}") for i in range(3)]
        wb = [A.alloc("fwb", [128, KC, 128], BF16) for _ in range(2)]
        wbR = [Res("fwb0"), Res("fwb1")]
        ps, psR = self.ps, self.psR

        def issue(u):
            P.dma("sp", stg[u % 3][:], wv[:, :, col0 + u * 128:col0 + (u + 1) * 128], f"fst{u % 3}", writes=[stgR[u % 3]])

        def cast(u):
            eng = "act" if u % 2 == 0 else "pool"
            if eng == "act":
                P.op("act", lambda e: e.activation(out=wb[u % 2][:], in_=stg[u % 3][:], func=AF.Copy),
                     reads=[stgR[u % 3]], writes=[wbR[u % 2]])
            else:
                P.op("pool", lambda e: e.tensor_copy(out=wb[u % 2][:], in_=stg[u % 3][:]),
                     reads=[stgR[u % 3]], writes=[wbR[u % 2]])

        issue(0)
        if nu > 1:
            issue(1)
        cast(0)
        for u in range(nu):
            if u + 2 < nu:
                issue(u + 2)
            if u + 1 < nu:
                cast(u + 1)
            for h in range(2):
                pi = (2 * u + h) % 4
                for kc in range(KC):
                    P.op("pe", lambda e, pi=pi, kc=kc, h=h, u=u: e.matmul(
                        ps[pi][:], lhsT=wb[u % 2][:, kc, :], rhs=src[:, kc, h * 512:(h + 1) * 512],
                        start=(kc == 0), stop=(kc == KC - 1)), reads=[wbR[u % 2], srcR[kc]], writes=[psR[pi]])
                evac(u, h, ps[pi], psR[pi])
        self.release(m)

    def tm_proj(self, w, col0, ncols, evac, tag, src=None, srcR=None):
        P, A = self.P, self.A
        src = self.hT if src is None else src
        srcR = self.hR if srcR is None else srcR
        m = A.mark()
        wv = w.rearrange("(kc p) f -> p kc f", p=128)
        nb = ncols // 512
        stg = [A.alloc("tstg", [128, 4, 512], F32) for _ in range(4)]
        stgR = [Res(f"tstg{i}") for i in range(4)]
        wb = [A.alloc("twb", [128, KC, 512], BF16) for _ in range(2)]
        wbR = [[Res(f"twb{i}_{q}") for q in range(4)] for i in range(2)]
        ps, psR = self.ps, self.psR
        n = [0]
        for cb in range(nb):
            for q in range(4):
                s = n[0] % 4
                n[0] += 1
                P.dma("sp", stg[s][:], wv[:, q * 4:(q + 1) * 4, col0 + cb * 512:col0 + (cb + 1) * 512], f"tst{s}",
                      writes=[stgR[s]])
                if q % 2 == 0:
                    P.op("act", lambda e, s=s, q=q, cb=cb: e.activation(out=wb[cb % 2][:, q * 4:(q + 1) * 4, :], in_=stg[s][:], func=AF.Copy),
                         reads=[stgR[s]], writes=[wbR[cb % 2][q]])
                else:
                    P.op("pool", lambda e, s=s, q=q, cb=cb: e.tensor_copy(out=wb[cb % 2][:, q * 4:(q + 1) * 4, :], in_=stg[s][:]),
                         reads=[stgR[s]], writes=[wbR[cb % 2][q]])
            for tt in range(T // 128):
                pi = tt % 4
                for kc in range(KC):
                    P.op("pe", lambda e, pi=pi, kc=kc, tt=tt, cb=cb: e.matmul(
                        ps[pi][:], lhsT=src[:, kc, tt * 128:(tt + 1) * 128], rhs=wb[cb % 2][:, kc, :],
                        start=(kc == 0), stop=(kc == KC - 1)), reads=[wbR[cb % 2][kc // 4], srcR[kc]], writes=[psR[pi]])
                evac(cb, tt, ps[pi], psR[pi])
        self.release(m)

    def mlstm_proj(self, w_in, l, xq, xk, xv, xg, ogs, xR, ogsR, hook=None):
        P, A = self.P, self.A
        m = A.mark()
        ob = [A.alloc("ob", [128, T], BF16) for _ in range(3)]
        obR = [Res(f"ob{i}") for i in range(3)]
        o32 = [A.alloc("o32", [128, T], F32) for _ in range(2)]
        o32R = [Res("o32a"), Res("o32b")]
        tb = [A.alloc("tb", [128, 512], BF16) for _ in range(3)]
        tbR = [Res(f"tb{i}") for i in range(3)]

        def ev_qk(u, h, ps, psR):
            b = u % 3
            P.op("act", lambda e: e.activation(out=ob[b][:, h * 512:(h + 1) * 512], in_=ps[:], func=AF.Copy),
                 reads=[psR], writes=[obR[b]])
            if h == 1:
                head, j = (u, 0) if u < 8 else (u - 8, 1)
                P.dma("sp", xq[head // 4, head % 4, j], ob[b][:], f"ob{b}", reads=[obR[b], xR["q"]])

        self.fm_proj(w_in, 0, 2048, ev_qk, "qk")
        if hook:
            hook("q")

        def ev_og(u, h, ps, psR):
            b = u % 2
            P.op("act", lambda e: e.activation(out=o32[b][:, h * 512:(h + 1) * 512], in_=ps[:], func=AF.Sigmoid),
                 reads=[psR], writes=[o32R[b]])
            if h == 1:
                P.dma("sp", ogs[u * 128:(u + 1) * 128, :], o32[b][:], f"o32{b}", reads=[o32R[b]], writes=[ogsR[u]])

        self.fm_proj(w_in, 4096, 2048, ev_og, "og")

        cnt = [0]

        def ev_k(cb, tt, ps, psR):
            b = cnt[0] % 3
            cnt[0] += 1
            P.op("act", lambda e: e.activation(out=tb[b][:], in_=ps[:], func=AF.Copy), reads=[psR], writes=[tbR[b]])
            P.dma("sp", xk[cb, tt * 128:(tt + 1) * 128, :], tb[b][:], f"tb{b}", reads=[tbR[b], xR["k"]])

        self.tm_proj(w_in, 1024, 1024, ev_k, "k")
        if hook:
            hook("k")

        def ev_v(cb, tt, ps, psR):
            b = cnt[0] % 3
            cnt[0] += 1
            P.op("act", lambda e: e.activation(out=tb[b][:], in_=ps[:], func=AF.Copy), reads=[psR], writes=[tbR[b]])
            P.dma("sp", xv[cb // 2, tt * 128:(tt + 1) * 128, (cb % 2) * 512:(cb % 2 + 1) * 512], tb[b][:], f"tb{b}",
                  reads=[tbR[b], xR["v"]])

        self.tm_proj(w_in, 2048, 2048, ev_v, "v")
        if hook:
            hook("v")

        gs = A.alloc("gs", [128, KC, 16], F32)
        gsR = Res("gs")
        gw = A.alloc("gw", [128, KC, 16], BF16)
        gwR = Res("gw")
        gio = A.alloc("gio", [8, 2, T], F32)
        gioR = Res("gio")
        gt = A.alloc("gt", [8, T], F32)
        gtR = Res("gt")
        bb = A.alloc("bb", [8, 2], F32)
        bbR = Res("bb")
        wv = w_in.rearrange("(kc p) f -> p kc f", p=128)
        P.dma("sp", gs[:], wv[:, :, 6144:6160], "gs", writes=[gsR])
        P.op("dve", lambda e: e.tensor_copy(out=gw[:], in_=gs[:]), reads=[gsR], writes=[gwR])
        cb = 256 + 2 * l
        P.op("dve", lambda e: e.tensor_scalar(out=bb[:, 0:1], in0=self.cst[0:8, cb:cb + 1], scalar1=1.0 / 15.0, scalar2=None,
                                              op0=ALU.mult), reads=[self.cstR], writes=[bbR])
        P.op("dve", lambda e: e.tensor_scalar(out=bb[:, 1:2], in0=self.cst[0:8, cb + 1:cb + 2], scalar1=-1.0, scalar2=None,
                                              op0=ALU.mult), reads=[self.cstR, bbR], writes=[bbR])
        ps, psR = self.ps, self.psR
        for h in range(2):
            for gi in range(2):
                pi = 2 * h + gi
                for kc in range(KC):
                    P.op("pe", lambda e, pi=pi, kc=kc, h=h, gi=gi: e.matmul(
                        ps[pi][0:8, :], lhsT=gw[:, kc, gi * 8:(gi + 1) * 8], rhs=self.hT[:, kc, h * 512:(h + 1) * 512],
                        start=(kc == 0), stop=(kc == KC - 1)), reads=[gwR, self.hR[kc]], writes=[psR[pi]])
            P.op("act", lambda e, h=h: e.activation(out=gt[:, h * 512:(h + 1) * 512], in_=ps[2 * h][0:8, :], func=AF.Tanh,
                                                    scale=1.0 / 15.0, bias=bb[:, 0:1]), reads=[psR[2 * h], bbR], writes=[gtR])
            P.op("dve", lambda e, h=h: e.tensor_scalar(out=gio[:, 0, h * 512:(h + 1) * 512], in0=gt[:, h * 512:(h + 1) * 512],
                                                       scalar1=15.0, scalar2=None, op0=ALU.mult), reads=[gtR], writes=[gioR])
            P.op("act", lambda e, h=h: e.activation(out=gt[:, h * 512:(h + 1) * 512], in_=ps[2 * h + 1][0:8, :], func=AF.Exp,
                                                    scale=-1.0, bias=bb[:, 1:2]), reads=[psR[2 * h + 1], bbR, gioR], writes=[gtR])
            P.op("act", lambda e, h=h: e.activation(out=gt[:, h * 512:(h + 1) * 512], in_=gt[:, h * 512:(h + 1) * 512], func=AF.Ln,
                                                    scale=1.0, bias=self.one_ap()[0:8, :]), reads=[gtR, self.cstR], writes=[gtR])
            P.op("dve", lambda e, h=h: e.tensor_scalar(out=gio[:, 1, h * 512:(h + 1) * 512], in0=gt[:, h * 512:(h + 1) * 512],
                                                       scalar1=-1.0, scalar2=None, op0=ALU.mult), reads=[gtR], writes=[gioR])
        for d in range(2):
            for gi in range(2):
                P.dma("sp", xg[d, gi], gio[4 * d:4 * d + 4, gi, :], "gio", reads=[gioR, xR["g"]])
        self.release(m)

    def one_ap(self):
        return self.ones32[:, 0:1]

    def load_mix_consts(self, mc16, mc32, idx_d, sel_d, hng_d):
        P = self.P
        self.mc32 = mc32
        P.dma("sp", self.c16[:], mc16, "c16", writes=[self.c16R])
        P.dma("sp", self.c32[:], mc32[:, 0:512], "c32", writes=[self.c32R])
        P.dma("sp", self.idx[:], idx_d, "idx", writes=[self.idxR])
        P.dma("sp", self.sel[:], sel_d, "sel", writes=[self.selR])
        P.dma("sp", self.hgt[:], hng_d, "hgt", writes=[self.hgR])
        P.op("pool", lambda e: e.memset(self.on16[:], 1.0), writes=[self.c16R], reads=[])
        self.ident = self.c16[:, 0:128]
        self.negmask = self.c16[:, 128:640]
        self.maskd = self.c16[:, 640:768]
        self.utri = self.c16[:, 768:896]
        self.ltri = self.c16[:, 896:1024]
        self.ind4 = self.c32[:, 0:512]

    def mlstm_mix(self, gq, gk, gv, gg, l, xh, gR_in, xhR, half_hook=None):
        P, A = self.P, self.A
        A.reset(self.low_mark)
        NT, NCH, L = 2048, 16, 128
        SC = 128 ** -0.5
        ps, psR = self.ps, self.psR
        qk = A.alloc("qk", [128, 4, 2, T], BF16)
        qkR = [Res(f"qk{hl}") for hl in range(4)]
        ktm = A.alloc("ktm", [128, 8, 512], BF16)
        vtm = A.alloc("vtm", [128, 8, 1024], BF16)
        kvR = Res("kv")
        hg = self.hgt[:, l * 8:(l + 1) * 8]
        hgR = self.hgR

        def load_half(hf):
            for hl in range(4):
                for j in range(2):
                    self.idma(qk[:, hl, j, :], gq, 16 + hf * 8 + hl * 2 + j, f"qk{hl}", reads=[gR_in], writes=[qkR[hl]])
            for tt in range(8):
                self.idma(ktm[:, tt, :], gk, hf * 8 + tt, "kvk", reads=[gR_in], writes=[kvR])
                self.idma(vtm[:, tt, :], gv, 16 + hf * 8 + tt, "kvv", reads=[gR_in], writes=[kvR])

        load_half(0)
        IG = A.alloc("IG", [4, NT], F32)
        LF = A.alloc("LF", [4, NT], F32)
        NM = A.alloc("NM", [4, NT], F32)
        R2 = A.alloc("R2", [4, NT], F32)
        MS = A.alloc("MS", [4, 32], F32)
        ON4 = A.alloc("ON4", [4, 128], F32)
        tA = A.alloc("tA", [4, T], F32)
        tB = A.alloc("tB", [4, T], F32)
        gR = Res("gates")
        G = [gR, self.c32R]
        P.dma("sp", NM[:], self.mc32[:, 512:2560], "mg", writes=[gR])
        P.dma("sp", R2[:], self.mc32[:, 2560:4608], "mg", writes=[gR])
        for hf in range(2):
            for gi, dst in enumerate((IG, LF)):
                P.dma("sp", tA[:], gg[hf * 16 + gi * 4:hf * 16 + gi * 4 + 4, :], "mg", reads=[gR_in], writes=[gR])
                P.dma("sp", tB[:], gg[hf * 16 + 8 + gi * 4:hf * 16 + 8 + gi * 4 + 4, :], "mg", reads=[gR_in], writes=[gR])
                P.op("dve", lambda e: e.tensor_scalar(out=tB[:], in0=tB[:], scalar1=self.sel[0:4, 1:2], scalar2=None, op0=ALU.mult),
                     reads=G + [self.selR], writes=[gR])
                P.op("dve", lambda e, dst=dst, hf=hf: e.scalar_tensor_tensor(out=dst[:, hf * T:(hf + 1) * T], in0=tA[:], scalar=self.sel[0:4, 0:1],
                                                                             in1=tB[:], op0=ALU.mult, op1=ALU.add),
                     reads=G + [self.selR], writes=[gR])
        c3 = lambda t: t[:].rearrange("p (c t) -> p c t", t=L)
        P.op("dve", lambda e: e.memset(ON4[:], 1.0), reads=G, writes=[gR])
        P.op("dve", lambda e: e.tensor_tensor_scan(out=LF[:], data0=NM[:], data1=LF[:], initial=0.0, op0=ALU.mult, op1=ALU.add),
             reads=G, writes=[gR])
        P.op("dve", lambda e: e.tensor_tensor(out=IG[:], in0=IG[:], in1=LF[:], op=ALU.subtract), reads=G, writes=[gR])
        P.op("dve", lambda e: e.tensor_tensor_scan(out=NM[:], data0=R2[:], data1=IG[:], initial=0.0, op0=ALU.add, op1=ALU.max),
             reads=G, writes=[gR])
        P.op("dve", lambda e: e.memset(MS[:], 0.0), reads=G, writes=[gR])
        P.op("dve", lambda e: e.tensor_tensor_scan(out=MS[:, 1:17], data0=c3(NM)[:, :, L - 1], data1=c3(LF)[:, :, L - 1], initial=0.0,
                                                   op0=ALU.max, op1=ALU.add), reads=G, writes=[gR])
        P.op("dve", lambda e: e.tensor_tensor(out=c3(NM), in0=c3(NM), in1=MS[:, 0:16].unsqueeze(2).to_broadcast([4, NCH, L]), op=ALU.max),
             reads=G, writes=[gR])
        P.op("dve", lambda e: e.tensor_scalar(out=NM[:], in0=NM[:], scalar1=-1.0, scalar2=None, op0=ALU.mult), reads=G, writes=[gR])
        nmm = [A.alloc("nmm", [4, 4, L], F32) for _ in range(2)]
        nmd = [A.alloc("nmd", [4, 4, L], F32) for _ in range(2)]
        nmt = [A.alloc("nmt", [4, 4, L], F32) for _ in range(2)]
        tmc = [A.alloc("tmc", [4, L], F32) for _ in range(2)]
        nmR = [Res("nm0"), Res("nm1")]
        ED = [A.alloc("ED", [128, 512], F32) for _ in range(2)]
        DEC = [A.alloc("DEC", [128, 512], F32) for _ in range(2)]
        EMT = [A.alloc("EMT", [128, 512], F32) for _ in range(2)]
        eR = [[Res(f"e{k}_{b}") for b in range(2)] for k in range(3)]
        swt = [A.alloc("swt", [128, L], BF16) for _ in range(2)]
        swR = [Res("sw0"), Res("sw1")]
        qd = [A.alloc("qd", [128, L], BF16) for _ in range(2)]
        qdR = [Res("qd0"), Res("qd1")]
        wv = [A.alloc("wv", [128, 384], BF16) for _ in range(2)]
        wvR = [Res("wv0"), Res("wv1")]
        rr = [A.alloc("rr", [128, L], F32) for _ in range(2)]
        rrR = [Res("rr0"), Res("rr1")]
        hs = [A.alloc("hs", [128, 2, L], F32) for _ in range(2)]
        hsR = [Res("hs0"), Res("hs1")]
        sq = [A.alloc("hsq", [128, 2, L], F32) for _ in range(2)]
        sqR = [Res("hsq0"), Res("hsq1")]
        ho = [A.alloc("ho", [128, 2, L], F32) for _ in range(3)]
        hoR = [Res(f"ho{i}") for i in range(3)]
        C32 = [A.alloc("C32", [128, 384], F32) for _ in range(4)]
        C16 = [A.alloc("C16", [128, 384], BF16) for _ in range(4)]
        cR = [Res(f"C32_{h}") for h in range(4)]
        c16R = [Res(f"C16_{h}") for h in range(4)]
        ind3 = self.ind4.rearrange("p (h t) -> p h t", t=L)
        f2 = lambda t: t[:].rearrange("p h t -> p (h t)")
        n = 0
        for c in range(NCH):
            cb = c % 2
            sl = slice(c * L, (c + 1) * L)
            hf = c // 8
            cl = c % 8
            lsl = slice(cl * L, (cl + 1) * L)
            if c == 8:
                load_half(1)
            def prep(c):
                cb = c % 2
                sl = slice(c * L, (c + 1) * L)
                bc = lambda t: t[:, sl].unsqueeze(1).to_broadcast([4, 4, L])
                P.op("dve", lambda e: e.tensor_tensor(out=nmm[cb][:], in0=ind3, in1=bc(NM), op=ALU.mult), reads=G + [nmR[cb]], writes=[nmR[cb]])
                P.op("dve", lambda e: e.scalar_tensor_tensor(out=nmd[cb][:], in0=bc(NM), scalar=MS[:, c:c + 1], in1=ind3, op0=ALU.add, op1=ALU.mult),
                     reads=G + [nmR[cb]], writes=[nmR[cb]])
                P.op("dve", lambda e: e.tensor_tensor(out=tmc[cb][:], in0=NM[:, sl], in1=LF[:, sl], op=ALU.subtract), reads=G + [nmR[cb]], writes=[nmR[cb]])
                P.op("dve", lambda e: e.tensor_tensor(out=nmt[cb][:], in0=ind3, in1=tmc[cb][:].unsqueeze(1).to_broadcast([4, 4, L]), op=ALU.mult),
                     reads=G + [nmR[cb]], writes=[nmR[cb]])
                P.op("pe", lambda e: e.matmul(ps[0][:], lhsT=self.ident, rhs=self.negmask, start=True, stop=False), reads=[self.c16R], writes=[psR[0]])
                P.op("pe", lambda e: e.matmul(ps[0][:], lhsT=IG[:, sl], rhs=self.ind4, start=False, stop=False), reads=G, writes=[psR[0]])
                P.op("pe", lambda e: e.matmul(ps[0][:], lhsT=ON4[:], rhs=f2(nmm[cb]), start=False, stop=True), reads=G + [nmR[cb]], writes=[psR[0]])
                P.op("pe", lambda e: e.matmul(ps[1][:], lhsT=ON4[:], rhs=f2(nmd[cb]), start=True, stop=True), reads=G + [nmR[cb]], writes=[psR[1]])
                P.op("pe", lambda e: e.matmul(ps[2][:], lhsT=ON4[:], rhs=f2(nmt[cb]), start=True, stop=True), reads=G + [nmR[cb]], writes=[psR[2]])
                for k, (dst, pi) in enumerate(((ED, 0), (DEC, 1), (EMT, 2))):
                    P.op("act", lambda e, dst=dst, pi=pi: e.activation(out=dst[cb][:], in_=ps[pi][:], func=AF.Exp),
                         reads=[psR[pi]], writes=[eR[k][cb]])

            if c == 0:
                prep(0)
            def st1(hl, c=c, cb=cb, lsl=lsl, cl=cl, hf=hf):
                b = hl % 2
                hsl = slice(hl * L, (hl + 1) * L)
                P.op("pe", lambda e: e.matmul(ps[3][:, b * L:(b + 1) * L], lhsT=qk[:, hl, 1, lsl], rhs=qk[:, hl, 0, lsl], start=True, stop=True),
                     reads=[qkR[hl]], writes=[psR[3]])

            def st2(hl, c=c, cb=cb, lsl=lsl, cl=cl, hf=hf):
                b = hl % 2
                hsl = slice(hl * L, (hl + 1) * L)
                P.op("dve", lambda e: e.scalar_tensor_tensor(out=swt[b][:], in0=ps[3][:, b * L:(b + 1) * L], scalar=SC, in1=ED[cb][:, hsl],
                                                             op0=ALU.mult, op1=ALU.mult),
                     reads=[psR[3], eR[0][cb]], writes=[swR[b]])
                if c > 0:
                    P.op("pool", lambda e: e.tensor_tensor(out=qd[b][:], in0=qk[:, hl, 0, lsl], in1=DEC[cb][:, hsl], op=ALU.mult),
                         reads=[qkR[hl], eR[1][cb]], writes=[qdR[b]])

            def st3(hl, c=c, cb=cb, lsl=lsl, cl=cl, hf=hf):
                b = hl % 2
                pn = 4 + b
                for vc in range(3):
                    lh = vtm[:, cl, hl * 256 + vc * 128: hl * 256 + (vc + 1) * 128] if vc < 2 else self.on16[:]
                    P.op("pe", lambda e, vc=vc, lh=lh: e.matmul(ps[pn][:, vc * L:(vc + 1) * L], lhsT=lh, rhs=swt[b][:], start=True, stop=(c == 0)),
                         reads=[kvR, swR[b], self.c16R], writes=[psR[pn]])
                    if c > 0:
                        P.op("pe", lambda e, vc=vc: e.matmul(ps[pn][:, vc * L:(vc + 1) * L], lhsT=C16[hl][:, vc * 128:(vc + 1) * 128], rhs=qd[b][:],
                                                             start=False, stop=True),
                             reads=[c16R[hl], qdR[b]], writes=[psR[pn]])

            def st4(hl, c=c, cb=cb, lsl=lsl, cl=cl, hf=hf):
                b = hl % 2
                pn = 4 + b
                hsl = slice(hl * L, (hl + 1) * L)
                P.op("act", lambda e: e.activation(out=rr[b][:], in_=ps[pn][:, 2 * L:3 * L], func=AF.Abs), reads=[psR[pn]], writes=[rrR[b]])
                P.op("dve", lambda e: e.tensor_tensor(out=rr[b][:], in0=rr[b][:], in1=EMT[cb][:, hsl], op=ALU.max),
                     reads=[eR[2][cb], rrR[b]], writes=[rrR[b]])
                P.op("act", lambda e: e.activation(out=rr[b][:], in_=rr[b][:], func=AF.Ln), reads=[rrR[b]], writes=[rrR[b]])
                P.op("act", lambda e: e.activation(out=rr[b][:], in_=rr[b][:], func=AF.Exp, scale=-1.0), reads=[rrR[b]], writes=[rrR[b]])
                P.op("dve", lambda e: e.tensor_tensor(out=hs[b][:], in0=ps[pn][:, 0:2 * L].rearrange("p (v t) -> p v t", t=L),
                                                      in1=rr[b][:].unsqueeze(1).to_broadcast([128, 2, L]), op=ALU.mult),
                     reads=[psR[pn], rrR[b]], writes=[hsR[b]])
                P.op("act", lambda e: e.activation(out=sq[b][:], in_=hs[b][:], func=AF.Square), reads=[hsR[b]], writes=[sqR[b]])

            def st5(hl, c=c, cb=cb, lsl=lsl, cl=cl, hf=hf):
                b = hl % 2
                for vc in range(2):
                    P.op("pe", lambda e, vc=vc: e.matmul(ps[3][:, (2 + b) * L:(3 + b) * L], lhsT=self.ones32[:], rhs=sq[b][:, vc, :], start=(vc == 0), stop=(vc == 1)),
                         reads=[sqR[b], self.onesR], writes=[psR[3]])

            def st6(hl, c=c, cb=cb, lsl=lsl, cl=cl, hf=hf):
                b = hl % 2
                P.op("act", lambda e: e.activation(out=rr[b][:], in_=ps[3][:, (2 + b) * L:(3 + b) * L], func=AF.Ln, scale=1.0 / 256.0, bias=self.eps_ap()),
                     reads=[psR[3], self.cstR, rrR[b]], writes=[rrR[b]])
                P.op("act", lambda e: e.activation(out=rr[b][:], in_=rr[b][:], func=AF.Exp, scale=-0.5), reads=[rrR[b]], writes=[rrR[b]])
                o = (c * 4 + hl) % 3
                for vc in range(2):
                    P.op("dve", lambda e, vc=vc: e.scalar_tensor_tensor(out=ho[o][:, vc, :], in0=hs[b][:, vc, :], scalar=hg[:, hl * 2 + vc:hl * 2 + vc + 1],
                                                                        in1=rr[b][:], op0=ALU.mult, op1=ALU.mult),
                         reads=[hsR[b], rrR[b], hgR], writes=[hoR[o]])
                P.dma("sp", xh[hf, hl * 256:(hl + 1) * 256, cl * L:(cl + 1) * L].rearrange("(v p) t -> p v t", p=128), ho[o][:], f"ho{o}",
                      reads=[hoR[o], xhR[hf]])

            def st7(hl, c=c, cb=cb, lsl=lsl, cl=cl, hf=hf):
                b = hl % 2
                pc = 6 if b == 0 else 7
                if c >= NCH - 1:
                    return
                wcol = ED[cb][:, hl * L + L - 1: hl * L + L]
                P.op("act", lambda e: e.activation(out=wv[b][:, 0:256], in_=vtm[:, cl, hl * 256:(hl + 1) * 256], func=AF.Copy, scale=wcol),
                     reads=[kvR, eR[0][cb]], writes=[wvR[b]])
                P.op("act", lambda e: e.activation(out=wv[b][:, 256:384], in_=self.on16[:], func=AF.Copy, scale=wcol),
                     reads=[self.c16R, eR[0][cb], wvR[b]], writes=[wvR[b]])
                P.op("pe", lambda e: e.matmul(ps[pc][:, 0:384], lhsT=ktm[:, cl, hl * 128:(hl + 1) * 128], rhs=wv[b][:], start=True, stop=True),
                     reads=[kvR, wvR[b]], writes=[psR[pc]])

            def st8(hl, c=c, cb=cb, lsl=lsl, cl=cl, hf=hf):
                b = hl % 2
                pc = 6 if b == 0 else 7
                if c >= NCH - 1:
                    return
                dcol = DEC[cb][:, hl * L + L - 1: hl * L + L]
                if c == 0:
                    P.op("dve", lambda e: e.tensor_copy(out=C32[hl][:], in_=ps[pc][:, 0:384]), reads=[psR[pc]], writes=[cR[hl]])
                else:
                    P.op("dve", lambda e: e.scalar_tensor_tensor(out=C32[hl][:], in0=C32[hl][:], scalar=dcol, in1=ps[pc][:, 0:384],
                                                                 op0=ALU.mult, op1=ALU.add),
                         reads=[psR[pc], cR[hl], eR[1][cb]], writes=[cR[hl]])
                P.op("pool", lambda e: e.tensor_scalar(out=C16[hl][:], in0=C32[hl][:], scalar1=SC, scalar2=0.0, op0=ALU.mult, op1=ALU.add),
                     reads=[cR[hl]], writes=[c16R[hl]])

            for hp in range(2):
                for st in (st1, st2, st3, st7, st4, st5, st8, st6):
                    for hl in (2 * hp, 2 * hp + 1):
                        st(hl)
                if hp == 0 and c + 1 < NCH:
                    prep(c + 1)
            if c == 7 and half_hook:
                half_hook(0)
        if half_hook:
            half_hook(1)
        self.release(self.low_mark)
        A.reset(self.phase_mark)

    def post_mix(self, gh, ghR, ogs, ogsR, w_out):
        P, A = self.P, self.A
        m = A.mark()
        hc = [A.alloc("hc", [128, T], F32) for _ in range(2)]
        oc = [A.alloc("oc", [128, T], F32) for _ in range(2)]
        hcR = [Res("hc0"), Res("hc1")]
        ocR = [Res("oc0"), Res("oc1")]
        for kc in range(KC):
            b = kc % 2
            self.idma(hc[b][:], gh, 32 + kc, f"hc{b}", reads=[ghR], writes=[hcR[b]])
            if ogs is not None:
                P.dma("sp", oc[b][:], ogs[kc * 128:(kc + 1) * 128, :], f"oc{b}", reads=[ogsR[kc]], writes=[ocR[b]])
                P.op("dve", lambda e, kc=kc, b=b: e.tensor_tensor(out=self.hT[:, kc, :], in0=hc[b][:], in1=oc[b][:], op=ALU.mult),
                     reads=[hcR[b], ocR[b]], writes=[self.hR[kc]])
            else:
                P.op("dve", lambda e, kc=kc, b=b: e.tensor_copy(out=self.hT[:, kc, :], in_=hc[b][:]),
                     reads=[hcR[b]], writes=[self.hR[kc]])
        self.release(m)

        def ev(u, h, ps, psR):
            P.op("dve", lambda e: e.tensor_tensor(out=self.xT[:, u, h * 512:(h + 1) * 512], in0=ps[:], in1=self.xT[:, u, h * 512:(h + 1) * 512],
                                                  op=ALU.add), reads=[psR, self.xR[u][h]], writes=[self.xR[u][h]])

        self.fm_proj(w_out, 0, D, ev, "wo")

    def load_x(self, xT_d):
        xv = xT_d.rearrange("(kc p) t -> p kc t", p=128)
        for kc in range(KC):
            self.P.dma("sp", self.xT[:, kc, :], xv[:, kc, :], f"xin{kc}", writes=self.xR[kc])

    def store_x(self, xT_d):
        xv = xT_d.rearrange("(kc p) t -> p kc t", p=128)
        for kc in range(KC):
            self.P.dma("sp", xv[:, kc, :], self.xT[:, kc, :], "xout", reads=self.xR[kc])

    def load_cst(self, cst_d):
        self.P.dma("sp", self.cst[:], cst_d, "cst", writes=[self.cstR])

    def sb_fmproj(self, w, col0, nheads, dst, xR, head0=0):
        P, A = self.P, self.A
        m = A.mark()
        ob = [A.alloc("sob", [128, T], BF16) for _ in range(3)]
        obR = [Res(f"sob{i}") for i in range(3)]

        def ev(u, h, ps, psR):
            b = u % 3
            P.op("act", lambda e: e.activation(out=ob[b][:, h * 512:(h + 1) * 512], in_=ps[:], func=AF.Copy),
                 reads=[psR], writes=[obR[b]])
            if h == 1:
                head = head0 + u
                P.dma("sp", dst[head // 8, head % 8], ob[b][:], f"sob{b}", reads=[obR[b], xR])

        self.fm_proj(w, col0, nheads * 128, ev, "sbfm")
        self.release(m)

    def sb_vproj(self, w, col0, dst, xR):
        P, A = self.P, self.A
        m = A.mark()
        tb = [A.alloc("stb", [128, 512], BF16) for _ in range(3)]
        tbR = [Res(f"stb{i}") for i in range(3)]
        cnt = [0]

        def ev(cb, tt, ps, psR):
            b = cnt[0] % 3
            cnt[0] += 1
            P.op("act", lambda e: e.activation(out=tb[b][:], in_=ps[:], func=AF.Copy), reads=[psR], writes=[tbR[b]])
            P.dma("sp", dst[cb // 2, tt * 128:(tt + 1) * 128, (cb % 2) * 512:(cb % 2 + 1) * 512], tb[b][:], f"stb{b}",
                  reads=[tbR[b], xR])

        self.tm_proj(w, col0, 2048, ev, "sbv")
        self.release(m)

    def sb_mix(self, gqq, gkk, gvv, xo, gR_in, xoR, half_hook=None):
        P, A = self.P, self.A
        A.reset(self.low_mark)
        NT, L = 2048, 128
        SC = 128 ** -0.5
        ps, psR = self.ps, self.psR
        qT = A.alloc("sqT", [128, 8, NT], BF16)
        kT = A.alloc("skT", [128, 8, NT], BF16)
        vtm = A.alloc("svtm", [128, 16, 1024], BF16)
        qR = [Res(f"sq{h}") for h in range(8)]
        kR = [Res(f"sk{h}") for h in range(8)]
        vR = [Res("sv0"), Res("sv1")]
        gRq, gRkv = gR_in
        for hf in range(2):
            for tt in range(8):
                self.idma(vtm[:, hf * 8 + tt, :], gvv, 16 + hf * 8 + tt, f"sv{hf}", reads=[gRkv], writes=[vR[hf]])
        for hl in range(8):
            for hf in range(2):
                self.idma(kT[:, hl, hf * 1024:(hf + 1) * 1024], gkk, 16 + hf * 8 + hl, f"sk{hl}", reads=[gRkv], writes=[kR[hl]])
        for hl in range(8):
            for hf in range(2):
                self.idma(qT[:, hl, hf * 1024:(hf + 1) * 1024], gqq, 16 + hf * 8 + hl, f"sq{hl}", reads=[gRq], writes=[qR[hl]])
        E = [A.alloc("sE", [128, 512], F32) for _ in range(2)]
        SP = [A.alloc("sSP", [128, 512], F32) for _ in range(2)]
        LB = [A.alloc("sLB", [128, 512], BF16) for _ in range(3)]
        T1 = [A.alloc("sT1", [128, 512], F32) for _ in range(5)]
        AT = [A.alloc("sAT", [128, 512], BF16) for _ in range(2)]
        OB = [A.alloc("sOB", [128, 512], F32) for _ in range(2)]
        eR = [Res(f"sE{i}") for i in range(2)]
        spR = [Res(f"sSP{i}") for i in range(2)]
        lbR = [Res(f"sLB{i}") for i in range(3)]
        t1R = [Res(f"sT1{i}") for i in range(5)]
        atR = [Res("sAT0"), Res("sAT1")]
        obR = [Res("sOB0"), Res("sOB1")]
        tiles = []
        grp = 0
        for G in range(4):
            for hl in range(8):
                kbs = list(range(4 * G + 3, -1, -1))
                for kb in kbs:
                    tiles.append((hl, G, kb, grp % 2, kb == kbs[0], kb == 0))
                grp += 1
        nt = len(tiles)
        last_d0 = max(i for i, t_ in enumerate(tiles) if t_[1] == 1)

        class Gm:
            pass

        def geom(k):
            g = Gm()
            g.hl, g.G, g.kb, g.g2, g.first, g.last = tiles[k]
            g.c0 = max(g.kb, 4 * g.G) - 4 * g.G
            g.diag = g.kb >= 4 * g.G
            r0 = (g.c0 + 1) * L if g.diag else 0
            g.cs = slice(g.c0 * L, 512)
            g.ds = slice(g.c0 * L, (g.c0 + 1) * L)
            g.rs = slice(r0, 512)
            g.has_rt = r0 < 512
            g.pz, g.pa = k % 2, 2 + k % 2
            g.prt = 4 if g.g2 == 0 else 7
            g.po = 5 + g.g2
            g.q0 = (4 * g.G + g.c0) * L
            g.e, g.sp, g.lb, g.t1, g.at = k % 2, k % 2, k % 3, k % 5, k % 2
            return g

        def pe_z(k):
            g = geom(k)
            P.op("pe", lambda e: e.matmul(ps[g.pz][:, g.cs], lhsT=kT[:, g.hl, g.kb * L:(g.kb + 1) * L], rhs=qT[:, g.hl, g.q0:(4 * g.G + 4) * L], start=True, stop=True),
                 reads=[kR[g.hl], qR[g.hl]], writes=[psR[g.pz]])

        def act_a(k):
            g = geom(k)
            P.op("act", lambda e: e.activation(out=E[g.e][:, g.cs], in_=ps[g.pz][:, g.cs], func=AF.Exp, scale=SC), reads=[psR[g.pz]], writes=[eR[g.e]])
            P.op("act", lambda e: e.activation(out=SP[g.sp][:, g.cs], in_=E[g.e][:, g.cs], func=AF.Ln, scale=1.0, bias=self.one_ap()),
                 reads=[eR[g.e], self.onesR], writes=[spR[g.sp]])

        def dve_t1(k):
            g = geom(k)
            P.op("dve", lambda e: e.scalar_tensor_tensor(out=T1[g.t1][:, g.cs], in0=ps[g.pz][:, g.cs], scalar=SC, in1=SP[g.sp][:, g.cs], op0=ALU.mult, op1=ALU.subtract),
                 reads=[psR[g.pz], spR[g.sp]], writes=[t1R[g.t1]])

        def pool_lb(k):
            g = geom(k)
            P.op("pool", lambda e: e.tensor_scalar(out=LB[g.lb][:, g.cs], in0=SP[g.sp][:, g.cs], scalar1=-1.0, scalar2=0.0, op0=ALU.mult, op1=ALU.add),
                 reads=[spR[g.sp]], writes=[lbR[g.lb]])
            if g.diag:
                P.op("pool", lambda e: e.tensor_tensor(out=LB[g.lb][:, g.ds], in0=LB[g.lb][:, g.ds], in1=self.maskd, op=ALU.mult),
                     reads=[lbR[g.lb], self.c16R], writes=[lbR[g.lb]])

        def pe_u(k):
            g = geom(k)
            P.op("pe", lambda e: e.matmul(ps[g.prt][:, g.cs], lhsT=self.utri, rhs=LB[g.lb][:, g.cs], start=g.first, stop=False),
                 reads=[lbR[g.lb], self.c16R], writes=[psR[g.prt]])

        def dve_x(k):
            g = geom(k)
            P.op("dve", lambda e: e.tensor_tensor(out=T1[g.t1][:, g.cs], in0=ps[g.prt][:, g.cs], in1=T1[g.t1][:, g.cs], op=ALU.add),
                 reads=[psR[g.prt], t1R[g.t1]], writes=[t1R[g.t1]])

        def pe_ones(k):
            g = geom(k)
            if not g.last:
                P.op("pe", lambda e: e.matmul(ps[g.prt][:, g.cs], lhsT=self.ltri, rhs=LB[g.lb][:, g.cs], start=False, stop=(g.kb == 1)),
                     reads=[lbR[g.lb], self.c16R], writes=[psR[g.prt]])

        def act_b(k):
            g = geom(k)
            P.op("act", lambda e: e.activation(out=AT[g.at][:, g.cs], in_=T1[g.t1][:, g.cs], func=AF.Exp), reads=[t1R[g.t1]], writes=[atR[g.at]])

        def pool_at(k):
            g = geom(k)
            if g.diag:
                P.op("pool", lambda e: e.tensor_tensor(out=AT[g.at][:, g.ds], in0=AT[g.at][:, g.ds], in1=self.maskd, op=ALU.mult),
                     reads=[atR[g.at], self.c16R], writes=[atR[g.at]])

        def pe_av(k):
            g = geom(k)
            P.op("pe", lambda e: e.matmul(ps[g.po][:, g.cs], lhsT=vtm[:, g.kb, g.hl * L:(g.hl + 1) * L], rhs=AT[g.at][:, g.cs], start=g.first, stop=g.last),
                 reads=[vR[g.kb // 8], atR[g.at]], writes=[psR[g.po]])
            if g.last:
                P.op("act", lambda e: e.activation(out=OB[g.g2][:], in_=ps[g.po][:], func=AF.Copy), reads=[psR[g.po]], writes=[obR[g.g2]])
                P.dma("sp", xo[g.G // 2, g.hl * L:(g.hl + 1) * L, (g.G % 2) * 512:(g.G % 2 + 1) * 512], OB[g.g2][:], f"sOB{g.g2}", reads=[obR[g.g2], xoR[g.G // 2]])
                if half_hook and k == last_d0:
                    half_hook(0)

        ok = lambda k: 0 <= k < nt
        for j in range(nt + 7):
            if ok(j - 6):
                pe_av(j - 6)
            if ok(j):
                pe_z(j)
            if j == 3:
                pe_u(0)
            if ok(j - 4):
                dve_x(j - 4)
                pe_ones(j - 4)
                if ok(j - 3):
                    pe_u(j - 3)
            if ok(j - 5):
                act_b(j - 5)
                pool_at(j - 5)
            if ok(j - 1):
                act_a(j - 1)
                dve_t1(j - 1)
            if ok(j - 2):
                pool_lb(j - 2)
        if half_hook:
            half_hook(1)
        self.release(self.low_mark)
        A.reset(self.phase_mark)


def _mk(nc):
    def I(name, shape, dt=F32):
        return nc.dram_tensor(name, list(shape), dt, kind="ExternalInput").ap()

    def O(name, shape, dt=F32):
        return nc.dram_tensor(name, list(shape), dt, kind="ExternalOutput").ap()

    return I, O


A_IN = 6160
XQ = (2, 4, 2, 128, T)
XK = (2, T, 512)
XV = (2, T, 1024)
XG = (2, 2, 4, T)
XH = (2, 1024, T)
SQ = (2, 8, 128, T)
SV = (2, T, 1024)


def build_fused(dbg=None):
    nc = bass.Bass("TRN2", target_bir_lowering=False)
    I, O = _mk(nc)

    def N(name, shape, dt=F32):
        return nc.dram_tensor(name, list(shape), dt, kind="Internal").ap()

    B = Builder(nc, fused=True)
    P = B.P
    cst = I("cst", [128, NCST])
    idx = I("idx", [128, 48], mybir.dt.uint32)
    sel = I("sel", [128, 2])
    hng = I("hng", [128, 16])
    mc16, mc32 = I("mc16", [128, 1024], BF16), I("mc32", [4, 4608])
    xT = I("xT", [D, T])
    yT = O("yT", [D, T])
    if dbg is None:
        f1i, f1o = I("ffn1_w_in", [4, D, 2 * DFF]), I("ffn1_w_out", [4, DFF, D])
        f2i, f2o = I("ffn2_w_in", [4, D, 2 * DFF]), I("ffn2_w_out", [4, DFF, D])
        kvw = I("kv_w", [D, 2 * D])
        bwq, bwo = I("b_w_q", [2, D, D]), I("b_w_out", [2, D, D])
    awi, awo = I("a_w_in", [2, D, A_IN]), I("a_w_out", [2, D, D])
    xq, gq = N("xq", [2048, T], BF16), N("gq", [4096, T], BF16)
    xk, gk = N("xk", [2048, 512], BF16), N("gk", [4096, 512], BF16)
    xv, gv = N("xv", [2048, 1024], BF16), N("gv", [4096, 1024], BF16)
    xg, gg = N("xg", [16, T]), N("gg", [32, T])
    xh, gh = N("xh", [2048, T]), N("gh", [4096, T])
    xkk, gkk = N("xkk", [2048, T], BF16), N("gkk", [4096, T], BF16)
    ogs = N("ogs", [D, T])
    xR = {k: Res("xbuf_" + k) for k in ("q", "k", "v", "g", "kk")}
    gR, ghR = Res("gbuf"), Res("ghbuf")
    gRq, gRkv = Res("gbuf_q"), Res("gbuf_kv")
    xhR = [Res("xh0"), Res("xh1")]
    ogsR = [Res(f"ogs{i}") for i in range(KC)]
    xq5 = xq.rearrange("(d h j p) t -> d h j p t", d=2, h=4, j=2)
    xk3 = xk.rearrange("(d t) c -> d t c", d=2)
    xv3 = xv.rearrange("(d t) c -> d t c", d=2)
    xg4 = xg.rearrange("(d g h) t -> d g h t", d=2, g=2)
    xh3 = xh.rearrange("(d r) t -> d r t", d=2)
    xqq4 = xq.rearrange("(d h p) t -> d h p t", d=2, h=8)
    xkk4 = xkk.rearrange("(d h p) t -> d h p t", d=2, h=8)

    def xh_hook(hf):
        for c in (2 * hf, 2 * hf + 1):
            P.coll("AllGather", xh[c * 512:(c + 1) * 512, :], gh[c * 1024:(c + 1) * 1024, :], "cc", writes=[xhR[hf], ghR])

    def proj_hook(which):
        a_, g_, n_ = {"q": (xq, gq, 1024), "k": (xk, gk, 2048), "v": (xv, gv, 1024)}[which]
        P.coll_rows(a_, g_, n_, writes=[xR[which], gR])

    B.load_cst(cst)
    B.load_mix_consts(mc16, mc32, idx, sel, hng)
    B.load_x(xT)
    for l in range(2):
        B.rmsnorm(l)
        B.ffn(f1i[l], f1o[l], "f1")
        B.rmsnorm(4 + l)
        B.mlstm_proj(awi[l], l, xq5, xk3, xv3, xg4, ogs, xR, ogsR, hook=proj_hook)
        P.coll_rows(xg, gg, 16, writes=[xR["g"], gR])
        B.mlstm_mix(gq, gk, gv, gg, l, xh3, gR, xhR, half_hook=xh_hook)
        B.post_mix(gh, ghR, ogs, ogsR, awo[l])
        B.rmsnorm(8 + l)
        B.ffn(f2i[l], f2o[l], "f2")
    B.rmsnorm(12)
    B.sb_fmproj(kvw, 0, 16, xkk4, xR["kk"])
    P.coll_rows(xkk, gkk, 1024, writes=[xR["kk"], gRkv])
    B.sb_vproj(kvw, 2048, xv3, xR["v"])
    P.coll_rows(xv, gv, 1024, reads=[gR], writes=[xR["v"], gRkv])
    for j in range(2):
        l = 2 + j
        B.rmsnorm(l)
        B.ffn(f1i[l], f1o[l], "f1")
        B.rmsnorm(4 + l)
        B.sb_fmproj(bwq[j], 0, 16, xqq4, xR["q"])
        P.coll_rows(xq, gq, 1024, reads=[gR], writes=[xR["q"], gRq])
        B.sb_mix(gq, gkk, gv, xh3, (gRq, gRkv), xhR, half_hook=xh_hook)
        B.post_mix(gh, ghR, None, None, bwo[j])
        B.rmsnorm(8 + l)
        B.ffn(f2i[l], f2o[l], "f2")
    B.final_norm(13)
    B.store_x(yT)
    P.emit()
    return nc


def make_idx(r):
    idx = np.zeros((128, 48), np.uint32)
    p = np.arange(128)
    for t, n in enumerate((2048, 1024, 512)):
        for hf in range(2):
            for m_ in range(8):
                R = r * 1024 + m_ * 128 + p
                idx[:, t * 16 + hf * 8 + m_] = (R // n) * 2 * n + hf * n + (R % n)
    return idx


def mix_consts():
    import ml_dtypes
    c16 = np.zeros((128, 1024), np.float32)
    c16[:, 0:128] = np.eye(128)
    s = np.arange(128)[:, None]
    t = np.arange(128)[None, :]
    nm = np.where(s > t, -30000.0, 0.0)
    c16[:, 128:640] = np.tile(nm, (1, 4))
    c16[:, 640:768] = (s < t)
    c16[:, 768:896] = (s > t)
    c16[:, 896:1024] = (s <= t)
    c32 = np.zeros((4, 4608), np.float32)
    for h in range(4):
        c32[h, h * 128:(h + 1) * 128] = 1.0
    r1 = np.ones(2048, np.float32)
    r1[::128] = 0.0
    r2 = np.zeros(2048, np.float32)
    r2[::128] = -1e30
    c32[:, 512:2560] = r1
    c32[:, 2560:4608] = r2
    return c16.astype(ml_dtypes.bfloat16), c32


def pack_cst(inp):
    c = np.zeros((128, NCST), np.float32)
    vecs = [inp["ffn1_norm"][l] for l in range(4)] + [inp["mix_norm"][l] for l in range(4)] + \
           [inp["ffn2_norm"][l] for l in range(4)] + [inp["kv_norm"], inp["final_norm"]] + \
           [inp["a_head_norm"][l] for l in range(2)]
    for i, v in enumerate(vecs):
        c[:, i * 16:(i + 1) * 16] = np.asarray(v, np.float32).reshape(16, 128).T
    for l in range(2):
        c[0:8, 256 + 2 * l] = inp["a_b_gate"][l][0:8]
        c[0:8, 257 + 2 * l] = inp["a_b_gate"][l][8:16]
    c[:, EPSC] = EPS
    return c


_NC = []


def kernel(**inp):
    inp = {k: np.asarray(v) for k, v in inp.items()}
    NCORE = 8
    if not _NC:
        _NC.append(build_fused())
    nc = _NC[0]
    cst = pack_cst(inp)
    mc16, mc32 = mix_consts()
    x = inp["x"]
    ca = np.ascontiguousarray
    wnames = ["ffn1_w_in", "ffn1_w_out", "ffn2_w_in", "ffn2_w_out", "a_w_in", "a_w_out", "kv_w", "b_w_q", "b_w_out"]
    w = {k: ca(inp[k], dtype=np.float32) for k in wnames}
    maps = []
    p = np.arange(128, dtype=np.uint32)[:, None]
    for c in range(NCORE):
        b, r = c // 2, c % 2
        idx = make_idx(r)
        sel = np.zeros((128, 2), np.float32)
        sel[:, 0] = 1.0 - r
        sel[:, 1] = float(r)
        hng = np.zeros((128, 16), np.float32)
        for l in range(2):
            hng[:, l * 8:(l + 1) * 8] = inp["a_head_norm"][l].reshape(8, 2, 128)[4 * r:4 * r + 4].reshape(8, 128).T
        mp = {"cst": cst, "idx": idx, "sel": sel, "hng": hng, "mc16": mc16, "mc32": mc32,
              "xT": ca(x[b, r * T:(r + 1) * T].T)}
        mp.update(w)
        maps.append(mp)
    res = run_bass_kernel_spmd(nc, maps, core_ids=list(range(NCORE)))
    out = np.empty((4, 2048, D), np.float32)
    for c in range(NCORE):
        b, r = c // 2, c % 2
        out[b, r * T:(r + 1) * T, :] = res.results[c]["yT"].T
    return out
```

```python
import numpy as np
import concourse.bass as bass
import concourse.mybir as mybir
from concourse.bass_utils import run_bass_kernel_spmd

F32 = mybir.dt.float32
BF16 = mybir.dt.bfloat16
AF = mybir.ActivationFunctionType
ALU = mybir.AluOpType

D = 2048
KC = 16
DFF = 5632
NFF = DFF // 128
T = 1024
EPS = 1e-6

ENGS = ["pe", "act", "dve", "pool", "sp"]
PAIRS = [[0, 1], [2, 3], [4, 5], [6, 7]]


class Res:
    __slots__ = ("name", "w", "r")

    def __init__(self, name):
        self.name = name
        self.w = None
        self.r = []


class Op:
    __slots__ = ("eng", "fn", "deps", "inc", "cnt", "dma_sem", "dma_val", "is_coll")

    def __init__(self, eng, fn):
        self.eng = eng
        self.fn = fn
        self.deps = set()
        self.inc = False
        self.cnt = 0
        self.dma_sem = None
        self.dma_val = 0
        self.is_coll = False


class Prog:
    def __init__(self, nc):
        self.nc = nc
        self.ops = {e: [] for e in ENGS}
        self.dma_cnt = {}
        self.globalR = Res("global")
        self.rank = {}
        self.use_rank = False

    def _track(self, o, reads, writes):
        deps = set()
        if self.globalR not in writes:
            reads = list(reads) + [self.globalR]
        for r in reads:
            if r.w is not None:
                deps.add(r.w)
        for w in writes:
            if w.w is not None:
                deps.add(w.w)
            deps.update(w.r)
        deps.discard(o)
        o.deps = deps
        for r in reads:
            r.r.append(o)
        for w in writes:
            w.w = o
            w.r = []

    def op(self, eng, fn, reads=(), writes=()):
        o = Op(eng, fn)
        self._track(o, reads, writes)
        self.ops[eng].append(o)
        return o

    def dma(self, queue, out, in_, sem, reads=(), writes=(), **kw):
        def fn(e):
            src = in_(self.rank[queue]) if callable(in_) else in_
            return e.dma_start(out=out, in_=src, **kw)

        o = Op(queue, fn)
        o.dma_sem = sem
        self.dma_cnt[sem] = self.dma_cnt.get(sem, 0) + 16
        o.dma_val = self.dma_cnt[sem]
        self._track(o, reads, writes)
        self.ops[queue].append(o)
        return o

    def coll(self, kind, in_ap, out_ap, sem, reads=(), writes=()):
        o = Op("pool", lambda e: e.collective_compute(kind, ALU.bypass, replica_groups=PAIRS, ins=[in_ap.opt()], outs=[out_ap.opt()]))
        o.dma_sem = sem
        o.is_coll = True
        self.dma_cnt[sem] = self.dma_cnt.get(sem, 0) + 1
        o.dma_val = self.dma_cnt[sem]
        self._track(o, reads, writes)
        self.ops["pool"].append(o)
        return o

    def coll_rows(self, x2d, g2d, n, reads=(), writes=()):
        rows = x2d.shape[0]
        for c in range(rows // n):
            self.coll("AllGather", x2d[c * n:(c + 1) * n, :], g2d[c * 2 * n:(c + 1) * 2 * n, :], "cc", reads=reads, writes=writes)

    def barrier(self, scratch):
        self.op("dve", lambda e: e.memset(scratch, 0.0), writes=[self.globalR])

    def emit(self, final_dma_ops=()):
        nc = self.nc
        for e in ENGS:
            for o in self.ops[e]:
                for d in o.deps:
                    if d.dma_sem is None:
                        if d.eng == "pe" and o.eng == "pe":
                            continue
                        d.inc = True
        for e in ENGS:
            c = 0
            for o in self.ops[e]:
                if o.dma_sem is None and o.inc:
                    c += 1
                    o.cnt = c
        from contextlib import ExitStack

        with ExitStack() as st:
            esem = {e: st.enter_context(nc.semaphore("s_" + e)) for e in ENGS}
            dsem = {k: st.enter_context(nc.semaphore("d_" + k)) for k in self.dma_cnt}
            block = st.enter_context(nc.Block())

            def run(eng_name, eng):
                waited = {}
                if self.use_rank and eng_name == "sp":
                    self.rank[eng_name] = eng.cc_rank(PAIRS)
                for o in self.ops[eng_name]:
                    need = {}
                    for d in o.deps:
                        if d.dma_sem is not None:
                            key = ("d", d.dma_sem)
                            val = d.dma_val
                        else:
                            if d.eng == "pe" and eng_name == "pe":
                                continue
                            key = ("e", d.eng)
                            val = d.cnt
                        if val > need.get(key, 0):
                            need[key] = val
                    for key, val in need.items():
                        if val > waited.get(key, 0):
                            waited[key] = val
                            s = dsem[key[1]] if key[0] == "d" else esem[key[1]]
                            eng.wait_ge(s, val)
                    ins = o.fn(eng)
                    if o.is_coll:
                        ins.then_inc(dsem[o.dma_sem], 1)
                    elif o.dma_sem is not None:
                        ins.then_inc(dsem[o.dma_sem], 16)
                    elif o.inc:
                        ins.then_inc(esem[eng_name], 1)
                if eng_name == "sp":
                    for k, v in self.dma_cnt.items():
                        eng.wait_ge(dsem[k], v)

            @block.tensor
            def _(e):
                run("pe", e)

            @block.scalar
            def _(e):
                run("act", e)

            @block.vector
            def _(e):
                run("dve", e)

            @block.gpsimd
            def _(e):
                run("pool", e)

            @block.sync
            def _(e):
                run("sp", e)


class Arena:
    def __init__(self, nc, nbytes):
        self.nc = nc
        self.slab = nc.alloc_sbuf_tensor("arena_slab", [128, nbytes // 4], F32)
        self.base = nc.lookup_mloc(self.slab).addr
        self.nbytes = nbytes
        self.off = 0
        self.n = 0
        self.hi = 0

    def alloc(self, name, shape, dtype):
        sz = 1
        for s in shape[1:]:
            sz *= s
        sz *= 2 if dtype == BF16 else 4
        sz = (sz + 63) // 64 * 64
        assert self.off + sz <= self.nbytes, (name, self.off, sz, self.nbytes)
        self.n += 1
        t = self.nc.alloc_sbuf_tensor_at(f"{name}_{self.n}", list(shape), dtype, offset=self.base + self.off)
        self.off += sz
        self.hi = max(self.hi, self.off)
        return t

    def mark(self):
        return self.off

    def reset(self, m):
        self.off = m


ARENA = 204 * 1024
NCST = 264
EPSC = 260


class Builder:
    def __init__(self, nc, resident=True, fused=False):
        self.nc = nc
        self.fused = fused
        self.P = Prog(nc)
        self.A = Arena(nc, ARENA)
        A = self.A
        self.xT = A.alloc("xT", [128, KC, T], F32)
        self.xR = [[Res(f"x{k}_{h}") for h in range(2)] for k in range(KC)]
        self.cst = A.alloc("cst", [128, NCST], F32)
        self.cstR = Res("cst")
        self.ones32 = A.alloc("ones32", [128, 128], F32)
        self.onesR = Res("ones32")
        self.rs = A.alloc("rs", [128, T], F32)
        self.rsR = Res("rs")
        self.ps = [nc.alloc_psum_tensor(f"ps{i}", [128, 512], F32) for i in range(8)]
        self.psR = [Res(f"ps{i}") for i in range(8)]
        self.P.op("pool", lambda e: e.memset(self.ones32[:], 1.0), writes=[self.onesR])
        self.bscr = A.alloc("bscr", [128, 16], F32)
        self.idx = A.alloc("idx", [128, 48], mybir.dt.uint32)
        self.idxR = Res("idx")
        self.sel = A.alloc("sel", [128, 2], F32)
        self.selR = Res("sel")
        self.hgt = A.alloc("hgt", [128, 16], F32)
        self.hgR = Res("hgt")
        self.c16 = A.alloc("c16", [128, 1024], BF16)
        self.c16R = Res("c16")
        self.c32 = A.alloc("c32", [4, 512], F32)
        self.c32R = Res("c32")
        self.on16 = A.alloc("on16", [128, 128], BF16)
        self.low_mark = A.mark()
        self.hT = A.alloc("hT", [128, KC, T], BF16)
        self.hR = [Res(f"h{k}") for k in range(KC)]
        self.phase_mark = A.mark()

    def idma(self, out, in2d, col, sem, reads=(), writes=()):
        P = self.P
        off = bass.IndirectOffsetOnAxis(ap=self.idx[:, col:col + 1], axis=0)
        o = Op("pool", lambda e: e.indirect_dma_start(out=out, out_offset=None, in_=in2d, in_offset=off))
        o.dma_sem = sem
        P.dma_cnt[sem] = P.dma_cnt.get(sem, 0) + 16
        o.dma_val = P.dma_cnt[sem]
        P._track(o, list(reads) + [self.idxR], writes)
        P.ops["pool"].append(o)
        return o

    def release(self, m):
        self.A.reset(m)
        self.P.barrier(self.bscr[:, 0:1])

    def rmsnorm(self, gi, out=None, outR=None):
        P, A = self.P, self.A
        out = self.hT if out is None else out
        outR = self.hR if outR is None else outR
        m = A.mark()
        sq = [A.alloc("sq", [128, T], F32) for _ in range(2)]
        sqR = [Res("sq0"), Res("sq1")]
        xT, ones32, rs, ps, psR = self.xT, self.ones32, self.rs, self.ps, self.psR
        for kc in range(KC):
            b = kc % 2
            P.op("act", lambda e, kc=kc, b=b: e.activation(out=sq[b][:], in_=xT[:, kc, :], func=AF.Square),
                 reads=self.xR[kc], writes=[sqR[b]])
            for h in range(2):
                P.op("pe", lambda e, kc=kc, b=b, h=h: e.matmul(ps[6 + h][:], lhsT=ones32[:], rhs=sq[b][:, h * 512:(h + 1) * 512],
                                                              start=(kc == 0), stop=(kc == KC - 1)),
                     reads=[sqR[b], self.onesR], writes=[psR[6 + h]])
        for h in range(2):
            P.op("act", lambda e, h=h: e.activation(out=rs[:, h * 512:(h + 1) * 512], in_=ps[6 + h][:], func=AF.Sqrt,
                                                    scale=1.0 / D, bias=self.eps_ap()),
                 reads=[psR[6 + h], self.cstR], writes=[self.rsR])
        P.op("dve", lambda e: e.reciprocal(out=rs[:], in_=rs[:]), reads=[self.rsR], writes=[self.rsR])
        for kc in range(KC):
            P.op("dve", lambda e, kc=kc: e.scalar_tensor_tensor(out=out[:, kc, :], in0=xT[:, kc, :],
                                                                scalar=self.cst[:, gi * 16 + kc:gi * 16 + kc + 1], in1=rs[:],
                                                                op0=ALU.mult, op1=ALU.mult),
                 reads=self.xR[kc] + [self.rsR, self.cstR], writes=[outR[kc]])
        self.release(m)

    def final_norm(self, gi):
        P, A = self.P, self.A
        m = A.mark()
        sq = [A.alloc("sq", [128, T], F32) for _ in range(2)]
        sqR = [Res("sq0"), Res("sq1")]
        xT, ones32, rs, ps, psR = self.xT, self.ones32, self.rs, self.ps, self.psR
        for kc in range(KC):
            b = kc % 2
            P.op("act", lambda e, kc=kc, b=b: e.activation(out=sq[b][:], in_=xT[:, kc, :], func=AF.Square),
                 reads=self.xR[kc], writes=[sqR[b]])
            for h in range(2):
                P.op("pe", lambda e, kc=kc, b=b, h=h: e.matmul(ps[6 + h][:], lhsT=ones32[:], rhs=sq[b][:, h * 512:(h + 1) * 512],
                                                              start=(kc == 0), stop=(kc == KC - 1)),
                     reads=[sqR[b], self.onesR], writes=[psR[6 + h]])
        for h in range(2):
            P.op("act", lambda e, h=h: e.activation(out=rs[:, h * 512:(h + 1) * 512], in_=ps[6 + h][:], func=AF.Sqrt,
                                                    scale=1.0 / D, bias=self.eps_ap()),
                 reads=[psR[6 + h], self.cstR], writes=[self.rsR])
        P.op("dve", lambda e: e.reciprocal(out=rs[:], in_=rs[:]), reads=[self.rsR], writes=[self.rsR])
        for kc in range(KC):
            P.op("dve", lambda e, kc=kc: e.scalar_tensor_tensor(out=xT[:, kc, :], in0=xT[:, kc, :],
                                                                scalar=self.cst[:, gi * 16 + kc:gi * 16 + kc + 1], in1=rs[:],
                                                                op0=ALU.mult, op1=ALU.mult),
                 reads=self.xR[kc] + [self.rsR, self.cstR], writes=self.xR[kc])
        self.release(m)

    def eps_ap(self):
        return self.cst[:, EPSC:EPSC + 1]

    def ffn(self, w_in, w_out, tag):
        P, A = self.P, self.A
        m = A.mark()
        w_in_v = w_in.rearrange("(kc p) f -> p kc f", p=128)
        w_out_v = w_out.rearrange("(j p) d -> p j d", p=128)
        NST = 6
        stg = [A.alloc("stg", [128, 2048], F32) for _ in range(NST)]
        stgR = [Res(f"stg{i}") for i in range(NST)]
        wg = [A.alloc("wg", [128, KC, 128], BF16) for _ in range(2)]
        wu = [A.alloc("wu", [128, KC, 128], BF16) for _ in range(2)]
        wgR = [Res("wg0"), Res("wg1")]
        wuR = [Res("wu0"), Res("wu1")]
        wo = [A.alloc("wo", [128, D], BF16) for _ in range(4)]
        woR = [Res(f"wo{i}") for i in range(4)]
        g = [A.alloc("g", [128, T], BF16) for _ in range(4)]
        gR = [[Res(f"g{i}_{h}") for h in range(2)] for i in range(4)]
        sg = [A.alloc("sg", [128, 512], F32) for _ in range(2)]
        sgR = [Res("sg0"), Res("sg1")]
        xT, hT, ps, psR = self.xT, self.hT, self.ps, self.psR

        def dma_issue(j):
            s = 3 * (j % 2)
            P.dma("sp", stg[s][:].rearrange("p (k f) -> p k f", k=KC), w_in_v[:, :, j * 128:(j + 1) * 128], f"st{s}",
                  writes=[stgR[s]])
            P.dma("sp", stg[s + 1][:].rearrange("p (k f) -> p k f", k=KC), w_in_v[:, :, DFF + j * 128:DFF + (j + 1) * 128],
                  f"st{s+1}", writes=[stgR[s + 1]])
            P.dma("sp", stg[s + 2][:], w_out_v[:, j, :], f"st{s+2}", writes=[stgR[s + 2]])

        def cast_in(j):
            s = 3 * (j % 2)
            b = j % 2
            P.op("act", lambda e: e.activation(out=wg[b][:].rearrange("p k f -> p (k f)"), in_=stg[s][:], func=AF.Copy),
                 reads=[stgR[s]], writes=[wgR[b]])
            P.op("pool", lambda e: e.tensor_copy(out=wu[b][:].rearrange("p k f -> p (k f)"), in_=stg[s + 1][:]),
                 reads=[stgR[s + 1]], writes=[wuR[b]])

        def cast_wo(j):
            s = 3 * (j % 2)
            P.op("pool", lambda e: e.tensor_copy(out=wo[j % 4][:], in_=stg[s + 2][:]),
                 reads=[stgR[s + 2]], writes=[woR[j % 4]])

        def win(j, after_half=None):
            b = j % 2
            gs = j % 4
            for h in range(2):
                for (wt, wR, pi) in ((wg, wgR, h), (wu, wuR, 2 + h)):
                    for kc in range(KC):
                        P.op("pe", lambda e, wt=wt, pi=pi, kc=kc, h=h: e.matmul(
                            ps[pi][:], lhsT=wt[b][:, kc, :], rhs=hT[:, kc, h * 512:(h + 1) * 512],
                            start=(kc == 0), stop=(kc == KC - 1)),
                            reads=[wR[b], self.hR[kc]], writes=[psR[pi]])
                P.op("act", lambda e, h=h: e.activation(out=sg[h][:], in_=ps[h][:], func=AF.Silu),
                     reads=[psR[h]], writes=[sgR[h]])
                P.op("dve", lambda e, h=h: e.tensor_tensor(out=g[gs][:, h * 512:(h + 1) * 512], in0=sg[h][:], in1=ps[2 + h][:],
                                                           op=ALU.mult),
                     reads=[sgR[h], psR[2 + h]], writes=[gR[gs][h]])
                if after_half is not None:
                    after_half(h)

        ycnt = [0]

        def wout(grp, dr=range(KC)):
            for d in dr:
                for h in range(2):
                    pi = 4 + (ycnt[0] % 4)
                    ycnt[0] += 1
                    for n, j in enumerate(grp):
                        P.op("pe", lambda e, pi=pi, j=j, d=d, h=h, n=n: e.matmul(
                            ps[pi][:], lhsT=wo[j % 4][:, d * 128:(d + 1) * 128], rhs=g[j % 4][:, h * 512:(h + 1) * 512],
                            start=(n == 0), stop=(n == len(grp) - 1)),
                            reads=[woR[j % 4], gR[j % 4][h]], writes=[psR[pi]])
                    P.op("dve", lambda e, pi=pi, d=d, h=h: e.scalar_tensor_tensor(
                        out=xT[:, d, h * 512:(h + 1) * 512], in0=ps[pi][:], scalar=0.5,
                        in1=xT[:, d, h * 512:(h + 1) * 512], op0=ALU.mult, op1=ALU.add),
                        reads=[psR[pi], self.xR[d][h]], writes=[self.xR[d][h]])

        dma_issue(0)
        dma_issue(1)
        cast_in(0)
        cast_wo(0)
        for j in range(NFF):
            if j + 2 < NFF:
                dma_issue(j + 2)
            if j + 1 < NFF:
                cast_in(j + 1)
            g0 = j - 2 if j % 2 == 0 else j - 3
            if g0 >= 0:
                win(j, lambda h, j=j, g0=g0: wout((g0, g0 + 1), range(4 * (2 * (j % 2) + h), 4 * (2 * (j % 2) + h) + 4)))
            else:
                win(j)
            if j + 1 < NFF:
                cast_wo(j + 1)
        wout((NFF - 2, NFF - 1))
        self.release(m)

    def fm_proj(self, w, col0, ncols, evac, tag, src=None, srcR=None):
        P, A = self.P, self.A
        src = self.hT if src is None else src
        srcR = self.hR if srcR is None else srcR
        m = A.mark()
        wv = w.rearrange("(kc p) f -> p kc f", p=128)
        nu = ncols // 128
        stg = [A.alloc("fstg", [128, KC, 128], F32) for _ in range(3)]
        stgR = [Res(f"fstg{i}") for i in range(3)]
        wb = [A.alloc("fwb", [128, KC, 128], BF16) for _ in range(2)]
        wbR = [Res("fwb0"), Res("fwb1")]
        ps, psR = self.ps, self.psR

        def issue(u):
            P.dma("sp", stg[u % 3][:], wv[:, :, col0 + u * 128:col0 + (u + 1) * 128], f"fst{u % 3}", writes=[stgR[u % 3]])

        def cast(u):
            eng = "act" if u % 2 == 0 else "pool"
            if eng == "act":
                P.op("act", lambda e: e.activation(out=wb[u % 2][:], in_=stg[u % 3][:], func=AF.Copy),
                     reads=[stgR[u % 3]], writes=[wbR[u % 2]])
            else:
                P.op("pool", lambda e: e.tensor_copy(out=wb[u % 2][:], in_=stg[u % 3][:]),
                     reads=[stgR[u % 3]], writes=[wbR[u % 2]])

        issue(0)
        if nu > 1:
            issue(1)
        cast(0)
        for u in range(nu):
            if u + 2 < nu:
                issue(u + 2)
            if u + 1 < nu:
                cast(u + 1)
            for h in range(2):
                pi = (2 * u + h) % 4
                for kc in range(KC):
                    P.op("pe", lambda e, pi=pi, kc=kc, h=h, u=u: e.matmul(
                        ps[pi][:], lhsT=wb[u % 2][:, kc, :], rhs=src[:, kc, h * 512:(h + 1) * 512],
                        start=(kc == 0), stop=(kc == KC - 1)), reads=[wbR[u % 2], srcR[kc]], writes=[psR[pi]])
                evac(u, h, ps[pi], psR[pi])
        self.release(m)

    def tm_proj(self, w, col0, ncols, evac, tag, src=None, srcR=None):
        P, A = self.P, self.A
        src = self.hT if src is None else src
        srcR = self.hR if srcR is None else srcR
        m = A.mark()
        wv = w.rearrange("(kc p) f -> p kc f", p=128)
        nb = ncols // 512
        stg = [A.alloc("tstg", [128, 4, 512], F32) for _ in range(4)]
        stgR = [Res(f"tstg{i}") for i in range(4)]
        wb = [A.alloc("twb", [128, KC, 512], BF16) for _ in range(2)]
        wbR = [[Res(f"twb{i}_{q}") for q in range(4)] for i in range(2)]
        ps, psR = self.ps, self.psR
        n = [0]
        for cb in range(nb):
            for q in range(4):
                s = n[0] % 4
                n[0] += 1
                P.dma("sp", stg[s][:], wv[:, q * 4:(q + 1) * 4, col0 + cb * 512:col0 + (cb + 1) * 512], f"tst{s}",
                      writes=[stgR[s]])
                if q % 2 == 0:
                    P.op("act", lambda e, s=s, q=q, cb=cb: e.activation(out=wb[cb % 2][:, q * 4:(q + 1) * 4, :], in_=stg[s][:], func=AF.Copy),
                         reads=[stgR[s]], writes=[wbR[cb % 2][q]])
                else:
                    P.op("pool", lambda e, s=s, q=q, cb=cb: e.tensor_copy(out=wb[cb % 2][:, q * 4:(q + 1) * 4, :], in_=stg[s][:]),
                         reads=[stgR[s]], writes=[wbR[cb % 2][q]])
            for tt in range(T // 128):
                pi = tt % 4
                for kc in range(KC):
                    P.op("pe", lambda e, pi=pi, kc=kc, tt=tt, cb=cb: e.matmul(
                        ps[pi][:], lhsT=src[:, kc, tt * 128:(tt + 1) * 128], rhs=wb[cb % 2][:, kc, :],
                        start=(kc == 0), stop=(kc == KC - 1)), reads=[wbR[cb % 2][kc // 4], srcR[kc]], writes=[psR[pi]])
                evac(cb, tt, ps[pi], psR[pi])
        self.release(m)

    def mlstm_proj(self, w_in, l, xq, xk, xv, xg, ogs, xR, ogsR, hook=None):
        P, A = self.P, self.A
        m = A.mark()
        ob = [A.alloc("ob", [128, T], BF16) for _ in range(3)]
        obR = [Res(f"ob{i}") for i in range(3)]
        o32 = [A.alloc("o32", [128, T], F32) for _ in range(2)]
        o32R = [Res("o32a"), Res("o32b")]
        tb = [A.alloc("tb", [128, 512], BF16) for _ in range(3)]
        tbR = [Res(f"tb{i}") for i in range(3)]

        def ev_qk(u, h, ps, psR):
            b = u % 3
            P.op("act", lambda e: e.activation(out=ob[b][:, h * 512:(h + 1) * 512], in_=ps[:], func=AF.Copy),
                 reads=[psR], writes=[obR[b]])
            if h == 1:
                head, j = (u, 0) if u < 8 else (u - 8, 1)
                P.dma("sp", xq[head // 4, head % 4, j], ob[b][:], f"ob{b}", reads=[obR[b], xR["q"]])

        self.fm_proj(w_in, 0, 2048, ev_qk, "qk")
        if hook:
            hook("q")

        def ev_og(u, h, ps, psR):
            b = u % 2
            P.op("act", lambda e: e.activation(out=o32[b][:, h * 512:(h + 1) * 512], in_=ps[:], func=AF.Sigmoid),
                 reads=[psR], writes=[o32R[b]])
            if h == 1:
                P.dma("sp", ogs[u * 128:(u + 1) * 128, :], o32[b][:], f"o32{b}", reads=[o32R[b]], writes=[ogsR[u]])

        self.fm_proj(w_in, 4096, 2048, ev_og, "og")

        cnt = [0]

        def ev_k(cb, tt, ps, psR):
            b = cnt[0] % 3
            cnt[0] += 1
            P.op("act", lambda e: e.activation(out=tb[b][:], in_=ps[:], func=AF.Copy), reads=[psR], writes=[tbR[b]])
            P.dma("sp", xk[cb, tt * 128:(tt + 1) * 128, :], tb[b][:], f"tb{b}", reads=[tbR[b], xR["k"]])

        self.tm_proj(w_in, 1024, 1024, ev_k, "k")
        if hook:
            hook("k")

        def ev_v(cb, tt, ps, psR):
            b = cnt[0] % 3
            cnt[0] += 1
            P.op("act", lambda e: e.activation(out=tb[b][:], in_=ps[:], func=AF.Copy), reads=[psR], writes=[tbR[b]])
            P.dma("sp", xv[cb // 2, tt * 128:(tt + 1) * 128, (cb % 2) * 512:(cb % 2 + 1) * 512], tb[b][:], f"tb{b}",
                  reads=[tbR[b], xR["v"]])

        self.tm_proj(w_in, 2048, 2048, ev_v, "v")
        if hook:
            hook("v")

        gs = A.alloc("gs", [128, KC, 16], F32)
        gsR = Res("gs")
        gw = A.alloc("gw", [128, KC, 16], BF16)
        gwR = Res("gw")
        gio = A.alloc("gio", [8, 2, T], F32)
        gioR = Res("gio")
        gt = A.alloc("gt", [8, T], F32)
        gtR = Res("gt")
        bb = A.alloc("bb", [8, 2], F32)
        bbR = Res("bb")
        wv = w_in.rearrange("(kc p) f -> p kc f", p=128)
        P.dma("sp", gs[:], wv[:, :, 6144:6160], "gs", writes=[gsR])
        P.op("dve", lambda e: e.tensor_copy(out=gw[:], in_=gs[:]), reads=[gsR], writes=[gwR])
        cb = 256 + 2 * l
        P.op("dve", lambda e: e.tensor_scalar(out=bb[:, 0:1], in0=self.cst[0:8, cb:cb + 1], scalar1=1.0 / 15.0, scalar2=None,
                                              op0=ALU.mult), reads=[self.cstR], writes=[bbR])
        P.op("dve", lambda e: e.tensor_scalar(out=bb[:, 1:2], in0=self.cst[0:8, cb + 1:cb + 2], scalar1=-1.0, scalar2=None,
                                              op0=ALU.mult), reads=[self.cstR, bbR], writes=[bbR])
        ps, psR = self.ps, self.psR
        for h in range(2):
            for gi in range(2):
                pi = 2 * h + gi
                for kc in range(KC):
                    P.op("pe", lambda e, pi=pi, kc=kc, h=h, gi=gi: e.matmul(
                        ps[pi][0:8, :], lhsT=gw[:, kc, gi * 8:(gi + 1) * 8], rhs=self.hT[:, kc, h * 512:(h + 1) * 512],
                        start=(kc == 0), stop=(kc == KC - 1)), reads=[gwR, self.hR[kc]], writes=[psR[pi]])
            P.op("act", lambda e, h=h: e.activation(out=gt[:, h * 512:(h + 1) * 512], in_=ps[2 * h][0:8, :], func=AF.Tanh,
                                                    scale=1.0 / 15.0, bias=bb[:, 0:1]), reads=[psR[2 * h], bbR], writes=[gtR])
            P.op("dve", lambda e, h=h: e.tensor_scalar(out=gio[:, 0, h * 512:(h + 1) * 512], in0=gt[:, h * 512:(h + 1) * 512],
                                                       scalar1=15.0, scalar2=None, op0=ALU.mult), reads=[gtR], writes=[gioR])
            P.op("act", lambda e, h=h: e.activation(out=gt[:, h * 512:(h + 1) * 512], in_=ps[2 * h + 1][0:8, :], func=AF.Exp,
                                                    scale=-1.0, bias=bb[:, 1:2]), reads=[psR[2 * h + 1], bbR, gioR], writes=[gtR])
            P.op("act", lambda e, h=h: e.activation(out=gt[:, h * 512:(h + 1) * 512], in_=gt[:, h * 512:(h + 1) * 512], func=AF.Ln,
                                                    scale=1.0, bias=self.one_ap()[0:8, :]), reads=[gtR, self.cstR], writes=[gtR])
            P.op("dve", lambda e, h=h: e.tensor_scalar(out=gio[:, 1, h * 512:(h + 1) * 512], in0=gt[:, h * 512:(h + 1) * 512],
                                                       scalar1=-1.0, scalar2=None, op0=ALU.mult), reads=[gtR], writes=[gioR])
        for d in range(2):
            for gi in range(2):
                P.dma("sp", xg[d, gi], gio[4 * d:4 * d + 4, gi, :], "gio", reads=[gioR, xR["g"]])
        self.release(m)

    def one_ap(self):
        return self.ones32[:, 0:1]

    def load_mix_consts(self, mc16, mc32, idx_d, sel_d, hng_d):
        P = self.P
        self.mc32 = mc32
        P.dma("sp", self.c16[:], mc16, "c16", writes=[self.c16R])
        P.dma("sp", self.c32[:], mc32[:, 0:512], "c32", writes=[self.c32R])
        P.dma("sp", self.idx[:], idx_d, "idx", writes=[self.idxR])
        P.dma("sp", self.sel[:], sel_d, "sel", writes=[self.selR])
        P.dma("sp", self.hgt[:], hng_d, "hgt", writes=[self.hgR])
        P.op("pool", lambda e: e.memset(self.on16[:], 1.0), writes=[self.c16R], reads=[])
        self.ident = self.c16[:, 0:128]
        self.negmask = self.c16[:, 128:640]
        self.maskd = self.c16[:, 640:768]
        self.utri = self.c16[:, 768:896]
        self.ltri = self.c16[:, 896:1024]
        self.ind4 = self.c32[:, 0:512]

    def mlstm_mix(self, gq, gk, gv, gg, l, xh, gR_in, xhR, half_hook=None):
        P, A = self.P, self.A
        A.reset(self.low_mark)
        NT, NCH, L = 2048, 16, 128
        SC = 128 ** -0.5
        ps, psR = self.ps, self.psR
        qk = A.alloc("qk", [128, 4, 2, T], BF16)
        qkR = [Res(f"qk{hl}") for hl in range(4)]
        ktm = A.alloc("ktm", [128, 8, 512], BF16)
        vtm = A.alloc("vtm", [128, 8, 1024], BF16)
        kvR = Res("kv")
        hg = self.hgt[:, l * 8:(l + 1) * 8]
        hgR = self.hgR

        def load_half(hf):
            for hl in range(4):
                for j in range(2):
                    self.idma(qk[:, hl, j, :], gq, 16 + hf * 8 + hl * 2 + j, f"qk{hl}", reads=[gR_in], writes=[qkR[hl]])
            for tt in range(8):
                self.idma(ktm[:, tt, :], gk, hf * 8 + tt, "kvk", reads=[gR_in], writes=[kvR])
                self.idma(vtm[:, tt, :], gv, 16 + hf * 8 + tt, "kvv", reads=[gR_in], writes=[kvR])

        load_half(0)
        IG = A.alloc("IG", [4, NT], F32)
        LF = A.alloc("LF", [4, NT], F32)
        NM = A.alloc("NM", [4, NT], F32)
        R2 = A.alloc("R2", [4, NT], F32)
        MS = A.alloc("MS", [4, 32], F32)
        ON4 = A.alloc("ON4", [4, 128], F32)
        tA = A.alloc("tA", [4, T], F32)
        tB = A.alloc("tB", [4, T], F32)
        gR = Res("gates")
        G = [gR, self.c32R]
        P.dma("sp", NM[:], self.mc32[:, 512:2560], "mg", writes=[gR])
        P.dma("sp", R2[:], self.mc32[:, 2560:4608], "mg", writes=[gR])
        for hf in range(2):
            for gi, dst in enumerate((IG, LF)):
                P.dma("sp", tA[:], gg[hf * 16 + gi * 4:hf * 16 + gi * 4 + 4, :], "mg", reads=[gR_in], writes=[gR])
                P.dma("sp", tB[:], gg[hf * 16 + 8 + gi * 4:hf * 16 + 8 + gi * 4 + 4, :], "mg", reads=[gR_in], writes=[gR])
                P.op("dve", lambda e: e.tensor_scalar(out=tB[:], in0=tB[:], scalar1=self.sel[0:4, 1:2], scalar2=None, op0=ALU.mult),
                     reads=G + [self.selR], writes=[gR])
                P.op("dve", lambda e, dst=dst, hf=hf: e.scalar_tensor_tensor(out=dst[:, hf * T:(hf + 1) * T], in0=tA[:], scalar=self.sel[0:4, 0:1],
                                                                             in1=tB[:], op0=ALU.mult, op1=ALU.add),
                     reads=G + [self.selR], writes=[gR])
        c3 = lambda t: t[:].rearrange("p (c t) -> p c t", t=L)
        P.op("dve", lambda e: e.memset(ON4[:], 1.0), reads=G, writes=[gR])
        P.op("dve", lambda e: e.tensor_tensor_scan(out=LF[:], data0=NM[:], data1=LF[:], initial=0.0, op0=ALU.mult, op1=ALU.add),
             reads=G, writes=[gR])
        P.op("dve", lambda e: e.tensor_tensor(out=IG[:], in0=IG[:], in1=LF[:], op=ALU.subtract), reads=G, writes=[gR])
        P.op("dve", lambda e: e.tensor_tensor_scan(out=NM[:], data0=R2[:], data1=IG[:], initial=0.0, op0=ALU.add, op1=ALU.max),
             reads=G, writes=[gR])
        P.op("dve", lambda e: e.memset(MS[:], 0.0), reads=G, writes=[gR])
        P.op("dve", lambda e: e.tensor_tensor_scan(out=MS[:, 1:17], data0=c3(NM)[:, :, L - 1], data1=c3(LF)[:, :, L - 1], initial=0.0,
                                                   op0=ALU.max, op1=ALU.add), reads=G, writes=[gR])
        P.op("dve", lambda e: e.tensor_tensor(out=c3(NM), in0=c3(NM), in1=MS[:, 0:16].unsqueeze(2).to_broadcast([4, NCH, L]), op=ALU.max),
             reads=G, writes=[gR])
        P.op("dve", lambda e: e.tensor_scalar(out=NM[:], in0=NM[:], scalar1=-1.0, scalar2=None, op0=ALU.mult), reads=G, writes=[gR])
        nmm = [A.alloc("nmm", [4, 4, L], F32) for _ in range(2)]
        nmd = [A.alloc("nmd", [4, 4, L], F32) for _ in range(2)]
        nmt = [A.alloc("nmt", [4, 4, L], F32) for _ in range(2)]
        tmc = [A.alloc("tmc", [4, L], F32) for _ in range(2)]
        nmR = [Res("nm0"), Res("nm1")]
        ED = [A.alloc("ED", [128, 512], F32) for _ in range(2)]
        DEC = [A.alloc("DEC", [128, 512], F32) for _ in range(2)]
        EMT = [A.alloc("EMT", [128, 512], F32) for _ in range(2)]
        eR = [[Res(f"e{k}_{b}") for b in range(2)] for k in range(3)]
        swt = [A.alloc("swt", [128, L], BF16) for _ in range(2)]
        swR = [Res("sw0"), Res("sw1")]
        qd = [A.alloc("qd", [128, L], BF16) for _ in range(2)]
        qdR = [Res("qd0"), Res("qd1")]
        wv = [A.alloc("wv", [128, 384], BF16) for _ in range(2)]
        wvR = [Res("wv0"), Res("wv1")]
        rr = [A.alloc("rr", [128, L], F32) for _ in range(2)]
        rrR = [Res("rr0"), Res("rr1")]
        hs = [A.alloc("hs", [128, 2, L], F32) for _ in range(2)]
        hsR = [Res("hs0"), Res("hs1")]
        sq = [A.alloc("hsq", [128, 2, L], F32) for _ in range(2)]
        sqR = [Res("hsq0"), Res("hsq1")]
        ho = [A.alloc("ho", [128, 2, L], F32) for _ in range(3)]
        hoR = [Res(f"ho{i}") for i in range(3)]
        C32 = [A.alloc("C32", [128, 384], F32) for _ in range(4)]
        C16 = [A.alloc("C16", [128, 384], BF16) for _ in range(4)]
        cR = [Res(f"C32_{h}") for h in range(4)]
        c16R = [Res(f"C16_{h}") for h in range(4)]
        ind3 = self.ind4.rearrange("p (h t) -> p h t", t=L)
        f2 = lambda t: t[:].rearrange("p h t -> p (h t)")
        n = 0
        for c in range(NCH):
            cb = c % 2
            sl = slice(c * L, (c + 1) * L)
            hf = c // 8
            cl = c % 8
            lsl = slice(cl * L, (cl + 1) * L)
            if c == 8:
                load_half(1)
            def prep(c):
                cb = c % 2
                sl = slice(c * L, (c + 1) * L)
                bc = lambda t: t[:, sl].unsqueeze(1).to_broadcast([4, 4, L])
                P.op("dve", lambda e: e.tensor_tensor(out=nmm[cb][:], in0=ind3, in1=bc(NM), op=ALU.mult), reads=G + [nmR[cb]], writes=[nmR[cb]])
                P.op("dve", lambda e: e.scalar_tensor_tensor(out=nmd[cb][:], in0=bc(NM), scalar=MS[:, c:c + 1], in1=ind3, op0=ALU.add, op1=ALU.mult),
                     reads=G + [nmR[cb]], writes=[nmR[cb]])
                P.op("dve", lambda e: e.tensor_tensor(out=tmc[cb][:], in0=NM[:, sl], in1=LF[:, sl], op=ALU.subtract), reads=G + [nmR[cb]], writes=[nmR[cb]])
                P.op("dve", lambda e: e.tensor_tensor(out=nmt[cb][:], in0=ind3, in1=tmc[cb][:].unsqueeze(1).to_broadcast([4, 4, L]), op=ALU.mult),
                     reads=G + [nmR[cb]], writes=[nmR[cb]])
                P.op("pe", lambda e: e.matmul(ps[0][:], lhsT=self.ident, rhs=self.negmask, start=True, stop=False), reads=[self.c16R], writes=[psR[0]])
                P.op("pe", lambda e: e.matmul(ps[0][:], lhsT=IG[:, sl], rhs=self.ind4, start=False, stop=False), reads=G, writes=[psR[0]])
                P.op("pe", lambda e: e.matmul(ps[0][:], lhsT=ON4[:], rhs=f2(nmm[cb]), start=False, stop=True), reads=G + [nmR[cb]], writes=[psR[0]])
                P.op("pe", lambda e: e.matmul(ps[1][:], lhsT=ON4[:], rhs=f2(nmd[cb]), start=True, stop=True), reads=G + [nmR[cb]], writes=[psR[1]])
                P.op("pe", lambda e: e.matmul(ps[2][:], lhsT=ON4[:], rhs=f2(nmt[cb]), start=True, stop=True), reads=G + [nmR[cb]], writes=[psR[2]])
                for k, (dst, pi) in enumerate(((ED, 0), (DEC, 1), (EMT, 2))):
                    P.op("act", lambda e, dst=dst, pi=pi: e.activation(out=dst[cb][:], in_=ps[pi][:], func=AF.Exp),
                         reads=[psR[pi]], writes=[eR[k][cb]])

            if c == 0:
                prep(0)
            def st1(hl, c=c, cb=cb, lsl=lsl, cl=cl, hf=hf):
                b = hl % 2
                hsl = slice(hl * L, (hl + 1) * L)
                P.op("pe", lambda e: e.matmul(ps[3][:, b * L:(b + 1) * L], lhsT=qk[:, hl, 1, lsl], rhs=qk[:, hl, 0, lsl], start=True, stop=True),
                     reads=[qkR[hl]], writes=[psR[3]])

            def st2(hl, c=c, cb=cb, lsl=lsl, cl=cl, hf=hf):
                b = hl % 2
                hsl = slice(hl * L, (hl + 1) * L)
                P.op("dve", lambda e: e.scalar_tensor_tensor(out=swt[b][:], in0=ps[3][:, b * L:(b + 1) * L], scalar=SC, in1=ED[cb][:, hsl],
                                                             op0=ALU.mult, op1=ALU.mult),
                     reads=[psR[3], eR[0][cb]], writes=[swR[b]])
                if c > 0:
                    P.op("pool", lambda e: e.tensor_tensor(out=qd[b][:], in0=qk[:, hl, 0, lsl], in1=DEC[cb][:, hsl], op=ALU.mult),
                         reads=[qkR[hl], eR[1][cb]], writes=[qdR[b]])

            def st3(hl, c=c, cb=cb, lsl=lsl, cl=cl, hf=hf):
                b = hl % 2
                pn = 4 + b
                for vc in range(3):
                    lh = vtm[:, cl, hl * 256 + vc * 128: hl * 256 + (vc + 1) * 128] if vc < 2 else self.on16[:]
                    P.op("pe", lambda e, vc=vc, lh=lh: e.matmul(ps[pn][:, vc * L:(vc + 1) * L], lhsT=lh, rhs=swt[b][:], start=True, stop=(c == 0)),
                         reads=[kvR, swR[b], self.c16R], writes=[psR[pn]])
                    if c > 0:
                        P.op("pe", lambda e, vc=vc: e.matmul(ps[pn][:, vc * L:(vc + 1) * L], lhsT=C16[hl][:, vc * 128:(vc + 1) * 128], rhs=qd[b][:],
                                                             start=False, stop=True),
                             reads=[c16R[hl], qdR[b]], writes=[psR[pn]])

            def st4(hl, c=c, cb=cb, lsl=lsl, cl=cl, hf=hf):
                b = hl % 2
                pn = 4 + b
                hsl = slice(hl * L, (hl + 1) * L)
                P.op("act", lambda e: e.activation(out=rr[b][:], in_=ps[pn][:, 2 * L:3 * L], func=AF.Abs), reads=[psR[pn]], writes=[rrR[b]])
                P.op("dve", lambda e: e.tensor_tensor(out=rr[b][:], in0=rr[b][:], in1=EMT[cb][:, hsl], op=ALU.max),
                     reads=[eR[2][cb], rrR[b]], writes=[rrR[b]])
                P.op("act", lambda e: e.activation(out=rr[b][:], in_=rr[b][:], func=AF.Ln), reads=[rrR[b]], writes=[rrR[b]])
                P.op("act", lambda e: e.activation(out=rr[b][:], in_=rr[b][:], func=AF.Exp, scale=-1.0), reads=[rrR[b]], writes=[rrR[b]])
                P.op("dve", lambda e: e.tensor_tensor(out=hs[b][:], in0=ps[pn][:, 0:2 * L].rearrange("p (v t) -> p v t", t=L),
                                                      in1=rr[b][:].unsqueeze(1).to_broadcast([128, 2, L]), op=ALU.mult),
                     reads=[psR[pn], rrR[b]], writes=[hsR[b]])
                P.op("act", lambda e: e.activation(out=sq[b][:], in_=hs[b][:], func=AF.Square), reads=[hsR[b]], writes=[sqR[b]])

            def st5(hl, c=c, cb=cb, lsl=lsl, cl=cl, hf=hf):
                b = hl % 2
                for vc in range(2):
                    P.op("pe", lambda e, vc=vc: e.matmul(ps[3][:, (2 + b) * L:(3 + b) * L], lhsT=self.ones32[:], rhs=sq[b][:, vc, :], start=(vc == 0), stop=(vc == 1)),
                         reads=[sqR[b], self.onesR], writes=[psR[3]])

            def st6(hl, c=c, cb=cb, lsl=lsl, cl=cl, hf=hf):
                b = hl % 2
                P.op("act", lambda e: e.activation(out=rr[b][:], in_=ps[3][:, (2 + b) * L:(3 + b) * L], func=AF.Ln, scale=1.0 / 256.0, bias=self.eps_ap()),
                     reads=[psR[3], self.cstR, rrR[b]], writes=[rrR[b]])
                P.op("act", lambda e: e.activation(out=rr[b][:], in_=rr[b][:], func=AF.Exp, scale=-0.5), reads=[rrR[b]], writes=[rrR[b]])
                o = (c * 4 + hl) % 3
                for vc in range(2):
                    P.op("dve", lambda e, vc=vc: e.scalar_tensor_tensor(out=ho[o][:, vc, :], in0=hs[b][:, vc, :], scalar=hg[:, hl * 2 + vc:hl * 2 + vc + 1],
                                                                        in1=rr[b][:], op0=ALU.mult, op1=ALU.mult),
                         reads=[hsR[b], rrR[b], hgR], writes=[hoR[o]])
                P.dma("sp", xh[hf, hl * 256:(hl + 1) * 256, cl * L:(cl + 1) * L].rearrange("(v p) t -> p v t", p=128), ho[o][:], f"ho{o}",
                      reads=[hoR[o], xhR[hf]])

            def st7(hl, c=c, cb=cb, lsl=lsl, cl=cl, hf=hf):
                b = hl % 2
                pc = 6 if b == 0 else 7
                if c >= NCH - 1:
                    return
                wcol = ED[cb][:, hl * L + L - 1: hl * L + L]
                P.op("act", lambda e: e.activation(out=wv[b][:, 0:256], in_=vtm[:, cl, hl * 256:(hl + 1) * 256], func=AF.Copy, scale=wcol),
                     reads=[kvR, eR[0][cb]], writes=[wvR[b]])
                P.op("act", lambda e: e.activation(out=wv[b][:, 256:384], in_=self.on16[:], func=AF.Copy, scale=wcol),
                     reads=[self.c16R, eR[0][cb], wvR[b]], writes=[wvR[b]])
                P.op("pe", lambda e: e.matmul(ps[pc][:, 0:384], lhsT=ktm[:, cl, hl * 128:(hl + 1) * 128], rhs=wv[b][:], start=True, stop=True),
                     reads=[kvR, wvR[b]], writes=[psR[pc]])

            def st8(hl, c=c, cb=cb, lsl=lsl, cl=cl, hf=hf):
                b = hl % 2
                pc = 6 if b == 0 else 7
                if c >= NCH - 1:
                    return
                dcol = DEC[cb][:, hl * L + L - 1: hl * L + L]
                if c == 0:
                    P.op("dve", lambda e: e.tensor_copy(out=C32[hl][:], in_=ps[pc][:, 0:384]), reads=[psR[pc]], writes=[cR[hl]])
                else:
                    P.op("dve", lambda e: e.scalar_tensor_tensor(out=C32[hl][:], in0=C32[hl][:], scalar=dcol, in1=ps[pc][:, 0:384],
                                                                 op0=ALU.mult, op1=ALU.add),
                         reads=[psR[pc], cR[hl], eR[1][cb]], writes=[cR[hl]])
                P.op("pool", lambda e: e.tensor_scalar(out=C16[hl][:], in0=C32[hl][:], scalar1=SC, scalar2=0.0, op0=ALU.mult, op1=ALU.add),
                     reads=[cR[hl]], writes=[c16R[hl]])

            for hp in range(2):
                for st in (st1, st2, st3, st7, st4, st5, st8, st6):
                    for hl in (2 * hp, 2 * hp + 1):
                        st(hl)
                if hp == 0 and c + 1 < NCH:
                    prep(c + 1)
            if c == 7 and half_hook:
                half_hook(0)
        if half_hook:
            half_hook(1)
        self.release(self.low_mark)
        A.reset(self.phase_mark)

    def post_mix(self, gh, ghR, ogs, ogsR, w_out):
        P, A = self.P, self.A
        m = A.mark()
        hc = [A.alloc("hc", [128, T], F32) for _ in range(2)]
        oc = [A.alloc("oc", [128, T], F32) for _ in range(2)]
        hcR = [Res("hc0"), Res("hc1")]
        ocR = [Res("oc0"), Res("oc1")]
        for kc in range(KC):
            b = kc % 2
            self.idma(hc[b][:], gh, 32 + kc, f"hc{b}", reads=[ghR], writes=[hcR[b]])
            if ogs is not None:
                P.dma("sp", oc[b][:], ogs[kc * 128:(kc + 1) * 128, :], f"oc{b}", reads=[ogsR[kc]], writes=[ocR[b]])
                P.op("dve", lambda e, kc=kc, b=b: e.tensor_tensor(out=self.hT[:, kc, :], in0=hc[b][:], in1=oc[b][:], op=ALU.mult),
                     reads=[hcR[b], ocR[b]], writes=[self.hR[kc]])
            else:
                P.op("dve", lambda e, kc=kc, b=b: e.tensor_copy(out=self.hT[:, kc, :], in_=hc[b][:]),
                     reads=[hcR[b]], writes=[self.hR[kc]])
        self.release(m)

        def ev(u, h, ps, psR):
            P.op("dve", lambda e: e.tensor_tensor(out=self.xT[:, u, h * 512:(h + 1) * 512], in0=ps[:], in1=self.xT[:, u, h * 512:(h + 1) * 512],
                                                  op=ALU.add), reads=[psR, self.xR[u][h]], writes=[self.xR[u][h]])

        self.fm_proj(w_out, 0, D, ev, "wo")

    def load_x(self, xT_d):
        xv = xT_d.rearrange("(kc p) t -> p kc t", p=128)
        for kc in range(KC):
            self.P.dma("sp", self.xT[:, kc, :], xv[:, kc, :], f"xin{kc}", writes=self.xR[kc])

    def store_x(self, xT_d):
        xv = xT_d.rearrange("(kc p) t -> p kc t", p=128)
        for kc in range(KC):
            self.P.dma("sp", xv[:, kc, :], self.xT[:, kc, :], "xout", reads=self.xR[kc])

    def load_cst(self, cst_d):
        self.P.dma("sp", self.cst[:], cst_d, "cst", writes=[self.cstR])

    def sb_fmproj(self, w, col0, nheads, dst, xR, head0=0):
        P, A = self.P, self.A
        m = A.mark()
        ob = [A.alloc("sob", [128, T], BF16) for _ in range(3)]
        obR = [Res(f"sob{i}") for i in range(3)]

        def ev(u, h, ps, psR):
            b = u % 3
            P.op("act", lambda e: e.activation(out=ob[b][:, h * 512:(h + 1) * 512], in_=ps[:], func=AF.Copy),
                 reads=[psR], writes=[obR[b]])
            if h == 1:
                head = head0 + u
                P.dma("sp", dst[head // 8, head % 8], ob[b][:], f"sob{b}", reads=[obR[b], xR])

        self.fm_proj(w, col0, nheads * 128, ev, "sbfm")
        self.release(m)

    def sb_vproj(self, w, col0, dst, xR):
        P, A = self.P, self.A
        m = A.mark()
        tb = [A.alloc("stb", [128, 512], BF16) for _ in range(3)]
        tbR = [Res(f"stb{i}") for i in range(3)]
        cnt = [0]

        def ev(cb, tt, ps, psR):
            b = cnt[0] % 3
            cnt[0] += 1
            P.op("act", lambda e: e.activation(out=tb[b][:], in_=ps[:], func=AF.Copy), reads=[psR], writes=[tbR[b]])
            P.dma("sp", dst[cb // 2, tt * 128:(tt + 1) * 128, (cb % 2) * 512:(cb % 2 + 1) * 512], tb[b][:], f"stb{b}",
                  reads=[tbR[b], xR])

        self.tm_proj(w, col0, 2048, ev, "sbv")
        self.release(m)

    def sb_mix(self, gqq, gkk, gvv, xo, gR_in, xoR, half_hook=None):
        P, A = self.P, self.A
        A.reset(self.low_mark)
        NT, L = 2048, 128
        SC = 128 ** -0.5
        ps, psR = self.ps, self.psR
        qT = A.alloc("sqT", [128, 8, NT], BF16)
        kT = A.alloc("skT", [128, 8, NT], BF16)
        vtm = A.alloc("svtm", [128, 16, 1024], BF16)
        qR = [Res(f"sq{h}") for h in range(8)]
        kR = [Res(f"sk{h}") for h in range(8)]
        vR = [Res("sv0"), Res("sv1")]
        gRq, gRkv = gR_in
        for hf in range(2):
            for tt in range(8):
                self.idma(vtm[:, hf * 8 + tt, :], gvv, 16 + hf * 8 + tt, f"sv{hf}", reads=[gRkv], writes=[vR[hf]])
        for hl in range(8):
            for hf in range(2):
                self.idma(kT[:, hl, hf * 1024:(hf + 1) * 1024], gkk, 16 + hf * 8 + hl, f"sk{hl}", reads=[gRkv], writes=[kR[hl]])
        for hl in range(8):
            for hf in range(2):
                self.idma(qT[:, hl, hf * 1024:(hf + 1) * 1024], gqq, 16 + hf * 8 + hl, f"sq{hl}", reads=[gRq], writes=[qR[hl]])
        E = [A.alloc("sE", [128, 512], F32) for _ in range(2)]
        SP = [A.alloc("sSP", [128, 512], F32) for _ in range(2)]
        LB = [A.alloc("sLB", [128, 512], BF16) for _ in range(3)]
        T1 = [A.alloc("sT1", [128, 512], F32) for _ in range(5)]
        AT = [A.alloc("sAT", [128, 512], BF16) for _ in range(2)]
        OB = [A.alloc("sOB", [128, 512], F32) for _ in range(2)]
        eR = [Res(f"sE{i}") for i in range(2)]
        spR = [Res(f"sSP{i}") for i in range(2)]
        lbR = [Res(f"sLB{i}") for i in range(3)]
        t1R = [Res(f"sT1{i}") for i in range(5)]
        atR = [Res("sAT0"), Res("sAT1")]
        obR = [Res("sOB0"), Res("sOB1")]
        tiles = []
        grp = 0
        for G in range(4):
            for hl in range(8):
                kbs = list(range(4 * G + 3, -1, -1))
                for kb in kbs:
                    tiles.append((hl, G, kb, grp % 2, kb == kbs[0], kb == 0))
                grp += 1
        nt = len(tiles)
        last_d0 = max(i for i, t_ in enumerate(tiles) if t_[1] == 1)

        class Gm:
            pass

        def geom(k):
            g = Gm()
            g.hl, g.G, g.kb, g.g2, g.first, g.last = tiles[k]
            g.c0 = max(g.kb, 4 * g.G) - 4 * g.G
            g.diag = g.kb >= 4 * g.G
            r0 = (g.c0 + 1) * L if g.diag else 0
            g.cs = slice(g.c0 * L, 512)
            g.ds = slice(g.c0 * L, (g.c0 + 1) * L)
            g.rs = slice(r0, 512)
            g.has_rt = r0 < 512
            g.pz, g.pa = k % 2, 2 + k % 2
            g.prt = 4 if g.g2 == 0 else 7
            g.po = 5 + g.g2
            g.q0 = (4 * g.G + g.c0) * L
            g.e, g.sp, g.lb, g.t1, g.at = k % 2, k % 2, k % 3, k % 5, k % 2
            return g

        def pe_z(k):
            g = geom(k)
            P.op("pe", lambda e: e.matmul(ps[g.pz][:, g.cs], lhsT=kT[:, g.hl, g.kb * L:(g.kb + 1) * L], rhs=qT[:, g.hl, g.q0:(4 * g.G + 4) * L], start=True, stop=True),
                 reads=[kR[g.hl], qR[g.hl]], writes=[psR[g.pz]])

        def act_a(k):
            g = geom(k)
            P.op("act", lambda e: e.activation(out=E[g.e][:, g.cs], in_=ps[g.pz][:, g.cs], func=AF.Exp, scale=SC), reads=[psR[g.pz]], writes=[eR[g.e]])
            P.op("act", lambda e: e.activation(out=SP[g.sp][:, g.cs], in_=E[g.e][:, g.cs], func=AF.Ln, scale=1.0, bias=self.one_ap()),
                 reads=[eR[g.e], self.onesR], writes=[spR[g.sp]])

        def dve_t1(k):
            g = geom(k)
            P.op("dve", lambda e: e.scalar_tensor_tensor(out=T1[g.t1][:, g.cs], in0=ps[g.pz][:, g.cs], scalar=SC, in1=SP[g.sp][:, g.cs], op0=ALU.mult, op1=ALU.subtract),
                 reads=[psR[g.pz], spR[g.sp]], writes=[t1R[g.t1]])

        def pool_lb(k):
            g = geom(k)
            P.op("pool", lambda e: e.tensor_scalar(out=LB[g.lb][:, g.cs], in0=SP[g.sp][:, g.cs], scalar1=-1.0, scalar2=0.0, op0=ALU.mult, op1=ALU.add),
                 reads=[spR[g.sp]], writes=[lbR[g.lb]])
            if g.diag:
                P.op("pool", lambda e: e.tensor_tensor(out=LB[g.lb][:, g.ds], in0=LB[g.lb][:, g.ds], in1=self.maskd, op=ALU.mult),
                     reads=[lbR[g.lb], self.c16R], writes=[lbR[g.lb]])

        def pe_u(k):
            g = geom(k)
            P.op("pe", lambda e: e.matmul(ps[g.prt][:, g.cs], lhsT=self.utri, rhs=LB[g.lb][:, g.cs], start=g.first, stop=False),
                 reads=[lbR[g.lb], self.c16R], writes=[psR[g.prt]])

        def dve_x(k):
            g = geom(k)
            P.op("dve", lambda e: e.tensor_tensor(out=T1[g.t1][:, g.cs], in0=ps[g.prt][:, g.cs], in1=T1[g.t1][:, g.cs], op=ALU.add),
                 reads=[psR[g.prt], t1R[g.t1]], writes=[t1R[g.t1]])

        def pe_ones(k):
            g = geom(k)
            if not g.last:
                P.op("pe", lambda e: e.matmul(ps[g.prt][:, g.cs], lhsT=self.ltri, rhs=LB[g.lb][:, g.cs], start=False, stop=(g.kb == 1)),
                     reads=[lbR[g.lb], self.c16R], writes=[psR[g.prt]])

        def act_b(k):
            g = geom(k)
            P.op("act", lambda e: e.activation(out=AT[g.at][:, g.cs], in_=T1[g.t1][:, g.cs], func=AF.Exp), reads=[t1R[g.t1]], writes=[atR[g.at]])

        def pool_at(k):
            g = geom(k)
            if g.diag:
                P.op("pool", lambda e: e.tensor_tensor(out=AT[g.at][:, g.ds], in0=AT[g.at][:, g.ds], in1=self.maskd, op=ALU.mult),
                     reads=[atR[g.at], self.c16R], writes=[atR[g.at]])

        def pe_av(k):
            g = geom(k)
            P.op("pe", lambda e: e.matmul(ps[g.po][:, g.cs], lhsT=vtm[:, g.kb, g.hl * L:(g.hl + 1) * L], rhs=AT[g.at][:, g.cs], start=g.first, stop=g.last),
                 reads=[vR[g.kb // 8], atR[g.at]], writes=[psR[g.po]])
            if g.last:
                P.op("dve", lambda e: e.tensor_copy(out=OB[g.g2][:], in_=ps[g.po][:]), reads=[psR[g.po]], writes=[obR[g.g2]])
                P.dma("sp", xo[g.G // 2, g.hl * L:(g.hl + 1) * L, (g.G % 2) * 512:(g.G % 2 + 1) * 512], OB[g.g2][:], f"sOB{g.g2}", reads=[obR[g.g2], xoR[g.G // 2]])
                if half_hook and k == last_d0:
                    half_hook(0)

        ok = lambda k: 0 <= k < nt
        for j in range(nt + 7):
            if ok(j - 6):
                pe_av(j - 6)
            if ok(j):
                pe_z(j)
            if j == 3:
                pe_u(0)
            if ok(j - 4):
                dve_x(j - 4)
                pe_ones(j - 4)
                if ok(j - 3):
                    pe_u(j - 3)
            if ok(j - 5):
                act_b(j - 5)
                pool_at(j - 5)
            if ok(j - 1):
                act_a(j - 1)
                dve_t1(j - 1)
            if ok(j - 2):
                pool_lb(j - 2)
        if half_hook:
            half_hook(1)
        self.release(self.low_mark)
        A.reset(self.phase_mark)


def _mk(nc):
    def I(name, shape, dt=F32):
        return nc.dram_tensor(name, list(shape), dt, kind="ExternalInput").ap()

    def O(name, shape, dt=F32):
        return nc.dram_tensor(name, list(shape), dt, kind="ExternalOutput").ap()

    return I, O


A_IN = 6160
XQ = (2, 4, 2, 128, T)
XK = (2, T, 512)
XV = (2, T, 1024)
XG = (2, 2, 4, T)
XH = (2, 1024, T)
SQ = (2, 8, 128, T)
SV = (2, T, 1024)


def build_fused(dbg=None):
    nc = bass.Bass("TRN2", target_bir_lowering=False)
    I, O = _mk(nc)

    def N(name, shape, dt=F32):
        return nc.dram_tensor(name, list(shape), dt, kind="Internal").ap()

    B = Builder(nc, fused=True)
    P = B.P
    cst = I("cst", [128, NCST])
    idx = I("idx", [128, 48], mybir.dt.uint32)
    sel = I("sel", [128, 2])
    hng = I("hng", [128, 16])
    mc16, mc32 = I("mc16", [128, 1024], BF16), I("mc32", [4, 4608])
    xT = I("xT", [D, T])
    yT = O("yT", [D, T])
    if dbg is None:
        f1i, f1o = I("ffn1_w_in", [4, D, 2 * DFF]), I("ffn1_w_out", [4, DFF, D])
        f2i, f2o = I("ffn2_w_in", [4, D, 2 * DFF]), I("ffn2_w_out", [4, DFF, D])
        kvw = I("kv_w", [D, 2 * D])
        bwq, bwo = I("b_w_q", [2, D, D]), I("b_w_out", [2, D, D])
    awi, awo = I("a_w_in", [2, D, A_IN]), I("a_w_out", [2, D, D])
    xq, gq = N("xq", [2048, T], BF16), N("gq", [4096, T], BF16)
    xk, gk = N("xk", [2048, 512], BF16), N("gk", [4096, 512], BF16)
    xv, gv = N("xv", [2048, 1024], BF16), N("gv", [4096, 1024], BF16)
    xg, gg = N("xg", [16, T]), N("gg", [32, T])
    xh, gh = N("xh", [2048, T]), N("gh", [4096, T])
    xkk, gkk = N("xkk", [2048, T], BF16), N("gkk", [4096, T], BF16)
    ogs = N("ogs", [D, T])
    xR = {k: Res("xbuf_" + k) for k in ("q", "k", "v", "g", "kk")}
    gR, ghR = Res("gbuf"), Res("ghbuf")
    gRq, gRkv = Res("gbuf_q"), Res("gbuf_kv")
    xhR = [Res("xh0"), Res("xh1")]
    ogsR = [Res(f"ogs{i}") for i in range(KC)]
    xq5 = xq.rearrange("(d h j p) t -> d h j p t", d=2, h=4, j=2)
    xk3 = xk.rearrange("(d t) c -> d t c", d=2)
    xv3 = xv.rearrange("(d t) c -> d t c", d=2)
    xg4 = xg.rearrange("(d g h) t -> d g h t", d=2, g=2)
    xh3 = xh.rearrange("(d r) t -> d r t", d=2)
    xqq4 = xq.rearrange("(d h p) t -> d h p t", d=2, h=8)
    xkk4 = xkk.rearrange("(d h p) t -> d h p t", d=2, h=8)

    def xh_hook(hf):
        for c in (2 * hf, 2 * hf + 1):
            P.coll("AllGather", xh[c * 512:(c + 1) * 512, :], gh[c * 1024:(c + 1) * 1024, :], "cc", writes=[xhR[hf], ghR])

    def proj_hook(which):
        a_, g_, n_ = {"q": (xq, gq, 1024), "k": (xk, gk, 2048), "v": (xv, gv, 1024)}[which]
        P.coll_rows(a_, g_, n_, writes=[xR[which], gR])

    B.load_cst(cst)
    B.load_mix_consts(mc16, mc32, idx, sel, hng)
    B.load_x(xT)
    for l in range(2):
        B.rmsnorm(l)
        B.ffn(f1i[l], f1o[l], "f1")
        B.rmsnorm(4 + l)
        B.mlstm_proj(awi[l], l, xq5, xk3, xv3, xg4, ogs, xR, ogsR, hook=proj_hook)
        P.coll_rows(xg, gg, 16, writes=[xR["g"], gR])
        B.mlstm_mix(gq, gk, gv, gg, l, xh3, gR, xhR, half_hook=xh_hook)
        B.post_mix(gh, ghR, ogs, ogsR, awo[l])
        B.rmsnorm(8 + l)
        B.ffn(f2i[l], f2o[l], "f2")
    B.rmsnorm(12)
    B.sb_fmproj(kvw, 0, 16, xkk4, xR["kk"])
    P.coll_rows(xkk, gkk, 1024, writes=[xR["kk"], gRkv])
    B.sb_vproj(kvw, 2048, xv3, xR["v"])
    P.coll_rows(xv, gv, 1024, reads=[gR], writes=[xR["v"], gRkv])
    for j in range(2):
        l = 2 + j
        B.rmsnorm(l)
        B.ffn(f1i[l], f1o[l], "f1")
        B.rmsnorm(4 + l)
        B.sb_fmproj(bwq[j], 0, 16, xqq4, xR["q"])
        P.coll_rows(xq, gq, 1024, reads=[gR], writes=[xR["q"], gRq])
        B.sb_mix(gq, gkk, gv, xh3, (gRq, gRkv), xhR, half_hook=xh_hook)
        B.post_mix(gh, ghR, None, None, bwo[j])
        B.rmsnorm(8 + l)
        B.ffn(f2i[l], f2o[l], "f2")
    B.final_norm(13)
    B.store_x(yT)
    P.emit()
    return nc


def make_idx(r):
    idx = np.zeros((128, 48), np.uint32)
    p = np.arange(128)
    for t, n in enumerate((2048, 1024, 512)):
        for hf in range(2):
            for m_ in range(8):
                R = r * 1024 + m_ * 128 + p
                idx[:, t * 16 + hf * 8 + m_] = (R // n) * 2 * n + hf * n + (R % n)
    return idx


def mix_consts():
    import ml_dtypes
    c16 = np.zeros((128, 1024), np.float32)
    c16[:, 0:128] = np.eye(128)
    s = np.arange(128)[:, None]
    t = np.arange(128)[None, :]
    nm = np.where(s > t, -30000.0, 0.0)
    c16[:, 128:640] = np.tile(nm, (1, 4))
    c16[:, 640:768] = (s < t)
    c16[:, 768:896] = (s > t)
    c16[:, 896:1024] = (s <= t)
    c32 = np.zeros((4, 4608), np.float32)
    for h in range(4):
        c32[h, h * 128:(h + 1) * 128] = 1.0
    r1 = np.ones(2048, np.float32)
    r1[::128] = 0.0
    r2 = np.zeros(2048, np.float32)
    r2[::128] = -1e30
    c32[:, 512:2560] = r1
    c32[:, 2560:4608] = r2
    return c16.astype(ml_dtypes.bfloat16), c32


def pack_cst(inp):
    c = np.zeros((128, NCST), np.float32)
    vecs = [inp["ffn1_norm"][l] for l in range(4)] + [inp["mix_norm"][l] for l in range(4)] + \
           [inp["ffn2_norm"][l] for l in range(4)] + [inp["kv_norm"], inp["final_norm"]] + \
           [inp["a_head_norm"][l] for l in range(2)]
    for i, v in enumerate(vecs):
        c[:, i * 16:(i + 1) * 16] = np.asarray(v, np.float32).reshape(16, 128).T
    for l in range(2):
        c[0:8, 256 + 2 * l] = inp["a_b_gate"][l][0:8]
        c[0:8, 257 + 2 * l] = inp["a_b_gate"][l][8:16]
    c[:, EPSC] = EPS
    return c


_NC = []


def kernel(**inp):
    inp = {k: np.asarray(v) for k, v in inp.items()}
    NCORE = 8
    if not _NC:
        _NC.append(build_fused())
    nc = _NC[0]
    cst = pack_cst(inp)
    mc16, mc32 = mix_consts()
    x = inp["x"]
    ca = np.ascontiguousarray
    wnames = ["ffn1_w_in", "ffn1_w_out", "ffn2_w_in", "ffn2_w_out", "a_w_in", "a_w_out", "kv_w", "b_w_q", "b_w_out"]
    w = {k: ca(inp[k], dtype=np.float32) for k in wnames}
    maps = []
    p = np.arange(128, dtype=np.uint32)[:, None]
    for c in range(NCORE):
        b, r = c // 2, c % 2
        idx = make_idx(r)
        sel = np.zeros((128, 2), np.float32)
        sel[:, 0] = 1.0 - r
        sel[:, 1] = float(r)
        hng = np.zeros((128, 16), np.float32)
        for l in range(2):
            hng[:, l * 8:(l + 1) * 8] = inp["a_head_norm"][l].reshape(8, 2, 128)[4 * r:4 * r + 4].reshape(8, 128).T
        mp = {"cst": cst, "idx": idx, "sel": sel, "hng": hng, "mc16": mc16, "mc32": mc32,
              "xT": ca(x[b, r * T:(r + 1) * T].T)}
        mp.update(w)
        maps.append(mp)
    res = run_bass_kernel_spmd(nc, maps, core_ids=list(range(NCORE)))
    out = np.empty((4, 2048, D), np.float32)
    for c in range(NCORE):
        b, r = c // 2, c % 2
        out[b, r * T:(r + 1) * T, :] = res.results[c]["yT"].T
    return out
```
